# Optimizing a Trainium2 kernel written in Bass

```python
import math
import jax, jax.numpy as jnp
from jax import lax
import numpy as np

D_MODEL = 2048
BATCH = 2
SEQ = 8192
DEPTH = 2

HEAD_DIM = 64
N_Q_HEADS = 16
N_KV_HEADS = 4
GQA_GROUP = N_Q_HEADS // N_KV_HEADS
WINDOW = 128
ATTN_BLOCK = 128
ATTN_WIDTH = N_Q_HEADS * HEAD_DIM

GLA_HEADS = 4
GLA_DV = 256
GLA_DK = 128
GLA_WIDTH = GLA_HEADS * GLA_DV
GLA_GATE_RANK = 16
GLA_TAU = 16.0
GLA_CHUNK = 64

MIX_WIDTH = ATTN_WIDTH + GLA_WIDTH
D_FF = 4 * D_MODEL
EPS = 1e-6

IN_SIZES = (
    ATTN_WIDTH,
    N_KV_HEADS * HEAD_DIM,
    N_KV_HEADS * HEAD_DIM,
    GLA_HEADS * GLA_DK,
    GLA_HEADS * GLA_DK,
    GLA_WIDTH,
    GLA_WIDTH,
    GLA_GATE_RANK,
)
IN_WIDTH = int(sum(IN_SIZES))
SPLIT_POINTS = [int(v) for v in np.cumsum(IN_SIZES)[:-1]]

kernel_name = "hybrid_swa_gla_parallel_heads"


def rmsnorm(x, g):
    xf = x.astype(jnp.float32)
    y = xf * lax.rsqrt(jnp.mean(xf * xf, axis=-1, keepdims=True) + EPS)
    return (y * g.astype(jnp.float32)).astype(x.dtype)


def alibi_slopes(n):
    return jnp.asarray([2.0 ** (-8.0 * (i + 1) / n) for i in range(n)], dtype=jnp.float32)


def sliding_window_attention(q, k, v, sinks):
    B, S = q.shape[0], q.shape[1]
    nb = S // ATTN_BLOCK
    qb = q.reshape(B, nb, ATTN_BLOCK, N_KV_HEADS, GQA_GROUP, HEAD_DIM)

    def band(t):
        tb = t.reshape(B, nb, ATTN_BLOCK, N_KV_HEADS, HEAD_DIM)
        prev = jnp.concatenate([jnp.zeros_like(tb[:, :1]), tb[:, :-1]], axis=1)
        return jnp.concatenate([prev, tb], axis=2)

    kb, vb = band(k), band(v)
    scores = jnp.einsum('bnqhgd,bnkhd->bnhgqk', qb, kb).astype(jnp.float32) * (HEAD_DIM ** -0.5)

    qi = jnp.arange(ATTN_BLOCK)[:, None]
    kj = jnp.arange(2 * ATTN_BLOCK)[None, :]
    dist = qi + ATTN_BLOCK - kj
    blk = jnp.arange(nb)[:, None, None]
    valid = (dist >= 0)[None] & (dist < WINDOW)[None] & ((blk * ATTN_BLOCK - ATTN_BLOCK + kj[None]) >= 0)
    slopes = alibi_slopes(N_Q_HEADS).reshape(N_KV_HEADS, GQA_GROUP)
    alibi = -slopes[:, :, None, None] * dist.astype(jnp.float32)[None, None]
    scores = scores + alibi[None, None]
    scores = jnp.where(valid[None, :, None, None], scores, jnp.float32(-1e30))

    sink = sinks.astype(jnp.float32).reshape(N_KV_HEADS, GQA_GROUP)[None, None, :, :, None]
    m = jnp.maximum(jnp.max(scores, axis=-1), sink)
    p = jnp.exp(scores - m[..., None])
    denom = jnp.sum(p, axis=-1) + jnp.exp(sink - m)
    probs = (p / denom[..., None]).astype(vb.dtype)
    out = jnp.einsum('bnhgqk,bnkhd->bnqhgd', probs, vb)
    return out.reshape(B, S, N_Q_HEADS * HEAD_DIM)


def gla_chunked(q, k, v, log_a):
    B, S, H, dk = q.shape
    dv = v.shape[-1]
    C = GLA_CHUNK
    nc = S // C
    q, k, v, log_a = [t.reshape(B, nc, C, H, t.shape[-1]) for t in (q, k, v, log_a)]
    b = jnp.cumsum(log_a, axis=2)
    b_last = b[:, :, -1:]
    q_in = q * jnp.exp(b)
    k_in = k * jnp.exp(-b)
    k_state = k * jnp.exp(b_last - b)

    causal = jnp.tril(jnp.ones((C, C), dtype=bool))
    att = jnp.einsum('bnthd,bnshd->bnhts', q_in, k_in)
    att = jnp.where(causal, att, 0.0)
    o_intra = jnp.einsum('bnhts,bnshv->bnthv', att, v)

    def step(state, xs):
        qc, kc, vc, decay = xs
        o = jnp.einsum('bthd,bhdv->bthv', qc, state)
        state = state * decay[..., None] + jnp.einsum('bthd,bthv->bhdv', kc, vc)
        return state, o

    xs = (jnp.moveaxis(q_in, 1, 0), jnp.moveaxis(k_state, 1, 0), jnp.moveaxis(v, 1, 0),
          jnp.moveaxis(jnp.exp(b_last[:, :, 0]), 1, 0))
    state0 = jnp.zeros((B, H, dk, dv), jnp.float32)
    _, o_inter = lax.scan(step, state0, xs)
    o = o_intra + jnp.moveaxis(o_inter, 0, 1)
    return o.reshape(B, S, H, dv)


def setup_inputs(seed: int = 0) -> dict:
    key = jax.random.key(seed)
    ks = jax.random.split(key, 16)
    f32 = jnp.float32
    nrm = lambda k, shape, s: jax.random.normal(k, shape, f32) * s
    return {
        "x": nrm(ks[0], (BATCH, SEQ, D_MODEL), 1.0),
        "norm1_g": 1.0 + nrm(ks[1], (DEPTH, D_MODEL), 0.02),
        "w_in": nrm(ks[2], (DEPTH, D_MODEL, IN_WIDTH), D_MODEL ** -0.5),
        "q_norm_g": 1.0 + nrm(ks[3], (DEPTH, HEAD_DIM), 0.02),
        "k_norm_g": 1.0 + nrm(ks[4], (DEPTH, HEAD_DIM), 0.02),
        "attn_sinks": nrm(ks[5], (DEPTH, N_Q_HEADS), 0.5),
        "gla_gate_w": nrm(ks[6], (DEPTH, GLA_GATE_RANK, GLA_HEADS * GLA_DK), GLA_GATE_RANK ** -0.5),
        "gla_gate_b": nrm(ks[7], (DEPTH, GLA_HEADS * GLA_DK), 0.1),
        "gla_norm_g": 1.0 + nrm(ks[8], (DEPTH, GLA_DV), 0.02),
        "w_out": nrm(ks[9], (DEPTH, MIX_WIDTH, D_MODEL), MIX_WIDTH ** -0.5),
        "norm2_g": 1.0 + nrm(ks[10], (DEPTH, D_MODEL), 0.02),
        "w_up": nrm(ks[11], (DEPTH, D_MODEL, D_FF), D_MODEL ** -0.5),
        "w_down": nrm(ks[12], (DEPTH, D_FF, D_MODEL), D_FF ** -0.5),
    }


def reference(x, norm1_g, w_in, q_norm_g, k_norm_g, attn_sinks, gla_gate_w, gla_gate_b,
              gla_norm_g, w_out, norm2_g, w_up, w_down):
    B, S, _ = x.shape
    for l in range(DEPTH):
        h = rmsnorm(x, norm1_g[l])
        proj = h @ w_in[l]
        aq, ak, av, gq, gk, gv, gr, gz = jnp.split(proj, SPLIT_POINTS, axis=-1)

        aq = rmsnorm(aq.reshape(B, S, N_Q_HEADS, HEAD_DIM), q_norm_g[l])
        ak = rmsnorm(ak.reshape(B, S, N_KV_HEADS, HEAD_DIM), k_norm_g[l])
        av = av.reshape(B, S, N_KV_HEADS, HEAD_DIM)
        o_attn = sliding_window_attention(aq, ak, av, attn_sinks[l])

        gq = gq.reshape(B, S, GLA_HEADS, GLA_DK).astype(jnp.float32) * (GLA_DK ** -0.5)
        gk = gk.reshape(B, S, GLA_HEADS, GLA_DK).astype(jnp.float32)
        gv = gv.reshape(B, S, GLA_HEADS, GLA_DV).astype(jnp.float32)
        gate_logit = (gz @ gla_gate_w[l] + gla_gate_b[l]).astype(jnp.float32)
        log_a = (jax.nn.log_sigmoid(gate_logit) / GLA_TAU).reshape(B, S, GLA_HEADS, GLA_DK)
        o_gla = gla_chunked(gq, gk, gv, log_a).astype(x.dtype)
        o_gla = rmsnorm(o_gla, gla_norm_g[l]).reshape(B, S, GLA_WIDTH)
        o_gla = o_gla * jax.nn.silu(gr)

        mix = jnp.concatenate([o_attn, o_gla], axis=-1)
        x = x + mix @ w_out[l]

        h = rmsnorm(x, norm2_g[l])
        u = jnp.square(jax.nn.relu(h @ w_up[l]))
        x = x + u @ w_down[l]
    return x
```

```python
import numpy as np
import ml_dtypes
from contextlib import ExitStack
import concourse.bass as bass
import concourse.mybir as mybir
from concourse.bass_utils import run_bass_kernel_spmd

F32 = mybir.dt.float32
BF16 = mybir.dt.bfloat16
AF = mybir.ActivationFunctionType
ALU = mybir.AluOpType
AX = mybir.AxisListType

D = 2048
SEQ = 8192
DEPTH = 2
INW = 4624
DFF = 8192
PIPE_B = False
GLA_BANKS_PIPE = None
GT = 2048
NTB = GT // 128
EPS = 1e-6
NEG = -30000.0
ENGS = ("pe", "act", "dve", "pool", "sp")


class Sched:
    def __init__(self, same_engine_sync=True):
        self.ops = []
        self.ginc = {}
        self.same = (set(ENGS) if same_engine_sync is True else set() if not same_engine_sync else set(same_engine_sync)) - {'pe'}

    def op(self, eng, fn, reads=(), writes=(), dma=None, inc=16, atom=None):
        self.ops.append(dict(eng=eng, fn=fn, reads=tuple(reads), writes=tuple(writes), dma=dma, inc=inc, bar=False, atom=atom))
        if dma is not None:
            assert self.ginc.setdefault(dma, inc) == inc

    def barrier(self):
        for e in ENGS:
            self.ops.append(dict(eng=e, fn=None, reads=(), writes=(), dma=None, inc=0, bar=True))

    def resolve(self):
        last_w, readers = {}, {}
        eng_count = {e: 0 for e in ENGS}
        eng_last = {}
        dma_count, dma_last = {}, {}
        signal = set()
        for o in self.ops:
            e = o["eng"]
            if o["bar"]:
                deps = set(eng_last.values()) | set(dma_last.values())
                o["ev"] = None
                o["deps"] = deps
                for d in deps:
                    if d[0] == "eng":
                        signal.add((d[1], d[2]))
                continue
            if o["dma"] is None:
                ev = ("eng", e, eng_count[e])
                eng_count[e] += 1
            else:
                g = o["dma"]
                dma_count[g] = dma_count.get(g, 0) + 1
                ev = ("dma", g, dma_count[g])
            deps = set()
            for k in o["reads"]:
                if k in last_w:
                    deps.add(last_w[k])
            for k in o["writes"]:
                if k in last_w:
                    deps.add(last_w[k])
                deps.update(readers.get(k, ()))
            if o["dma"] is not None and o["dma"] in dma_last:
                deps.add(dma_last[o["dma"]])
            deps.discard(ev)
            red = {}
            for d in deps:
                kk = (d[0], d[1])
                if kk not in red or red[kk][2] < d[2]:
                    red[kk] = d
            deps = set(red.values())
            o["ev"], o["deps"] = ev, deps
            for d in deps:
                if d[0] == "eng":
                    if d[1] == e and e not in self.same:
                        continue
                    signal.add((d[1], d[2]))
            for k in o["reads"]:
                readers.setdefault(k, []).append(ev)
            for k in o["writes"]:
                last_w[k] = ev
                readers[k] = []
            if o["dma"] is not None:
                dma_last[o["dma"]] = ev
            else:
                eng_last[e] = ev
        self.signal = signal
        self.dma_groups = sorted(dma_count)

    def emit(self, nc):
        self.resolve()
        with ExitStack() as st:
            EPOCH = 30000
            cnt0 = {e: 0 for e in ENGS}
            idx0 = {e: 0 for e in ENGS}
            for o in self.ops:
                if o["dma"] is None and not o["bar"]:
                    e = o["eng"]
                    if (e, idx0[e]) in self.signal:
                        cnt0[e] += 1
                    idx0[e] += 1
            esem = {(e, k): st.enter_context(nc.semaphore(f"s_{e}{k}")) for e in ("pe", "act", "dve", "pool")
                    for k in range(cnt0[e] // EPOCH + 1)}
            dsem = {g: st.enter_context(nc.semaphore("d_" + str(g))) for g in self.dma_groups}
            block = st.enter_context(nc.Block())
            sigval, cnt, idxc = {}, {e: 0 for e in ENGS}, {e: 0 for e in ENGS}
            for o in self.ops:
                if o["dma"] is None and not o["bar"]:
                    e = o["eng"]
                    i = idxc[e]
                    idxc[e] += 1
                    if (e, i) in self.signal:
                        sigval[(e, i)] = (cnt[e] // EPOCH, cnt[e] % EPOCH + 1)
                        cnt[e] += 1
            self.sig_counts = cnt
            per = {e: [o for o in self.ops if o["eng"] == e] for e in ENGS}
            same = self.same

            def run_engine(e, engobj):
                seen = {}
                myidx = 0
                for o in per[e]:
                    need = {}
                    for d in o["deps"]:
                        if d[0] == "eng":
                            if d[1] == e and e not in same:
                                continue
                            key, val = ("eng", d[1]), sigval[(d[1], d[2])]
                        else:
                            key, val = ("dma", d[1]), (0, self.ginc[d[1]] * d[2])
                        if seen.get(key, (0, 0)) >= val:
                            continue
                        need[key] = max(need.get(key, (0, 0)), val)
                    for key, val in need.items():
                        engobj.wait_ge(esem[(key[1], val[0])] if key[0] == "eng" else dsem[key[1]], val[1])
                        seen[key] = val
                    if o["bar"]:
                        continue
                    ins = o["fn"](engobj)
                    if o["dma"] is not None:
                        ins.then_inc(dsem[o["dma"]], o["inc"])
                    else:
                        if (e, myidx) in self.signal:
                            ins.then_inc(esem[(e, sigval[(e, myidx)][0])], 1)
                        myidx += 1
                final = {}
                for o in per[e]:
                    if o["dma"] is not None:
                        final[o["dma"]] = max(final.get(o["dma"], 0), o["inc"] * o["ev"][2])
                for g, val in final.items():
                    if seen.get(("dma", g), (0, 0)) < (0, val):
                        engobj.wait_ge(dsem[g], val)

            for e, deco in (("sp", block.sync), ("pe", block.tensor), ("act", block.scalar),
                            ("dve", block.vector), ("pool", block.gpsimd)):
                if per[e]:
                    deco(lambda eng, e=e: run_engine(e, eng))


class Arena:
    def __init__(self, tile, ncols):
        self.tile, self.ncols, self.off, self.gen = tile, ncols, 0, 0

    def reset(self):
        self.off = 0
        self.gen += 1

    def alloc(self, cols, dtype=F32):
        n4 = cols if dtype == F32 else (cols + 1) // 2
        assert self.off + n4 <= self.ncols, (self.off, n4, self.ncols)
        v = self.tile[:, self.off:self.off + n4]
        self.off += n4
        return v if dtype == F32 else v.bitcast(BF16)


def host_consts():
    hs = np.arange(16)
    slopes = (2.0 ** (-8.0 * (hs + 1) / 16)).astype(np.float64)
    tk = np.arange(128)[:, None, None]
    tq = np.arange(128)[None, None, :]
    sl = slopes[None, :, None]
    dist_cur = tq - tk
    bias_cur = np.where(dist_cur >= 0, -sl * dist_cur, NEG)
    dist_prev = tq + 128 - tk
    bias_prev = np.where(dist_prev < 128, -sl * dist_prev, NEG)
    bias_first = np.full_like(bias_prev, NEG)
    s = np.arange(128)[:, None]
    t = np.arange(128)[None, :]
    same = (s // 64) == (t // 64)
    triu = np.where(same & (s <= t), -1.0 / 16, 0.0)
    strictl = np.where(same & (s > t), -1.0 / 16, 0.0)
    mask01 = np.where(same & (s <= t), 1.0, 0.0)
    chunksel = np.zeros((128, 2))
    chunksel[:64, 0] = -1.0 / 16
    chunksel[64:, 1] = -1.0 / 16
    c = dict(
        c_bias=np.stack([bias_prev, bias_cur, bias_first], 1).reshape(128, 3 * 2048).astype(np.float32),
        c_mats=np.concatenate([triu, strictl, mask01, chunksel], 1).astype(np.float32),
        c_ident=np.eye(128).astype(ml_dtypes.bfloat16),
    )
    return c


def build_program(n_tok, n_layers=DEPTH, same_sync=True, plan=None):
    n_groups = n_tok // GT
    nc = bass.Bass("TRN2", target_bir_lowering=False)
    dt = lambda name, shape, dtp=F32, kind="ExternalInput": nc.dram_tensor(name, shape, dtp, kind=kind).ap()
    lite = plan is not None and plan == [("pre", 0)]
    x_in = dt("x", [n_tok, D])
    norm1_g = dt("norm1_g", [DEPTH, D])
    w_in = dt("w_in", [DEPTH, D, INW])
    q_norm_g = dt("q_norm_g", [DEPTH, 64])
    k_norm_g = dt("k_norm_g", [DEPTH, 64])
    attn_sinks = dt("attn_sinks", [DEPTH, 16])
    gate_w = dt("gla_gate_w", [DEPTH, 16, 512])
    gate_b = dt("gla_gate_b", [DEPTH, 512])
    gla_norm_g = dt("gla_norm_g", [DEPTH, 256])
    if not lite:
        w_out = dt("w_out", [DEPTH, D, D])
        norm2_g = dt("norm2_g", [DEPTH, D])
        w_up = dt("w_up", [DEPTH, D, DFF])
        w_down = dt("w_down", [DEPTH, DFF, D])
    c_bias = dt("c_bias", [128, 3 * 2048])
    c_mats = dt("c_mats", [128, 386])
    c_ident = dt("c_ident", [128, 128], BF16)
    has_pre = plan is not None and any(p[0] == "pre" for p in plan)
    has_main = plan is not None and any(p[0] == "main" for p in plan)
    mid_out = plan is not None and len(plan) == 2
    y_out = x1_out = None
    if plan is None or plan == [("main", 1)]:
        y_out = dt("out", [n_tok, D], F32, "ExternalOutput")
    if mid_out:
        x1_out = dt("x1", [n_tok, D], F32, "ExternalOutput")
    if has_pre:
        pl_S = dt("pl_S", [128, 1028], F32, "ExternalOutput")
        pl_KT = dt("pl_KT", [64, 512], BF16, "ExternalOutput")
        pl_V = dt("pl_V", [128, 256], BF16, "ExternalOutput")
    if has_main:
        g_S = dt("g_S", [4, 128, 1028])
        h_KT = dt("h_KT", [64, 512], BF16)
        h_V = dt("h_V", [128, 256], BF16)
        cmask_d = dt("cmask", [128, 8])
    xa = nc.dram_tensor("xa", [n_tok, D], F32).ap()
    xmid = nc.dram_tensor("xmid", [GT, D], F32).ap()
    proj = nc.dram_tensor("proj", [GT, 4608], F32).ap()
    gzT_h = nc.dram_tensor("gzT_h", [16, GT], F32).ap()
    mixT_h = nc.dram_tensor("mixT_h", [NTB, 128, 16 * 128], BF16).ap()
    h2T_h = nc.dram_tensor("h2T_h", [NTB, 128, 16 * 128], BF16).ap()

    S = Sched(same_sync)
    with ExitStack() as st:
        sb = lambda name, shape, dtp=F32: st.enter_context(nc.sbuf_tensor(name, shape, dtp))
        ident = sb("ident", [128, 128], BF16)
        mats = sb("mats", [128, 386])
        triu, strictl, mask01, chunksel = mats[:, 0:128], mats[:, 128:256], mats[:, 256:384], mats[:, 384:386]
        g1b = sb("g1b", [128, D])
        g2b = sb("g2b", [128, D])
        g64 = sb("g64", [128, 128])
        gqk = sb("gqk", [128, 20, 64])
        esink = sb("esink", [128, 16])
        ggla = sb("ggla", [128, 256])
        gatew = sb("gatew", [17, 512])
        wgz = sb("wgz", [128, 16, 16], BF16)
        Sst = sb("Sst", [128, 4, 256])
        Smid = sb("Smid", [128, 4, 256])
        Sbf0 = [sb(f"Sbf0_{i}", [128, 4, 256], BF16) for i in range(2)]
        Sbf1 = sb("Sbf1", [128, 4, 256], BF16)
        KT = [sb(f"KT{i}", [64, 512], BF16) for i in range(2)]
        Vaug = [sb(f"Vaug{i}", [128, 4, 65], BF16) for i in range(2)]
        gzTa = [sb(f"gzTa{i}", [17, 128]) for i in range(2)]
        small = sb("small", [128, 64])
        cm = sb("cm", [128, 8])
        ARC = 38000
        arena_t = sb("arena", [128, ARC])
        ar = Arena(arena_t, ARC)
        psA = st.enter_context(nc.psum_tensor("psA", [128, 2048], BF16))
        psb = [st.enter_context(nc.psum_tensor(f"ps{i}", [128, 512], F32)) for i in range(6)]
        PB, PC, PD, PE_, PF, PG = range(6)
        pk = lambda i: ("ps", i)

        S.op("sp", lambda e: e.dma_start(out=ident[:], in_=c_ident), writes=["ident"], dma="c0")
        S.op("sp", lambda e: e.dma_start(out=mats[:], in_=c_mats), writes=["mats"], dma="c1")
        for i in range(2):
            S.op("pool", lambda e, i=i: e.memset(KT[i][:], 0.0), writes=[("KT", i)])
            S.op("pool", lambda e, i=i: e.memset(Vaug[i][:], 1.0), writes=[("Vaug", i)])
            S.op("pool", lambda e, i=i: e.memset(gzTa[i][:], 1.0), writes=[("gzTa", i)])

        def layer_consts(l):
            S.op("sp", lambda e: e.dma_start(out=g1b[:], in_=norm1_g[l].partition_broadcast(128)), writes=["g1b"], dma="c0")
            if not lite:
                S.op("sp", lambda e: e.dma_start(out=g2b[:], in_=norm2_g[l].partition_broadcast(128)), writes=["g2b"], dma="c1")
            S.op("sp", lambda e: e.dma_start(out=g64[:, 0:64], in_=q_norm_g[l].partition_broadcast(128)), writes=["g64a"], dma="c2")
            S.op("sp", lambda e: e.dma_start(out=g64[:, 64:128], in_=k_norm_g[l].partition_broadcast(128)), writes=["g64b"], dma="c3")
            S.op("dve", lambda e: e.tensor_copy(gqk[:, 0:16, :], g64[:, 0:64].unsqueeze(1).to_broadcast([128, 16, 64])),
                 reads=["g64a"], writes=["gqk_q"])
            S.op("dve", lambda e: e.tensor_copy(gqk[:, 16:20, :], g64[:, 64:128].unsqueeze(1).to_broadcast([128, 4, 64])),
                 reads=["g64b"], writes=["gqk_k"])
            S.op("sp", lambda e: e.dma_start(out=esink[:], in_=attn_sinks[l].partition_broadcast(128)), writes=["esink"], dma="c2")
            S.op("act", lambda e: e.activation(esink[:], esink[:], AF.Exp), reads=["esink"], writes=["esink"])
            S.op("sp", lambda e: e.dma_start(out=ggla[:], in_=gla_norm_g[l].partition_broadcast(128)), writes=["ggla"], dma="c3")
            S.op("sp", lambda e: e.dma_start(out=gatew[0:16, :], in_=gate_w[l]), writes=["gatew0"], dma="c0")
            S.op("sp", lambda e: e.dma_start(out=gatew[16:17, :], in_=gate_b[l:l + 1, :]), writes=["gatew1"], dma="c1")
            S.op("pool", lambda e: e.dma_start(out=wgz[:], in_=w_in[l, :, 4608:4624].rearrange("(kc p) n -> p kc n", p=128)),
                 writes=["wgz"], dma="c4")

        def norm_block(tag, dtag, src_rows, gb, gkey, xt, hb, dstT_fn, dst_keys, ssi):
            if src_rows is not None:
                S.op("sp", lambda e: e.dma_start(out=xt, in_=src_rows), writes=[tag + "xt"], dma=dtag + "x")
            S.op("act", lambda e: e.activation(hb, xt, AF.Square, accum_out=small[:, ssi:ssi + 1]),
                 reads=[tag + "xt"], writes=[tag + "hb", ("small", ssi)])
            S.op("act", lambda e: e.activation(small[:, ssi + 1:ssi + 2], small[:, ssi:ssi + 1], AF.Ln, scale=1.0 / D, bias=EPS),
                 reads=[("small", ssi)], writes=[("small", ssi + 1)])
            S.op("act", lambda e: e.activation(small[:, ssi + 2:ssi + 3], small[:, ssi + 1:ssi + 2], AF.Exp, scale=-0.5),
                 reads=[("small", ssi + 1)], writes=[("small", ssi + 2)])
            S.op("dve", lambda e: e.scalar_tensor_tensor(hb, xt, small[:, ssi + 2:ssi + 3], gb[:], ALU.mult, ALU.mult),
                 reads=[tag + "xt", ("small", ssi + 2), gkey, tag + "hb"], writes=[tag + "hb"])
            for kc in range(16):
                S.op("pe", lambda e, kc=kc: e.transpose(psA[:, kc * 128:(kc + 1) * 128], hb[:, kc * 128:(kc + 1) * 128], ident[:]),
                     reads=[tag + "hb", "ident"], writes=[("psA", kc // 8)])
            pv = psA[:].rearrange("p (k t) -> p k t", k=16)
            S.op("act", lambda e: e.copy(dstT_fn(0, 8), pv[:, 0:8, :]), reads=[("psA", 0)], writes=[dst_keys[0]])
            S.op("dve", lambda e: e.tensor_copy(dstT_fn(8, 16), pv[:, 8:16, :]), reads=[("psA", 1)], writes=[dst_keys[1]])

        def phase_A(l, x_src, r0):
            ar.reset()
            g = f"A{ar.gen}."
            gd = "A."
            hT = ar.alloc(16 * GT, BF16).rearrange("p (k t) -> p k t", k=16)
            wt = [ar.alloc(16 * 512, BF16).rearrange("p (k n) -> p k n", k=16) for _ in range(2)]
            xt = [ar.alloc(D) for _ in range(2)]
            hb = [ar.alloc(D, BF16) for _ in range(2)]
            stg = [ar.alloc(512) for _ in range(4)]
            gzs = ar.alloc(GT)
            for tb in range(NTB):
                i = tb % 2
                norm_block(g + f"{i}", gd + f"{i}", x_src[r0 + tb * 128: r0 + (tb + 1) * 128, :], g1b, "g1b", xt[i], hb[i],
                           lambda a, b, tb=tb: hT[:, a:b, tb * 128:(tb + 1) * 128],
                           [(g + "hT", tb, 0), (g + "hT", tb, 1)], 4 * i)
            hkeys = lambda tbs: [(g + "hT", tb, i) for tb in tbs for i in range(2)]
            for tt in range(GT // 512):
                for kc in range(16):
                    S.op("pe", lambda e, kc=kc, tt=tt: e.matmul(psb[PB][0:16, :], wgz[:, kc, :], hT[:, kc, tt * 512:(tt + 1) * 512],
                                                                start=(kc == 0), stop=(kc == 15)),
                         reads=hkeys(range(tt * 4, tt * 4 + 4)) + ["wgz"], writes=[pk(PB)])
                S.op("act", lambda e, tt=tt: e.copy(gzs[0:16, tt * 512:(tt + 1) * 512], psb[PB][0:16, :]),
                     reads=[pk(PB)], writes=[g + "gzs"])
            S.op("sp", lambda e: e.dma_start(out=gzT_h, in_=gzs[0:16, :]), reads=[g + "gzs"], writes=["gzT_h"], dma=gd + "gz")
            cnt = 0
            for nt in range(9):
                w = wt[nt % 2]
                wk = (g + "wt", nt % 2)
                S.op("pool", lambda e, nt=nt, w=w: e.dma_start(
                    out=w, in_=w_in[l, :, nt * 512:(nt + 1) * 512].rearrange("(kc p) n -> p kc n", p=128)),
                    writes=[wk], dma=gd + f"w{nt % 2}")
                for tb in range(NTB):
                    pi = PB + (cnt % 2)
                    for kc in range(16):
                        S.op("pe", lambda e, kc=kc, tb=tb, w=w, pi=pi: e.matmul(
                            psb[pi][:], hT[:, kc, tb * 128:(tb + 1) * 128], w[:, kc, :], start=(kc == 0), stop=(kc == 15)),
                            reads=hkeys([tb]) + [wk], writes=[pk(pi)])
                    sg = stg[cnt % 4]
                    sk = (g + "stg", cnt % 4)
                    if cnt % 2 == 0:
                        S.op("act", lambda e, sg=sg, pi=pi: e.copy(sg, psb[pi][:]), reads=[pk(pi)], writes=[sk])
                    else:
                        S.op("dve", lambda e, sg=sg, pi=pi: e.tensor_copy(sg, psb[pi][:]), reads=[pk(pi)], writes=[sk])
                    S.op("sp", lambda e, sg=sg, tb=tb, nt=nt: e.dma_start(
                        out=proj[tb * 128:(tb + 1) * 128, nt * 512:(nt + 1) * 512], in_=sg),
                        reads=[sk], writes=[("proj", tb)], dma=gd + f"st{cnt % 4}")
                    cnt += 1
            S.barrier()

        def phase_B(l, gblk0, first_group, init_payload=False):
            ar.reset()
            SST_ALL = [("Sst", h) for h in range(4)]
            GLA_BANKS = (PG, PE_, PE_)
            g = f"B{ar.gen}."
            gd = "B."
            bias = ar.alloc(3 * 2048)
            Pt = [ar.alloc(4608) for _ in range(2)]
            sqt = ar.alloc(1280)
            qkn = ar.alloc(1280, BF16)
            QTs = ar.alloc(2048, BF16)
            spt = [ar.alloc(512) for _ in range(2)]
            PT = [[ar.alloc(512, BF16) for _ in range(2)] for _ in range(2)]
            mix = [ar.alloc(D, BF16) for _ in range(2)]
            et = ar.alloc(512)
            lat = ar.alloc(512)
            Eb = ar.alloc(512)
            Enb = ar.alloc(512)
            Ebl = ar.alloc(512)
            qin = ar.alloc(512, BF16)
            kin = ar.alloc(512, BF16)
            kst = ar.alloc(512, BF16)
            gvb = ar.alloc(1024, BF16)
            sgt = ar.alloc(1024)
            qT0 = ar.alloc(512, BF16)
            qT1 = ar.alloc(512, BF16)
            kinT = ar.alloc(512, BF16)
            attm = ar.alloc(512, BF16)
            tmpo = [ar.alloc(256) for _ in range(2)]
            junk = ar.alloc(256)
            mTs = [ar.alloc(2048, BF16) for _ in range(2)]
            S.op("sp", lambda e: e.dma_start(out=bias, in_=c_bias), writes=[g + "bias"], dma=gd + "bias")
            S.op("pool", lambda e: e.memset(qT0, 0.0), writes=[g + "qT0"])
            S.op("pool", lambda e: e.memset(qT1, 0.0), writes=[g + "qT1"])
            if init_payload:
                Sj = ar.alloc(1028)
                tmpc = ar.alloc(256)
                fac = ar.alloc(4)
                S.op("sp", lambda e: e.dma_start(out=cm[:], in_=cmask_d), writes=["cm"], dma=gd + "cm")
                S.op("pool", lambda e: e.memset(Sst[:], 0.0), writes=SST_ALL)
                for j in range(3):
                    S.op("sp", lambda e, j=j: e.dma_start(out=Sj, in_=g_S[j]), writes=[g + "Sj"], dma=gd + "Sj")
                    S.op("act", lambda e: e.activation(fac, Sj[:, 1024:1028], AF.Exp), reads=[g + "Sj"], writes=[g + "fac"])
                    S.op("dve", lambda e, j=j: e.tensor_scalar(fac, fac, cm[:, j:j + 1], cm[:, 4 + j:5 + j], ALU.mult, ALU.add),
                         reads=[g + "fac", "cm"], writes=[g + "fac"])
                    for h in range(4):
                        S.op("dve", lambda e, j=j, h=h: e.tensor_scalar_mul(tmpc, Sj[:, h * 256:(h + 1) * 256], cm[:, j:j + 1]),
                             reads=[g + "Sj", "cm"], writes=[g + "tmpc"])
                        S.op("dve", lambda e, h=h: e.scalar_tensor_tensor(Sst[:, h, :], Sst[:, h, :], fac[:, h:h + 1], tmpc, ALU.mult, ALU.add),
                             reads=[("Sst", h), g + "fac", g + "tmpc"], writes=[("Sst", h)])
                S.op("act", lambda e: e.copy(Sbf0[0][:], Sst[:]), reads=SST_ALL, writes=[("Sbf0", 0, h) for h in range(4)])
                S.op("sp", lambda e: e.dma_start(out=KT[1][:], in_=h_KT), writes=[("KT", 1)], dma=gd + "hk")
                S.op("sp", lambda e: e.dma_start(out=Vaug[1][:, :, 0:64], in_=h_V.rearrange("p (j d) -> p j d", j=4)),
                     writes=[("Vaug", 1)], dma=gd + "hv")
            if first_group:
                for i in range(2):
                    S.op("pool", lambda e, i=i: e.memset(KT[i][:], 0.0), writes=[("KT", i)])
                    S.op("pool", lambda e, i=i: e.memset(Vaug[i][:, :, 0:64], 0.0), writes=[("Vaug", i)])
                S.op("pool", lambda e: e.memset(Sst[:], 0.0), writes=SST_ALL)
                S.op("pool", lambda e: e.memset(Sbf0[0][:], 0.0), writes=[("Sbf0", 0, h) for h in range(4)])
            def swa(tb):
                pb = tb % 2
                P = Pt[pb]
                Pk = (g + "P", pb)
                S.op("sp", lambda e, P=P, tb=tb: e.dma_start(out=P, in_=proj[tb * 128:(tb + 1) * 128, :]),
                     reads=[("proj", tb)], writes=[Pk], dma=gd + f"P{pb}")
                S.op("pool", lambda e, P=P: e.tensor_tensor(sqt, P[:, 0:1280], P[:, 0:1280], ALU.mult), reads=[Pk], writes=[g + "sqt"])
                S.op("dve", lambda e: e.tensor_reduce(small[:, 8:28], sqt.rearrange("p (h d) -> p h d", h=20), AX.X, ALU.add),
                     reads=[g + "sqt"], writes=["ssq"])
                S.op("act", lambda e: e.activation(small[:, 8:28], small[:, 8:28], AF.Ln, scale=1.0 / 64, bias=EPS),
                     reads=["ssq"], writes=["ssq"])
                S.op("act", lambda e: e.activation(small[:, 8:28], small[:, 8:28], AF.Exp, scale=-0.5), reads=["ssq"], writes=["ssq"])
                S.op("dve", lambda e, P=P: e.tensor_tensor(sqt.rearrange("p (h d) -> p h d", h=20),
                                                          P[:, 0:1280].rearrange("p (h d) -> p h d", h=20),
                                                          small[:, 8:28].unsqueeze(2).to_broadcast([128, 20, 64]), ALU.mult),
                     reads=[Pk, "ssq", g + "sqt"], writes=[g + "sqt"])
                S.op("pool", lambda e: e.tensor_tensor(qkn.rearrange("p (h d) -> p h d", h=20),
                                                      sqt.rearrange("p (h d) -> p h d", h=20), gqk[:], ALU.mult),
                     reads=[g + "sqt", "gqk_q", "gqk_k"], writes=[g + "qkn"])
                S.op("act", lambda e, P=P, pb=pb: e.copy(Vaug[pb][:, :, 0:64], P[:, 1280:1536].rearrange("p (j d) -> p j d", j=4)),
                     reads=[Pk], writes=[("Vaug", pb)])
                pbb = psb[PB][:].bitcast(BF16)
                for j in range(4):
                    S.op("pe", lambda e, j=j: e.transpose(pbb[0:64, j * 128:(j + 1) * 128], qkn[:, 1024 + j * 64:1024 + (j + 1) * 64], ident[:]),
                         reads=[g + "qkn", "ident"], writes=[pk(PB)])
                S.op("act", lambda e, pb=pb: e.copy(KT[pb][:], pbb[0:64, 0:512]), reads=[pk(PB)], writes=[("KT", pb)])
                for h in range(16):
                    S.op("pe", lambda e, h=h: e.transpose(psA[0:64, h * 128:(h + 1) * 128], qkn[:, h * 64:(h + 1) * 64], ident[:]),
                         reads=[g + "qkn", "ident"], writes=[("psA", h // 8)])
                S.op("dve", lambda e: e.tensor_copy(QTs[0:64, 0:1024], psA[0:64, 0:1024]), reads=[("psA", 0)], writes=[g + "QT0"])
                S.op("act", lambda e: e.copy(QTs[0:64, 1024:2048], psA[0:64, 1024:2048]), reads=[("psA", 1)], writes=[g + "QT1"])
                first_blk = (first_group or init_payload) and tb == 0
                def scores(j):
                    for c in range(2):
                        cnt = 2 * j + c
                        src = 1 - pb if c == 0 else pb
                        pi = PB + (cnt % 2)
                        bsel = (2 if first_blk else 0) if c == 0 else 1
                        S.op("pe", lambda e, j=j, src=src, pi=pi: e.matmul(psb[pi][:], KT[src][0:64, j * 128:(j + 1) * 128],
                                                                         QTs[0:64, j * 512:(j + 1) * 512], start=True, stop=True),
                             reads=[("KT", src), g + "QT0", g + "QT1"], writes=[pk(pi)])
                        S.op("dve", lambda e, j=j, pi=pi, bsel=bsel, cnt=cnt: e.scalar_tensor_tensor(
                            spt[cnt % 2], psb[pi][:], 0.125, bias[:, bsel * 2048 + j * 512: bsel * 2048 + (j + 1) * 512], ALU.mult, ALU.add),
                            reads=[pk(pi), g + "bias"], writes=[(g + "spt", cnt % 2)])
                        S.op("act", lambda e, j=j, c=c, cnt=cnt: e.activation(PT[j % 2][c], spt[cnt % 2], AF.Exp),
                             reads=[(g + "spt", cnt % 2)], writes=[(g + "PT", j % 2, c)])

                def pv(j):
                    pdv = psb[PD][:, 0:260].rearrange("p (h d) -> p h d", h=4)
                    for hl in range(4):
                        for c in range(2):
                            src = 1 - pb if c == 0 else pb
                            S.op("pe", lambda e, j=j, hl=hl, c=c, src=src: e.matmul(
                                psb[PD][:, hl * 65:(hl + 1) * 65], PT[j % 2][c][:, hl * 128:(hl + 1) * 128], Vaug[src][:, j, :],
                                start=(c == 0), stop=(c == 1)),
                                reads=[(g + "PT", j % 2, c), ("Vaug", src)], writes=[pk(PD)], atom=("pv", tb, j, hl))
                    S.op("dve", lambda e, j=j: e.tensor_tensor(small[:, 32:36], pdv[:, :, 64], esink[:, 4 * j:4 * j + 4], ALU.add),
                         reads=[pk(PD), "esink"], writes=["den"])
                    S.op("dve", lambda e: e.reciprocal(small[:, 32:36], small[:, 32:36]), reads=["den"], writes=["den"])
                    S.op("dve", lambda e, j=j, pb=pb: e.tensor_tensor(
                        mix[pb][:, j * 256:(j + 1) * 256].rearrange("p (h d) -> p h d", h=4), pdv[:, :, 0:64],
                        small[:, 32:36].unsqueeze(2).to_broadcast([128, 4, 64]), ALU.mult),
                        reads=[pk(PD), "den"], writes=[(g + "mixa", pb)])

                scores(0)
                for j in range(4):
                    if j + 1 < 4:
                        scores(j + 1)
                    pv(j)

            def gla(tb):
                pb = tb % 2
                P = Pt[pb]
                Pk = (g + "P", pb)
                WE = [pk(PE_)]
                TBK, ABK, UBK = GLA_BANKS
                WF = [pk(PF)]
                WG = [pk(PG)]
                S.op("sp", lambda e, tb=tb, pb=pb: e.dma_start(out=gzTa[pb][0:16, :], in_=gzT_h[:, tb * 128:(tb + 1) * 128]),
                     reads=["gzT_h"], writes=[("gzTa", pb)], dma=gd + f"gz{pb}")
                S.op("pe", lambda e, pb=pb: e.matmul(psb[PE_][:], gzTa[pb][:], gatew[:], start=True, stop=True),
                     reads=[("gzTa", pb), "gatew0", "gatew1"], writes=WE)
                S.op("act", lambda e: e.activation(et, psb[PE_][:], AF.Exp, scale=-1.0), reads=[pk(PE_)], writes=[g + "et"])
                S.op("act", lambda e: e.activation(lat, et, AF.Ln, bias=1.0), reads=[g + "et"], writes=[g + "lat"])
                S.op("pe", lambda e: e.matmul(psb[PF][:], triu, lat, start=True, stop=True), reads=["mats", g + "lat"], writes=WF)
                for h in range(4):
                    S.op("pe", lambda e, h=h: e.matmul(psb[PE_][:, 2 * h:2 * h + 2], lat[:, h * 128:(h + 1) * 128], chunksel, start=True, stop=True),
                         reads=["mats", g + "lat"], writes=WE)
                S.op("act", lambda e: e.activation(small[:, 40:48], psb[PE_][:, 0:8], AF.Exp), reads=WE, writes=["dec"])
                S.op("pe", lambda e: e.matmul(psb[PE_][:], strictl, lat, start=True, stop=True), reads=["mats", g + "lat"], writes=WE)
                S.op("act", lambda e: e.activation(Eb, psb[PF][:], AF.Exp), reads=WF, writes=[g + "Eb"])
                S.op("act", lambda e: e.activation(Enb, psb[PF][:], AF.Exp, scale=-1.0), reads=WF, writes=[g + "Enb"])
                S.op("act", lambda e: e.activation(Ebl, psb[PE_][:], AF.Exp), reads=[pk(PE_)], writes=[g + "Ebl"])
                S.op("dve", lambda e, P=P: e.scalar_tensor_tensor(qin, P[:, 1536:2048], 128.0 ** -0.5, Eb, ALU.mult, ALU.mult),
                     reads=[Pk, g + "Eb"], writes=[g + "qin"])
                S.op("pool", lambda e, P=P: e.tensor_tensor(kin, P[:, 2048:2560], Enb, ALU.mult), reads=[Pk, g + "Enb"], writes=[g + "kin"])
                S.op("pool", lambda e, P=P: e.tensor_tensor(kst, P[:, 2048:2560], Ebl, ALU.mult), reads=[Pk, g + "Ebl"], writes=[g + "kst"])
                S.op("act", lambda e, P=P: e.copy(gvb, P[:, 2560:3584]), reads=[Pk], writes=[g + "gvb"])
                S.op("act", lambda e, P=P: e.activation(sgt, P[:, 3584:4608], AF.Exp, scale=-1.0), reads=[Pk], writes=[g + "sgt"])
                S.op("dve", lambda e: e.tensor_scalar_add(sgt, sgt, 1.0), reads=[g + "sgt"], writes=[g + "sgt"])
                S.op("dve", lambda e: e.reciprocal(sgt, sgt), reads=[g + "sgt"], writes=[g + "sgt"])
                S.op("pool", lambda e, P=P: e.tensor_tensor(sgt, sgt, P[:, 3584:4608], ALU.mult), reads=[g + "sgt", Pk], writes=[g + "sgt"])
                pfb = psb[TBK][:].bitcast(BF16)
                for h in range(4):
                    S.op("pe", lambda e, h=h: e.transpose(pfb[:, h * 128:(h + 1) * 128], qin[:, h * 128:(h + 1) * 128], ident[:]),
                         reads=[g + "qin", "ident"], writes=[pk(TBK)])
                for h in range(4):
                    S.op("pe", lambda e, h=h: e.transpose(pfb[:, 512 + h * 128:512 + (h + 1) * 128], kin[:, h * 128:(h + 1) * 128], ident[:]),
                         reads=[g + "kin", "ident"], writes=[pk(TBK)])
                v3 = lambda ap: ap.rearrange("p (h t) -> p h t", h=4)
                S.op("act", lambda e: e.copy(v3(qT0)[:, :, 0:64], v3(pfb[:, 0:512])[:, :, 0:64]), reads=[pk(TBK)], writes=[g + "qT0"])
                S.op("dve", lambda e: e.tensor_copy(v3(qT1)[:, :, 64:128], v3(pfb[:, 0:512])[:, :, 64:128]), reads=[pk(TBK)], writes=[g + "qT1"])
                S.op("act", lambda e: e.copy(kinT, pfb[:, 512:1024]), reads=[pk(TBK)], writes=[g + "kinT"])
                for h in range(4):
                    S.op("pe", lambda e, h=h: e.matmul(psb[ABK][:, h * 128:h * 128 + 64], kinT[:, h * 128:(h + 1) * 128],
                                                       qT0[:, h * 128:h * 128 + 64], start=True, stop=True),
                         reads=[g + "kinT", g + "qT0"], writes=[pk(ABK)])
                    S.op("pe", lambda e, h=h: e.matmul(psb[ABK][:, h * 128 + 64:(h + 1) * 128], kinT[:, h * 128:(h + 1) * 128],
                                                       qT1[:, h * 128 + 64:(h + 1) * 128], start=True, stop=True),
                         reads=[g + "kinT", g + "qT1"], writes=[pk(ABK)])
                S.op("dve", lambda e: e.tensor_tensor(v3(attm), v3(psb[ABK][:]), mask01.unsqueeze(1).to_broadcast([128, 4, 128]), ALU.mult),
                     reads=[pk(ABK), "mats"], writes=[g + "attm"])
                s0 = Sbf0[pb]
                s0n = Sbf0[1 - pb]
                for h in range(4):
                    S.op("pe", lambda e, h=h: e.matmul(psb[UBK][:, 0:256], kst[0:64, h * 128:(h + 1) * 128], gvb[0:64, h * 256:(h + 1) * 256],
                                                       start=True, stop=True),
                         reads=[g + "kst", g + "gvb"], writes=[pk(UBK)])
                    S.op("dve", lambda e, h=h: e.scalar_tensor_tensor(Smid[:, h, :], Sst[:, h, :], small[:, 40 + 2 * h:41 + 2 * h],
                                                                      psb[UBK][:, 0:256], ALU.mult, ALU.add),
                         reads=[("Sst", h), "dec", pk(UBK)], writes=[("Smid", h)])
                    S.op("act", lambda e, h=h: e.copy(Sbf1[:, h, :], Smid[:, h, :]), reads=[("Smid", h)], writes=[("Sbf1", h)])
                    S.op("pe", lambda e, h=h: e.matmul(psb[PF][:, 0:256], kst[64:128, h * 128:(h + 1) * 128], gvb[64:128, h * 256:(h + 1) * 256],
                                                       start=True, stop=True),
                         reads=[g + "kst", g + "gvb"], writes=[pk(PF)])
                    og = psb[PG][:, 0:256]
                    ogk = pk(PG)
                    S.op("pe", lambda e, h=h, og=og: e.matmul(og, attm[:, h * 128:(h + 1) * 128], gvb[:, h * 256:(h + 1) * 256], start=True, stop=False),
                         reads=[g + "attm", g + "gvb"], writes=[ogk], atom=("og", tb, h))
                    S.op("pe", lambda e, h=h, og=og, s0=s0: e.matmul(og, qT0[:, h * 128:(h + 1) * 128], s0[:, h, :], start=False, stop=False),
                         reads=[g + "qT0", ("Sbf0", pb, h)], writes=[ogk], atom=("og", tb, h))
                    S.op("pe", lambda e, h=h, og=og: e.matmul(og, qT1[:, h * 128:(h + 1) * 128], Sbf1[:, h, :], start=False, stop=True),
                         reads=[g + "qT1", ("Sbf1", h)], writes=[ogk], atom=("og", tb, h))
                    S.op("dve", lambda e, h=h: e.scalar_tensor_tensor(Sst[:, h, :], Smid[:, h, :], small[:, 41 + 2 * h:42 + 2 * h],
                                                                      psb[PF][:, 0:256], ALU.mult, ALU.add),
                         reads=[("Smid", h), "dec", pk(PF), ("Sst", h)], writes=[("Sst", h)])
                    S.op("act", lambda e, h=h, s0n=s0n: e.copy(s0n[:, h, :], Sst[:, h, :]), reads=[("Sst", h)], writes=[("Sbf0", 1 - pb, h)])
                    S.op("act", lambda e, h=h, og=og: e.activation(junk, og, AF.Square, accum_out=small[:, 48 + h:49 + h]),
                         reads=[ogk], writes=[g + "junk", ("sso", h)])
                    S.op("act", lambda e, h=h: e.activation(small[:, 52 + h:53 + h], small[:, 48 + h:49 + h], AF.Ln, scale=1.0 / 256, bias=EPS),
                         reads=[("sso", h)], writes=[("sso2", h)])
                    S.op("act", lambda e, h=h: e.activation(small[:, 56 + h:57 + h], small[:, 52 + h:53 + h], AF.Exp, scale=-0.5),
                         reads=[("sso2", h)], writes=[("sso3", h)])
                    S.op("dve", lambda e, h=h, og=og: e.scalar_tensor_tensor(tmpo[h % 2], og, small[:, 56 + h:57 + h], ggla[:], ALU.mult, ALU.mult),
                         reads=[ogk, ("sso3", h), "ggla"], writes=[(g + "tmpo", h % 2)])
                    S.op("pool", lambda e, h=h, pb=pb: e.tensor_tensor(mix[pb][:, 1024 + h * 256:1024 + (h + 1) * 256], tmpo[h % 2],
                                                                      sgt[:, h * 256:(h + 1) * 256], ALU.mult),
                         reads=[(g + "tmpo", h % 2), g + "sgt"], writes=[(g + "mixg", pb)])

            def mixt(tb):
                pb = tb % 2
                for kc in range(16):
                    S.op("pe", lambda e, kc=kc, pb=pb: e.transpose(psA[:, kc * 128:(kc + 1) * 128], mix[pb][:, kc * 128:(kc + 1) * 128], ident[:]),
                         reads=[(g + "mixa", pb), (g + "mixg", pb), "ident"], writes=[("psA", kc // 8)])
                S.op("act", lambda e, pb=pb: e.copy(mTs[pb][:, 0:1024], psA[:, 0:1024]), reads=[("psA", 0)], writes=[(g + "mTs", pb)])
                S.op("dve", lambda e, pb=pb: e.tensor_copy(mTs[pb][:, 1024:2048], psA[:, 1024:2048]), reads=[("psA", 1)], writes=[(g + "mTs", pb)])
                S.op("sp", lambda e, pb=pb, tb=tb: e.dma_start(out=mixT_h[tb], in_=mTs[pb]), reads=[(g + "mTs", pb), (g + "mixa", pb), (g + "mixg", pb)],
                     writes=[("mixT_h", tb)], dma=gd + f"mT{pb}")

            def cap(fn, *a):
                n0 = len(S.ops)
                fn(*a)
                lst = S.ops[n0:]
                del S.ops[n0:]
                return lst

            def units(lst):
                u = []
                for o in lst:
                    if u and o.get("atom") is not None and u[-1][-1].get("atom") == o["atom"]:
                        u[-1].append(o)
                    else:
                        u.append([o])
                return u

            def merge(a, b):
                a, b = units(a), units(b)
                out, i, j = [], 0, 0
                while i < len(a) or j < len(b):
                    if j >= len(b) or (i < len(a) and i * len(b) <= j * len(a)):
                        out.extend(a[i]); i += 1
                    else:
                        out.extend(b[j]); j += 1
                return out

            swa(0)
            for tb in range(NTB):
                sa = []
                if tb >= 1:
                    sa += cap(mixt, tb - 1)
                if tb + 1 < NTB:
                    sa += cap(swa, tb + 1)
                sg_ = cap(gla, tb)
                S.ops.extend(merge(sa, sg_) if PIPE_B else sa + sg_)
            mixt(NTB - 1)
            S.barrier()

        def phase_Bs(l):
            ar.reset()
            g = f"S{ar.gen}."
            gd = "S."
            Pg = [ar.alloc(1536) for _ in range(2)]
            Pkv = ar.alloc(512)
            et = ar.alloc(512)
            lat = ar.alloc(512)
            Ebl = ar.alloc(512)
            kst = ar.alloc(512, BF16)
            gvb = ar.alloc(1024, BF16)
            Dl = ar.alloc(8)
            sq = ar.alloc(256)
            kn = ar.alloc(256, BF16)
            KTo = ar.alloc(512, BF16)
            Vo = ar.alloc(256, BF16)
            S.op("pool", lambda e: e.memset(Sst[:], 0.0), writes=["Sst"])
            S.op("pool", lambda e: e.memset(Dl, 0.0), writes=[g + "Dl"])
            for tb in range(NTB):
                pb = tb % 2
                P = Pg[pb]
                Pk = (g + "P", pb)
                S.op("sp", lambda e, P=P, tb=tb: e.dma_start(out=P, in_=proj[tb * 128:(tb + 1) * 128, 2048:3584]),
                     reads=[("proj", tb)], writes=[Pk], dma=gd + f"P{pb}")
                S.op("sp", lambda e, tb=tb, pb=pb: e.dma_start(out=gzTa[pb][0:16, :], in_=gzT_h[:, tb * 128:(tb + 1) * 128]),
                     reads=["gzT_h"], writes=[("gzTa", pb)], dma=gd + f"gz{pb}")
                S.op("pe", lambda e, pb=pb: e.matmul(psb[PE_][:], gzTa[pb][:], gatew[:], start=True, stop=True),
                     reads=[("gzTa", pb), "gatew0", "gatew1"], writes=[pk(PE_)])
                S.op("act", lambda e: e.activation(et, psb[PE_][:], AF.Exp, scale=-1.0), reads=[pk(PE_)], writes=[g + "et"])
                S.op("act", lambda e: e.activation(lat, et, AF.Ln, bias=1.0), reads=[g + "et"], writes=[g + "lat"])
                S.op("pe", lambda e: e.matmul(psb[PE_][:], strictl, lat, start=True, stop=True), reads=["mats", g + "lat"], writes=[pk(PE_)])
                for h in range(4):
                    S.op("pe", lambda e, h=h: e.matmul(psb[PG][:, 2 * h:2 * h + 2], lat[:, h * 128:(h + 1) * 128], chunksel, start=True, stop=True),
                         reads=["mats", g + "lat"], writes=[pk(PG)])
                S.op("act", lambda e: e.activation(Ebl, psb[PE_][:], AF.Exp), reads=[pk(PE_)], writes=[g + "Ebl"])
                S.op("act", lambda e: e.activation(small[:, 40:48], psb[PG][:, 0:8], AF.Exp), reads=[pk(PG)], writes=["dec"])
                S.op("dve", lambda e: e.tensor_tensor(Dl, psb[PG][:, 0:8], Dl, ALU.add), reads=[pk(PG), g + "Dl"], writes=[g + "Dl"])
                S.op("pool", lambda e, P=P: e.tensor_tensor(kst, P[:, 0:512], Ebl, ALU.mult), reads=[Pk, g + "Ebl"], writes=[g + "kst"])
                S.op("act", lambda e, P=P: e.copy(gvb, P[:, 512:1536]), reads=[Pk], writes=[g + "gvb"])
                for h in range(4):
                    S.op("pe", lambda e, h=h: e.matmul(psb[PD][:, 0:256], kst[0:64, h * 128:(h + 1) * 128], gvb[0:64, h * 256:(h + 1) * 256],
                                                       start=True, stop=True),
                         reads=[g + "kst", g + "gvb"], writes=[pk(PD)])
                    S.op("dve", lambda e, h=h: e.scalar_tensor_tensor(Smid[:, h, :], Sst[:, h, :], small[:, 40 + 2 * h:41 + 2 * h],
                                                                      psb[PD][:, 0:256], ALU.mult, ALU.add),
                         reads=["Sst", "dec", pk(PD)], writes=[("Smid", h)])
                    S.op("pe", lambda e, h=h: e.matmul(psb[PF][:, 0:256], kst[64:128, h * 128:(h + 1) * 128], gvb[64:128, h * 256:(h + 1) * 256],
                                                       start=True, stop=True),
                         reads=[g + "kst", g + "gvb"], writes=[pk(PF)])
                    S.op("dve", lambda e, h=h: e.scalar_tensor_tensor(Sst[:, h, :], Smid[:, h, :], small[:, 41 + 2 * h:42 + 2 * h],
                                                                      psb[PF][:, 0:256], ALU.mult, ALU.add),
                         reads=[("Smid", h), "dec", pk(PF), "Sst"], writes=["Sst"])
            tb = NTB - 1
            S.op("sp", lambda e: e.dma_start(out=Pkv, in_=proj[tb * 128:(tb + 1) * 128, 1024:1536]),
                 reads=[("proj", tb)], writes=[g + "Pkv"], dma=gd + "kv")
            S.op("pool", lambda e: e.tensor_tensor(sq, Pkv[:, 0:256], Pkv[:, 0:256], ALU.mult), reads=[g + "Pkv"], writes=[g + "sq"])
            S.op("dve", lambda e: e.tensor_reduce(small[:, 8:12], sq.rearrange("p (h d) -> p h d", h=4), AX.X, ALU.add),
                 reads=[g + "sq"], writes=["ssq"])
            S.op("act", lambda e: e.activation(small[:, 8:12], small[:, 8:12], AF.Ln, scale=1.0 / 64, bias=EPS), reads=["ssq"], writes=["ssq"])
            S.op("act", lambda e: e.activation(small[:, 8:12], small[:, 8:12], AF.Exp, scale=-0.5), reads=["ssq"], writes=["ssq"])
            S.op("dve", lambda e: e.tensor_tensor(sq.rearrange("p (h d) -> p h d", h=4), Pkv[:, 0:256].rearrange("p (h d) -> p h d", h=4),
                                                  small[:, 8:12].unsqueeze(2).to_broadcast([128, 4, 64]), ALU.mult),
                 reads=[g + "Pkv", "ssq", g + "sq"], writes=[g + "sq"])
            S.op("pool", lambda e: e.tensor_tensor(kn.rearrange("p (h d) -> p h d", h=4), sq.rearrange("p (h d) -> p h d", h=4),
                                                   gqk[:, 16:20, :], ALU.mult), reads=[g + "sq", "gqk_k"], writes=[g + "kn"])
            pgb = psb[PG][:].bitcast(BF16)
            for j in range(4):
                S.op("pe", lambda e, j=j: e.transpose(pgb[0:64, j * 128:(j + 1) * 128], kn[:, j * 64:(j + 1) * 64], ident[:]),
                     reads=[g + "kn", "ident"], writes=[pk(PG)])
            S.op("act", lambda e: e.copy(KTo[0:64, :], pgb[0:64, 0:512]), reads=[pk(PG)], writes=[g + "KTo"])
            S.op("act", lambda e: e.copy(Vo, Pkv[:, 256:512]), reads=[g + "Pkv"], writes=[g + "Vo"])
            S.op("dve", lambda e: e.tensor_tensor(small[:, 12:16], Dl.rearrange("p (h c) -> p h c", c=2)[:, :, 0],
                                                  Dl.rearrange("p (h c) -> p h c", c=2)[:, :, 1], ALU.add),
                 reads=[g + "Dl"], writes=["dl4"])
            S.op("sp", lambda e: e.dma_start(out=pl_S[:, 0:1024], in_=Sst[:].rearrange("p h d -> p (h d)")), reads=["Sst"], dma=gd + "o0")
            S.op("sp", lambda e: e.dma_start(out=pl_S[:, 1024:1028], in_=small[:, 12:16]), reads=["dl4"], dma=gd + "o1")
            S.op("sp", lambda e: e.dma_start(out=pl_KT, in_=KTo[0:64, :]), reads=[g + "KTo"], dma=gd + "o2")
            S.op("sp", lambda e: e.dma_start(out=pl_V, in_=Vo), reads=[g + "Vo"], dma=gd + "o3")
            S.barrier()

        def phase_C(l, x_src, r0):
            ar.reset()
            g = f"C{ar.gen}."
            gd = "C."
            mT = ar.alloc(NTB * 2048, BF16).rearrange("p (b k t) -> p b k t", b=NTB, k=16)
            wt = [ar.alloc(16 * 512, BF16).rearrange("p (k n) -> p k n", k=16) for _ in range(2)]
            xs = [ar.alloc(512) for _ in range(4)]
            for tb in range(NTB):
                S.op("sp", lambda e, tb=tb: e.dma_start(out=mT[:, tb].rearrange("p k t -> p (k t)"), in_=mixT_h[tb]),
                     reads=[("mixT_h", tb)], writes=[(g + "mT", tb)], dma=gd + f"m{tb % 4}")
            cnt = 0
            for nt in range(4):
                w = wt[nt % 2]
                wk = (g + "wt", nt % 2)
                S.op("pool", lambda e, nt=nt, w=w: e.dma_start(
                    out=w, in_=w_out[l, :, nt * 512:(nt + 1) * 512].rearrange("(kc p) n -> p kc n", p=128)),
                    writes=[wk], dma=gd + f"w{nt % 2}")
                for tb in range(NTB):
                    pi = PB + (cnt % 2)
                    x_ = xs[cnt % 4]
                    xk = (g + "xs", cnt % 4)
                    S.op("sp", lambda e, x_=x_, tb=tb, nt=nt: e.dma_start(
                        out=x_, in_=x_src[r0 + tb * 128:r0 + (tb + 1) * 128, nt * 512:(nt + 1) * 512]),
                        writes=[xk], dma=gd + f"xl{cnt % 4}")
                    for kc in range(16):
                        S.op("pe", lambda e, kc=kc, tb=tb, w=w, pi=pi: e.matmul(
                            psb[pi][:], mT[:, tb, kc, :], w[:, kc, :], start=(kc == 0), stop=(kc == 15)),
                            reads=[(g + "mT", tb), wk], writes=[pk(pi)])
                    S.op("dve", lambda e, x_=x_, pi=pi: e.tensor_tensor(x_, psb[pi][:], x_, ALU.add), reads=[pk(pi), xk], writes=[xk])
                    S.op("sp", lambda e, x_=x_, tb=tb, nt=nt: e.dma_start(
                        out=xmid[tb * 128:(tb + 1) * 128, nt * 512:(nt + 1) * 512], in_=x_),
                        reads=[xk], writes=[("xmid", tb)], dma=gd + f"xs{cnt % 4}")
                    cnt += 1
            S.barrier()

        def phase_D(l):
            ar.reset()
            g = f"D{ar.gen}."
            gd = "D."
            xt = [ar.alloc(D) for _ in range(2)]
            hb = [ar.alloc(D, BF16) for _ in range(2)]
            hs = [ar.alloc(2048, BF16) for _ in range(2)]
            for tb in range(NTB):
                i = tb % 2
                hv = hs[i].rearrange("p (k t) -> p k t", k=16)
                norm_block(g + f"{i}", gd + f"{i}", xmid[tb * 128:(tb + 1) * 128, :], g2b, "g2b", xt[i], hb[i],
                           lambda a, b, hv=hv: hv[:, a:b, :], [(g + "hs", i, 0), (g + "hs", i, 1)], 4 * i)
                S.op("sp", lambda e, i=i, tb=tb: e.dma_start(out=h2T_h[tb], in_=hs[i]),
                     reads=[(g + "hs", i, 0), (g + "hs", i, 1), ("xmid", tb)], writes=[("h2T_h", tb)], dma=gd + f"h{i}")
            S.barrier()

        def phase_CD(l, x_src, r0):
            ar.reset()
            g = f"F{ar.gen}."
            gd = "F."
            wo = ar.alloc(4 * 16 * 512, BF16).rearrange("p (n k c) -> p n k c", n=4, k=16)
            mTb = [ar.alloc(2048, BF16).rearrange("p (k t) -> p k t", k=16) for _ in range(2)]
            xr = [ar.alloc(D) for _ in range(2)]
            hb = [ar.alloc(D, BF16) for _ in range(2)]
            hs = [ar.alloc(2048, BF16) for _ in range(2)]
            for nt in range(4):
                S.op("pool", lambda e, nt=nt: e.dma_start(
                    out=wo[:, nt], in_=w_out[l, :, nt * 512:(nt + 1) * 512].rearrange("(kc p) n -> p kc n", p=128)),
                    writes=[(g + "wo", nt)], dma=gd + f"w{nt % 2}")
            cnt = 0
            for tb in range(NTB):
                i = tb % 2
                tag = g + f"{i}"
                S.op("sp", lambda e, i=i, tb=tb: e.dma_start(out=mTb[i].rearrange("p k t -> p (k t)"), in_=mixT_h[tb]),
                     reads=[("mixT_h", tb)], writes=[(g + "mT", i)], dma=gd + f"m{i}")
                S.op("sp", lambda e, i=i, tb=tb: e.dma_start(out=xr[i], in_=x_src[r0 + tb * 128:r0 + (tb + 1) * 128, :]),
                     writes=[tag + "xt"], dma=gd + f"x{i}")
                for nt in range(4):
                    pi = cnt % 6
                    cnt += 1
                    for kc in range(16):
                        S.op("pe", lambda e, kc=kc, nt=nt, i=i, pi=pi: e.matmul(
                            psb[pi][:], mTb[i][:, kc, :], wo[:, nt, kc, :], start=(kc == 0), stop=(kc == 15)),
                            reads=[(g + "mT", i), (g + "wo", nt)], writes=[pk(pi)])
                    S.op("dve", lambda e, nt=nt, i=i, pi=pi: e.tensor_tensor(
                        xr[i][:, nt * 512:(nt + 1) * 512], psb[pi][:], xr[i][:, nt * 512:(nt + 1) * 512], ALU.add),
                        reads=[pk(pi), tag + "xt"], writes=[tag + "xt"])
                S.op("sp", lambda e, i=i, tb=tb: e.dma_start(out=xmid[tb * 128:(tb + 1) * 128, :], in_=xr[i]),
                     reads=[tag + "xt"], writes=[("xmid", tb)], dma=gd + f"s{i}")
                hv = hs[i].rearrange("p (k t) -> p k t", k=16)
                norm_block(tag, gd + f"{i}", None, g2b, "g2b", xr[i], hb[i],
                           lambda a, b, hv=hv: hv[:, a:b, :], [(g + "hs", i, 0), (g + "hs", i, 1)], 4 * i)
                S.op("sp", lambda e, i=i, tb=tb: e.dma_start(out=h2T_h[tb], in_=hs[i]),
                     reads=[(g + "hs", i, 0), (g + "hs", i, 1)], writes=[("h2T_h", tb)], dma=gd + f"h{i}")
            S.barrier()

        def phase_E(l, y_dst, r0):
            ar.reset()
            g = f"E{ar.gen}."
            for half in range(2):
                ar.off = 0
                gd = "E."
                HT = GT // 2
                hT = ar.alloc(16 * HT, BF16).rearrange("p (k t) -> p k t", k=16)
                acc = ar.alloc(8 * D).rearrange("p (b d) -> p b d", b=8)
                wu = ar.alloc(16 * 512, BF16).rearrange("p (k f) -> p k f", k=16)
                wd = ar.alloc(4 * D, BF16).rearrange("p (f n) -> p f n", f=4)
                uT = ar.alloc(4 * HT, BF16).rearrange("p (f t) -> p f t", f=4)
                rt = [ar.alloc(512) for _ in range(2)]
                for b in range(8):
                    tb = half * 8 + b
                    S.op("sp", lambda e, b=b, tb=tb: e.dma_start(out=hT[:, :, b * 128:(b + 1) * 128],
                                                                 in_=h2T_h[tb].rearrange("p (k t) -> p k t", k=16)),
                         reads=[("h2T_h", tb)], writes=[(g + "hT", b)], dma=gd + f"h{b % 4}")
                for b in range(8):
                    tb = half * 8 + b
                    S.op("sp", lambda e, b=b, tb=tb: e.dma_start(out=acc[:, b, :], in_=xmid[tb * 128:(tb + 1) * 128, :]),
                         reads=[("xmid", tb)], writes=[(g + "acc", b)], dma=gd + f"a{b % 4}")
                cu = 0
                cd = 0
                for fcg in range(DFF // 512):
                    S.op("pool", lambda e, fcg=fcg: e.dma_start(
                        out=wu, in_=w_up[l, :, fcg * 512:(fcg + 1) * 512].rearrange("(kc p) f -> p kc f", p=128)),
                        writes=[g + "wu"], dma=gd + "wu")
                    S.op("pool", lambda e, fcg=fcg: e.dma_start(
                        out=wd, in_=w_down[l, fcg * 512:(fcg + 1) * 512, :].rearrange("(fb p) n -> p fb n", p=128)),
                        writes=[g + "wd"], dma=gd + "wd")
                    for fb in range(4):
                        for tt in range(HT // 512):
                            pi = PB + (cu % 2)
                            for kc in range(16):
                                S.op("pe", lambda e, kc=kc, fb=fb, tt=tt, pi=pi: e.matmul(
                                    psb[pi][:], wu[:, kc, fb * 128:(fb + 1) * 128], hT[:, kc, tt * 512:(tt + 1) * 512],
                                    start=(kc == 0), stop=(kc == 15)),
                                    reads=[g + "wu"] + [(g + "hT", b) for b in range(tt * 4, tt * 4 + 4)], writes=[pk(pi)])
                            r = rt[cu % 2]
                            rk = (g + "rt", cu % 2)
                            S.op("act", lambda e, r=r, pi=pi: e.activation(r, psb[pi][:], AF.Relu), reads=[pk(pi)], writes=[rk])
                            S.op("pool", lambda e, r=r, fb=fb, tt=tt: e.tensor_tensor(uT[:, fb, tt * 512:(tt + 1) * 512], r, r, ALU.mult),
                                 reads=[rk], writes=[(g + "uT", tt)])
                            cu += 1
                    for b in range(8):
                        for nt in range(4):
                            pi = PD + (cd % 4)
                            for fb in range(4):
                                S.op("pe", lambda e, fb=fb, b=b, nt=nt, pi=pi: e.matmul(
                                    psb[pi][:], uT[:, fb, b * 128:(b + 1) * 128], wd[:, fb, nt * 512:(nt + 1) * 512],
                                    start=(fb == 0), stop=(fb == 3)),
                                    reads=[(g + "uT", b // 4), g + "wd"], writes=[pk(pi)])
                            S.op("dve", lambda e, b=b, nt=nt, pi=pi: e.tensor_tensor(
                                acc[:, b, nt * 512:(nt + 1) * 512], psb[pi][:], acc[:, b, nt * 512:(nt + 1) * 512], ALU.add),
                                reads=[pk(pi), (g + "acc", b)], writes=[(g + "acc", b)])
                            cd += 1
                for b in range(8):
                    tb = half * 8 + b
                    S.op("sp", lambda e, b=b, tb=tb: e.dma_start(out=y_dst[r0 + tb * 128:r0 + (tb + 1) * 128, :], in_=acc[:, b, :]),
                         reads=[(g + "acc", b)], writes=[("ydst", r0, tb)], dma=gd + f"o{b % 4}")
            S.barrier()

        for step, l in (plan or []):
            layer_consts(l)
            src = x1_out if (mid_out and step == "pre") else x_in
            phase_A(l, src, 0)
            if step == "pre":
                phase_Bs(l)
            else:
                phase_B(l, 0, False, init_payload=True)
                phase_CD(l, src, 0)
                phase_E(l, x1_out if mid_out else y_out, 0)
        for l in range(n_layers if plan is None else 0):
            layer_consts(l)
            src = x_in if l == 0 else xa
            dst = y_out if l == n_layers - 1 else xa
            for gi in range(n_groups):
                r0 = gi * GT
                phase_A(l, src, r0)
                phase_B(l, gi * NTB, gi == 0)
                phase_CD(l, src, r0)
                phase_E(l, dst, r0)
        S.emit(nc)
    return nc, S


_CACHE = {}
MODE = "fused"
PLANS = ([("pre", 0)], [("main", 0), ("pre", 1)], [("main", 1)])


def kernel_unfused(x, **w):
    x = np.asarray(x, dtype=np.float32)
    B = x.shape[0]
    NP = SEQ // GT
    ncores = B * NP
    if "plans" not in _CACHE:
        _CACHE["plans"] = [build_program(GT, DEPTH, True, plan=list(p))[0] for p in PLANS]
    progs = _CACHE["plans"]
    consts = host_consts()
    shared = {k: np.ascontiguousarray(np.asarray(v, dtype=np.float32)) for k, v in w.items()}
    cb_first = consts["c_bias"]
    cb_rest = cb_first.copy()
    cb_rest[:, 2 * 2048:3 * 2048] = cb_first[:, 0:2048]
    cms = []
    for p in range(NP):
        cmk = np.zeros((128, 8), np.float32)
        for j in range(3):
            cmk[:, j] = 1.0 if j < p else 0.0
            cmk[:, 4 + j] = 1.0 - cmk[:, j]
        cms.append(cmk)
    lite_keys = ("norm1_g", "w_in", "q_norm_g", "k_norm_g", "attn_sinks", "gla_gate_w", "gla_gate_b", "gla_norm_g")

    def base(c, lite=False):
        p = c % NP
        d = {k: shared[k] for k in (lite_keys if lite else shared)}
        d.update(c_mats=consts["c_mats"], c_ident=consts["c_ident"], c_bias=cb_first if p == 0 else cb_rest)
        return d

    def halo(res):
        out = []
        zk = np.zeros((64, 512), ml_dtypes.bfloat16)
        zv = np.zeros((128, 256), ml_dtypes.bfloat16)
        for c in range(ncores):
            b, p = divmod(c, NP)
            gs = np.stack([np.asarray(res[b * NP + j]["pl_S"], np.float32) for j in range(NP)], 0)
            out.append(dict(g_S=gs, h_KT=res[c - 1]["pl_KT"] if p > 0 else zk, h_V=res[c - 1]["pl_V"] if p > 0 else zv,
                            cmask=cms[p]))
        return out

    xs = [np.ascontiguousarray(x[c // NP, (c % NP) * GT:(c % NP + 1) * GT]) for c in range(ncores)]
    r1 = run_bass_kernel_spmd(progs[0], [dict(base(c, True), x=xs[c]) for c in range(ncores)], core_ids=list(range(ncores))).results
    h1 = halo(r1)
    r2 = run_bass_kernel_spmd(progs[1], [dict(base(c), x=xs[c], **h1[c]) for c in range(ncores)], core_ids=list(range(ncores))).results
    h2 = halo(r2)
    r3 = run_bass_kernel_spmd(progs[2], [dict(base(c), x=np.asarray(r2[c]["x1"], np.float32), **h2[c]) for c in range(ncores)],
                              core_ids=list(range(ncores))).results
    out = np.empty((B, SEQ, D), np.float32)
    for c in range(ncores):
        out[c // NP, (c % NP) * GT:(c % NP + 1) * GT] = r3[c]["out"]
    return out


def kernel(x, norm1_g, w_in, q_norm_g, k_norm_g, attn_sinks, gla_gate_w, gla_gate_b,
           gla_norm_g, w_out, norm2_g, w_up, w_down):
    if MODE == "unfused":
        return kernel_unfused(x, norm1_g=norm1_g, w_in=w_in, q_norm_g=q_norm_g, k_norm_g=k_norm_g, attn_sinks=attn_sinks,
                              gla_gate_w=gla_gate_w, gla_gate_b=gla_gate_b, gla_norm_g=gla_norm_g, w_out=w_out,
                              norm2_g=norm2_g, w_up=w_up, w_down=w_down)
    x = np.asarray(x, dtype=np.float32)
    B = x.shape[0]
    if "nc" not in _CACHE:
        _CACHE["nc"] = build_program(SEQ, DEPTH)[0]
    nc = _CACHE["nc"]
    consts = host_consts()
    shared = dict(norm1_g=norm1_g, w_in=w_in, q_norm_g=q_norm_g, k_norm_g=k_norm_g, attn_sinks=attn_sinks,
                  gla_gate_w=gla_gate_w, gla_gate_b=gla_gate_b, gla_norm_g=gla_norm_g, w_out=w_out,
                  norm2_g=norm2_g, w_up=w_up, w_down=w_down)
    shared = {k: np.ascontiguousarray(np.asarray(v, dtype=np.float32)) for k, v in shared.items()}
    shared.update(consts)
    in_maps = [dict(shared, x=np.ascontiguousarray(x[b])) for b in range(B)]
    res = run_bass_kernel_spmd(nc, in_maps, core_ids=list(range(B)))
    return np.stack([res.results[b]["out"] for b in range(B)], 0).astype(np.float32)
```

```python
import numpy as np
import ml_dtypes
from contextlib import ExitStack
import concourse.bass as bass
import concourse.mybir as mybir
from concourse.bass_utils import run_bass_kernel_spmd

F32 = mybir.dt.float32
BF16 = mybir.dt.bfloat16
AF = mybir.ActivationFunctionType
ALU = mybir.AluOpType
AX = mybir.AxisListType

D = 2048
SEQ = 8192
DEPTH = 2
INW = 4624
DFF = 8192
PIPE_B = False
GLA_BANKS_PIPE = None
GT = 2048
NTB = GT // 128
EPS = 1e-6
NEG = -30000.0
ENGS = ("pe", "act", "dve", "pool", "sp")


class Sched:
    def __init__(self, same_engine_sync=True):
        self.ops = []
        self.ginc = {}
        self.same = (set(ENGS) if same_engine_sync is True else set() if not same_engine_sync else set(same_engine_sync)) - {'pe'}

    def op(self, eng, fn, reads=(), writes=(), dma=None, inc=16, atom=None):
        self.ops.append(dict(eng=eng, fn=fn, reads=tuple(reads), writes=tuple(writes), dma=dma, inc=inc, bar=False, atom=atom))
        if dma is not None:
            assert self.ginc.setdefault(dma, inc) == inc

    def barrier(self):
        for e in ENGS:
            self.ops.append(dict(eng=e, fn=None, reads=(), writes=(), dma=None, inc=0, bar=True))

    def resolve(self):
        last_w, readers = {}, {}
        eng_count = {e: 0 for e in ENGS}
        eng_last = {}
        dma_count, dma_last = {}, {}
        signal = set()
        for o in self.ops:
            e = o["eng"]
            if o["bar"]:
                deps = set(eng_last.values()) | set(dma_last.values())
                o["ev"] = None
                o["deps"] = deps
                for d in deps:
                    if d[0] == "eng":
                        signal.add((d[1], d[2]))
                continue
            if o["dma"] is None:
                ev = ("eng", e, eng_count[e])
                eng_count[e] += 1
            else:
                g = o["dma"]
                dma_count[g] = dma_count.get(g, 0) + 1
                ev = ("dma", g, dma_count[g])
            deps = set()
            for k in o["reads"]:
                if k in last_w:
                    deps.add(last_w[k])
            for k in o["writes"]:
                if k in last_w:
                    deps.add(last_w[k])
                deps.update(readers.get(k, ()))
            if o["dma"] is not None and o["dma"] in dma_last:
                deps.add(dma_last[o["dma"]])
            deps.discard(ev)
            red = {}
            for d in deps:
                kk = (d[0], d[1])
                if kk not in red or red[kk][2] < d[2]:
                    red[kk] = d
            deps = set(red.values())
            o["ev"], o["deps"] = ev, deps
            for d in deps:
                if d[0] == "eng":
                    if d[1] == e and e not in self.same:
                        continue
                    signal.add((d[1], d[2]))
            for k in o["reads"]:
                readers.setdefault(k, []).append(ev)
            for k in o["writes"]:
                last_w[k] = ev
                readers[k] = []
            if o["dma"] is not None:
                dma_last[o["dma"]] = ev
            else:
                eng_last[e] = ev
        self.signal = signal
        self.dma_groups = sorted(dma_count)

    def emit(self, nc):
        self.resolve()
        with ExitStack() as st:
            EPOCH = 30000
            cnt0 = {e: 0 for e in ENGS}
            idx0 = {e: 0 for e in ENGS}
            for o in self.ops:
                if o["dma"] is None and not o["bar"]:
                    e = o["eng"]
                    if (e, idx0[e]) in self.signal:
                        cnt0[e] += 1
                    idx0[e] += 1
            esem = {(e, k): st.enter_context(nc.semaphore(f"s_{e}{k}")) for e in ("pe", "act", "dve", "pool")
                    for k in range(cnt0[e] // EPOCH + 1)}
            dsem = {g: st.enter_context(nc.semaphore("d_" + str(g))) for g in self.dma_groups}
            block = st.enter_context(nc.Block())
            sigval, cnt, idxc = {}, {e: 0 for e in ENGS}, {e: 0 for e in ENGS}
            for o in self.ops:
                if o["dma"] is None and not o["bar"]:
                    e = o["eng"]
                    i = idxc[e]
                    idxc[e] += 1
                    if (e, i) in self.signal:
                        sigval[(e, i)] = (cnt[e] // EPOCH, cnt[e] % EPOCH + 1)
                        cnt[e] += 1
            self.sig_counts = cnt
            per = {e: [o for o in self.ops if o["eng"] == e] for e in ENGS}
            same = self.same

            def run_engine(e, engobj):
                seen = {}
                myidx = 0
                for o in per[e]:
                    need = {}
                    for d in o["deps"]:
                        if d[0] == "eng":
                            if d[1] == e and e not in same:
                                continue
                            key, val = ("eng", d[1]), sigval[(d[1], d[2])]
                        else:
                            key, val = ("dma", d[1]), (0, self.ginc[d[1]] * d[2])
                        if seen.get(key, (0, 0)) >= val:
                            continue
                        need[key] = max(need.get(key, (0, 0)), val)
                    for key, val in need.items():
                        engobj.wait_ge(esem[(key[1], val[0])] if key[0] == "eng" else dsem[key[1]], val[1])
                        seen[key] = val
                    if o["bar"]:
                        continue
                    ins = o["fn"](engobj)
                    if o["dma"] is not None:
                        ins.then_inc(dsem[o["dma"]], o["inc"])
                    else:
                        if (e, myidx) in self.signal:
                            ins.then_inc(esem[(e, sigval[(e, myidx)][0])], 1)
                        myidx += 1
                final = {}
                for o in per[e]:
                    if o["dma"] is not None:
                        final[o["dma"]] = max(final.get(o["dma"], 0), o["inc"] * o["ev"][2])
                for g, val in final.items():
                    if seen.get(("dma", g), (0, 0)) < (0, val):
                        engobj.wait_ge(dsem[g], val)

            for e, deco in (("sp", block.sync), ("pe", block.tensor), ("act", block.scalar),
                            ("dve", block.vector), ("pool", block.gpsimd)):
                if per[e]:
                    deco(lambda eng, e=e: run_engine(e, eng))


class Arena:
    def __init__(self, tile, ncols):
        self.tile, self.ncols, self.off, self.gen = tile, ncols, 0, 0

    def reset(self):
        self.off = 0
        self.gen += 1

    def alloc(self, cols, dtype=F32):
        n4 = cols if dtype == F32 else (cols + 1) // 2
        assert self.off + n4 <= self.ncols, (self.off, n4, self.ncols)
        v = self.tile[:, self.off:self.off + n4]
        self.off += n4
        return v if dtype == F32 else v.bitcast(BF16)


def host_consts():
    hs = np.arange(16)
    slopes = (2.0 ** (-8.0 * (hs + 1) / 16)).astype(np.float64)
    tk = np.arange(128)[:, None, None]
    tq = np.arange(128)[None, None, :]
    sl = slopes[None, :, None]
    dist_cur = tq - tk
    bias_cur = np.where(dist_cur >= 0, -sl * dist_cur, NEG)
    dist_prev = tq + 128 - tk
    bias_prev = np.where(dist_prev < 128, -sl * dist_prev, NEG)
    bias_first = np.full_like(bias_prev, NEG)
    s = np.arange(128)[:, None]
    t = np.arange(128)[None, :]
    same = (s // 64) == (t // 64)
    triu = np.where(same & (s <= t), -1.0 / 16, 0.0)
    strictl = np.where(same & (s > t), -1.0 / 16, 0.0)
    mask01 = np.where(same & (s <= t), 1.0, 0.0)
    chunksel = np.zeros((128, 2))
    chunksel[:64, 0] = -1.0 / 16
    chunksel[64:, 1] = -1.0 / 16
    c = dict(
        c_bias=np.stack([bias_prev, bias_cur, bias_first], 1).reshape(128, 3 * 2048).astype(np.float32),
        c_mats=np.concatenate([triu, strictl, mask01, chunksel], 1).astype(np.float32),
        c_ident=np.eye(128).astype(ml_dtypes.bfloat16),
    )
    return c


def build_program(n_tok, n_layers=DEPTH, same_sync=True, plan=None):
    n_groups = n_tok // GT
    nc = bass.Bass("TRN2", target_bir_lowering=False)
    dt = lambda name, shape, dtp=F32, kind="ExternalInput": nc.dram_tensor(name, shape, dtp, kind=kind).ap()
    lite = plan is not None and plan == [("pre", 0)]
    x_in = dt("x", [n_tok, D])
    norm1_g = dt("norm1_g", [DEPTH, D])
    w_in = dt("w_in", [DEPTH, D, INW])
    q_norm_g = dt("q_norm_g", [DEPTH, 64])
    k_norm_g = dt("k_norm_g", [DEPTH, 64])
    attn_sinks = dt("attn_sinks", [DEPTH, 16])
    gate_w = dt("gla_gate_w", [DEPTH, 16, 512])
    gate_b = dt("gla_gate_b", [DEPTH, 512])
    gla_norm_g = dt("gla_norm_g", [DEPTH, 256])
    if not lite:
        w_out = dt("w_out", [DEPTH, D, D])
        norm2_g = dt("norm2_g", [DEPTH, D])
        w_up = dt("w_up", [DEPTH, D, DFF])
        w_down = dt("w_down", [DEPTH, DFF, D])
    c_bias = dt("c_bias", [128, 3 * 2048])
    c_mats = dt("c_mats", [128, 386])
    c_ident = dt("c_ident", [128, 128], BF16)
    has_pre = plan is not None and any(p[0] == "pre" for p in plan)
    has_main = plan is not None and any(p[0] == "main" for p in plan)
    mid_out = plan is not None and len(plan) == 2
    y_out = x1_out = None
    if plan is None or plan == [("main", 1)]:
        y_out = dt("out", [n_tok, D], F32, "ExternalOutput")
    if mid_out:
        x1_out = dt("x1", [n_tok, D], F32, "ExternalOutput")
    if has_pre:
        pl_S = dt("pl_S", [128, 1028], F32, "ExternalOutput")
        pl_KT = dt("pl_KT", [64, 512], BF16, "ExternalOutput")
        pl_V = dt("pl_V", [128, 256], BF16, "ExternalOutput")
    if has_main:
        g_S = dt("g_S", [4, 128, 1028])
        h_KT = dt("h_KT", [64, 512], BF16)
        h_V = dt("h_V", [128, 256], BF16)
        cmask_d = dt("cmask", [128, 8])
    xa = nc.dram_tensor("xa", [n_tok, D], F32).ap()
    xmid = nc.dram_tensor("xmid", [GT, D], F32).ap()
    proj = nc.dram_tensor("proj", [GT, 4608], F32).ap()
    gzT_h = nc.dram_tensor("gzT_h", [16, GT], F32).ap()
    mixT_h = nc.dram_tensor("mixT_h", [NTB, 128, 16 * 128], BF16).ap()
    h2T_h = nc.dram_tensor("h2T_h", [NTB, 128, 16 * 128], BF16).ap()

    S = Sched(same_sync)
    with ExitStack() as st:
        sb = lambda name, shape, dtp=F32: st.enter_context(nc.sbuf_tensor(name, shape, dtp))
        ident = sb("ident", [128, 128], BF16)
        mats = sb("mats", [128, 386])
        triu, strictl, mask01, chunksel = mats[:, 0:128], mats[:, 128:256], mats[:, 256:384], mats[:, 384:386]
        g1b = sb("g1b", [128, D])
        g2b = sb("g2b", [128, D])
        g64 = sb("g64", [128, 128])
        gqk = sb("gqk", [128, 20, 64])
        esink = sb("esink", [128, 16])
        ggla = sb("ggla", [128, 256])
        gatew = sb("gatew", [17, 512])
        wgz = sb("wgz", [128, 16, 16], BF16)
        Sst = sb("Sst", [128, 4, 256])
        Smid = sb("Smid", [128, 4, 256])
        Sbf0 = [sb(f"Sbf0_{i}", [128, 4, 256], BF16) for i in range(2)]
        Sbf1 = sb("Sbf1", [128, 4, 256], BF16)
        KT = [sb(f"KT{i}", [64, 512], BF16) for i in range(2)]
        Vaug = [sb(f"Vaug{i}", [128, 4, 65], BF16) for i in range(2)]
        gzTa = [sb(f"gzTa{i}", [17, 128]) for i in range(2)]
        small = sb("small", [128, 64])
        cm = sb("cm", [128, 8])
        ARC = 38000
        arena_t = sb("arena", [128, ARC])
        ar = Arena(arena_t, ARC)
        psA = st.enter_context(nc.psum_tensor("psA", [128, 2048], BF16))
        psb = [st.enter_context(nc.psum_tensor(f"ps{i}", [128, 512], F32)) for i in range(6)]
        PB, PC, PD, PE_, PF, PG = range(6)
        pk = lambda i: ("ps", i)

        S.op("sp", lambda e: e.dma_start(out=ident[:], in_=c_ident), writes=["ident"], dma="c0")
        S.op("sp", lambda e: e.dma_start(out=mats[:], in_=c_mats), writes=["mats"], dma="c1")
        for i in range(2):
            S.op("pool", lambda e, i=i: e.memset(KT[i][:], 0.0), writes=[("KT", i)])
            S.op("pool", lambda e, i=i: e.memset(Vaug[i][:], 1.0), writes=[("Vaug", i)])
            S.op("pool", lambda e, i=i: e.memset(gzTa[i][:], 1.0), writes=[("gzTa", i)])

        def layer_consts(l):
            S.op("sp", lambda e: e.dma_start(out=g1b[:], in_=norm1_g[l].partition_broadcast(128)), writes=["g1b"], dma="c0")
            if not lite:
                S.op("sp", lambda e: e.dma_start(out=g2b[:], in_=norm2_g[l].partition_broadcast(128)), writes=["g2b"], dma="c1")
            S.op("sp", lambda e: e.dma_start(out=g64[:, 0:64], in_=q_norm_g[l].partition_broadcast(128)), writes=["g64a"], dma="c2")
            S.op("sp", lambda e: e.dma_start(out=g64[:, 64:128], in_=k_norm_g[l].partition_broadcast(128)), writes=["g64b"], dma="c3")
            S.op("dve", lambda e: e.tensor_copy(gqk[:, 0:16, :], g64[:, 0:64].unsqueeze(1).to_broadcast([128, 16, 64])),
                 reads=["g64a"], writes=["gqk_q"])
            S.op("dve", lambda e: e.tensor_copy(gqk[:, 16:20, :], g64[:, 64:128].unsqueeze(1).to_broadcast([128, 4, 64])),
                 reads=["g64b"], writes=["gqk_k"])
            S.op("sp", lambda e: e.dma_start(out=esink[:], in_=attn_sinks[l].partition_broadcast(128)), writes=["esink"], dma="c2")
            S.op("act", lambda e: e.activation(esink[:], esink[:], AF.Exp), reads=["esink"], writes=["esink"])
            S.op("sp", lambda e: e.dma_start(out=ggla[:], in_=gla_norm_g[l].partition_broadcast(128)), writes=["ggla"], dma="c3")
            S.op("sp", lambda e: e.dma_start(out=gatew[0:16, :], in_=gate_w[l]), writes=["gatew0"], dma="c0")
            S.op("sp", lambda e: e.dma_start(out=gatew[16:17, :], in_=gate_b[l:l + 1, :]), writes=["gatew1"], dma="c1")
            S.op("pool", lambda e: e.dma_start(out=wgz[:], in_=w_in[l, :, 4608:4624].rearrange("(kc p) n -> p kc n", p=128)),
                 writes=["wgz"], dma="c4")

        def norm_block(tag, dtag, src_rows, gb, gkey, xt, hb, dstT_fn, dst_keys, ssi):
            if src_rows is not None:
                S.op("sp", lambda e: e.dma_start(out=xt, in_=src_rows), writes=[tag + "xt"], dma=dtag + "x")
            S.op("act", lambda e: e.activation(hb, xt, AF.Square, accum_out=small[:, ssi:ssi + 1]),
                 reads=[tag + "xt"], writes=[tag + "hb", ("small", ssi)])
            S.op("act", lambda e: e.activation(small[:, ssi + 1:ssi + 2], small[:, ssi:ssi + 1], AF.Ln, scale=1.0 / D, bias=EPS),
                 reads=[("small", ssi)], writes=[("small", ssi + 1)])
            S.op("act", lambda e: e.activation(small[:, ssi + 2:ssi + 3], small[:, ssi + 1:ssi + 2], AF.Exp, scale=-0.5),
                 reads=[("small", ssi + 1)], writes=[("small", ssi + 2)])
            S.op("dve", lambda e: e.scalar_tensor_tensor(hb, xt, small[:, ssi + 2:ssi + 3], gb[:], ALU.mult, ALU.mult),
                 reads=[tag + "xt", ("small", ssi + 2), gkey, tag + "hb"], writes=[tag + "hb"])
            for kc in range(16):
                S.op("pe", lambda e, kc=kc: e.transpose(psA[:, kc * 128:(kc + 1) * 128], hb[:, kc * 128:(kc + 1) * 128], ident[:]),
                     reads=[tag + "hb", "ident"], writes=[("psA", kc // 8)])
            pv = psA[:].rearrange("p (k t) -> p k t", k=16)
            S.op("act", lambda e: e.copy(dstT_fn(0, 8), pv[:, 0:8, :]), reads=[("psA", 0)], writes=[dst_keys[0]])
            S.op("dve", lambda e: e.tensor_copy(dstT_fn(8, 16), pv[:, 8:16, :]), reads=[("psA", 1)], writes=[dst_keys[1]])

        def phase_A(l, x_src, r0):
            ar.reset()
            g = f"A{ar.gen}."
            gd = "A."
            hT = ar.alloc(16 * GT, BF16).rearrange("p (k t) -> p k t", k=16)
            wt = [ar.alloc(16 * 512, BF16).rearrange("p (k n) -> p k n", k=16) for _ in range(2)]
            xt = [ar.alloc(D) for _ in range(2)]
            hb = [ar.alloc(D, BF16) for _ in range(2)]
            stg = [ar.alloc(512) for _ in range(4)]
            gzs = ar.alloc(GT)
            for tb in range(NTB):
                i = tb % 2
                norm_block(g + f"{i}", gd + f"{i}", x_src[r0 + tb * 128: r0 + (tb + 1) * 128, :], g1b, "g1b", xt[i], hb[i],
                           lambda a, b, tb=tb: hT[:, a:b, tb * 128:(tb + 1) * 128],
                           [(g + "hT", tb, 0), (g + "hT", tb, 1)], 4 * i)
            hkeys = lambda tbs: [(g + "hT", tb, i) for tb in tbs for i in range(2)]
            for tt in range(GT // 512):
                for kc in range(16):
                    S.op("pe", lambda e, kc=kc, tt=tt: e.matmul(psb[PB][0:16, :], wgz[:, kc, :], hT[:, kc, tt * 512:(tt + 1) * 512],
                                                                start=(kc == 0), stop=(kc == 15)),
                         reads=hkeys(range(tt * 4, tt * 4 + 4)) + ["wgz"], writes=[pk(PB)])
                S.op("act", lambda e, tt=tt: e.copy(gzs[0:16, tt * 512:(tt + 1) * 512], psb[PB][0:16, :]),
                     reads=[pk(PB)], writes=[g + "gzs"])
            S.op("sp", lambda e: e.dma_start(out=gzT_h, in_=gzs[0:16, :]), reads=[g + "gzs"], writes=["gzT_h"], dma=gd + "gz")
            cnt = 0
            for nt in range(9):
                w = wt[nt % 2]
                wk = (g + "wt", nt % 2)
                S.op("pool", lambda e, nt=nt, w=w: e.dma_start(
                    out=w, in_=w_in[l, :, nt * 512:(nt + 1) * 512].rearrange("(kc p) n -> p kc n", p=128)),
                    writes=[wk], dma=gd + f"w{nt % 2}")
                for tb in range(NTB):
                    pi = PB + (cnt % 2)
                    for kc in range(16):
                        S.op("pe", lambda e, kc=kc, tb=tb, w=w, pi=pi: e.matmul(
                            psb[pi][:], hT[:, kc, tb * 128:(tb + 1) * 128], w[:, kc, :], start=(kc == 0), stop=(kc == 15)),
                            reads=hkeys([tb]) + [wk], writes=[pk(pi)])
                    sg = stg[cnt % 4]
                    sk = (g + "stg", cnt % 4)
                    if cnt % 2 == 0:
                        S.op("act", lambda e, sg=sg, pi=pi: e.copy(sg, psb[pi][:]), reads=[pk(pi)], writes=[sk])
                    else:
                        S.op("dve", lambda e, sg=sg, pi=pi: e.tensor_copy(sg, psb[pi][:]), reads=[pk(pi)], writes=[sk])
                    S.op("sp", lambda e, sg=sg, tb=tb, nt=nt: e.dma_start(
                        out=proj[tb * 128:(tb + 1) * 128, nt * 512:(nt + 1) * 512], in_=sg),
                        reads=[sk], writes=[("proj", tb)], dma=gd + f"st{cnt % 4}")
                    cnt += 1
            S.barrier()

        def phase_B(l, gblk0, first_group, init_payload=False):
            ar.reset()
            SST_ALL = [("Sst", h) for h in range(4)]
            GLA_BANKS = (PG, PE_, PE_)
            g = f"B{ar.gen}."
            gd = "B."
            bias = ar.alloc(3 * 2048)
            Pt = [ar.alloc(4608) for _ in range(2)]
            sqt = ar.alloc(1280)
            qkn = ar.alloc(1280, BF16)
            QTs = ar.alloc(2048, BF16)
            spt = [ar.alloc(512) for _ in range(2)]
            PT = [[ar.alloc(512, BF16) for _ in range(2)] for _ in range(2)]
            mix = [ar.alloc(D, BF16) for _ in range(2)]
            et = ar.alloc(512)
            lat = ar.alloc(512)
            Eb = ar.alloc(512)
            Enb = ar.alloc(512)
            Ebl = ar.alloc(512)
            qin = ar.alloc(512, BF16)
            kin = ar.alloc(512, BF16)
            kst = ar.alloc(512, BF16)
            gvb = ar.alloc(1024, BF16)
            sgt = ar.alloc(1024)
            qT0 = ar.alloc(512, BF16)
            qT1 = ar.alloc(512, BF16)
            kinT = ar.alloc(512, BF16)
            attm = ar.alloc(512, BF16)
            tmpo = [ar.alloc(256) for _ in range(2)]
            junk = ar.alloc(256)
            mTs = [ar.alloc(2048, BF16) for _ in range(2)]
            S.op("sp", lambda e: e.dma_start(out=bias, in_=c_bias), writes=[g + "bias"], dma=gd + "bias")
            S.op("pool", lambda e: e.memset(qT0, 0.0), writes=[g + "qT0"])
            S.op("pool", lambda e: e.memset(qT1, 0.0), writes=[g + "qT1"])
            if init_payload:
                Sj = ar.alloc(1028)
                tmpc = ar.alloc(256)
                fac = ar.alloc(4)
                S.op("sp", lambda e: e.dma_start(out=cm[:], in_=cmask_d), writes=["cm"], dma=gd + "cm")
                S.op("pool", lambda e: e.memset(Sst[:], 0.0), writes=SST_ALL)
                for j in range(3):
                    S.op("sp", lambda e, j=j: e.dma_start(out=Sj, in_=g_S[j]), writes=[g + "Sj"], dma=gd + "Sj")
                    S.op("act", lambda e: e.activation(fac, Sj[:, 1024:1028], AF.Exp), reads=[g + "Sj"], writes=[g + "fac"])
                    S.op("dve", lambda e, j=j: e.tensor_scalar(fac, fac, cm[:, j:j + 1], cm[:, 4 + j:5 + j], ALU.mult, ALU.add),
                         reads=[g + "fac", "cm"], writes=[g + "fac"])
                    for h in range(4):
                        S.op("dve", lambda e, j=j, h=h: e.tensor_scalar_mul(tmpc, Sj[:, h * 256:(h + 1) * 256], cm[:, j:j + 1]),
                             reads=[g + "Sj", "cm"], writes=[g + "tmpc"])
                        S.op("dve", lambda e, h=h: e.scalar_tensor_tensor(Sst[:, h, :], Sst[:, h, :], fac[:, h:h + 1], tmpc, ALU.mult, ALU.add),
                             reads=[("Sst", h), g + "fac", g + "tmpc"], writes=[("Sst", h)])
                S.op("act", lambda e: e.copy(Sbf0[0][:], Sst[:]), reads=SST_ALL, writes=[("Sbf0", 0, h) for h in range(4)])
                S.op("sp", lambda e: e.dma_start(out=KT[1][:], in_=h_KT), writes=[("KT", 1)], dma=gd + "hk")
                S.op("sp", lambda e: e.dma_start(out=Vaug[1][:, :, 0:64], in_=h_V.rearrange("p (j d) -> p j d", j=4)),
                     writes=[("Vaug", 1)], dma=gd + "hv")
            if first_group:
                for i in range(2):
                    S.op("pool", lambda e, i=i: e.memset(KT[i][:], 0.0), writes=[("KT", i)])
                    S.op("pool", lambda e, i=i: e.memset(Vaug[i][:, :, 0:64], 0.0), writes=[("Vaug", i)])
                S.op("pool", lambda e: e.memset(Sst[:], 0.0), writes=SST_ALL)
                S.op("pool", lambda e: e.memset(Sbf0[0][:], 0.0), writes=[("Sbf0", 0, h) for h in range(4)])
            def swa(tb, part):
                pb = tb % 2
                P = Pt[pb]
                Pk = (g + "P", pb)
                if part == "prep":
                    swa_prep(tb, pb, P, Pk)
                else:
                    swa_main(tb, pb, P, Pk)

            def swa_prep(tb, pb, P, Pk):
                S.op("sp", lambda e, P=P, tb=tb: e.dma_start(out=P, in_=proj[tb * 128:(tb + 1) * 128, :]),
                     reads=[("proj", tb)], writes=[Pk], dma=gd + f"P{pb}")
                S.op("pool", lambda e, P=P: e.tensor_tensor(sqt, P[:, 0:1280], P[:, 0:1280], ALU.mult), reads=[Pk], writes=[g + "sqt"])
                S.op("dve", lambda e: e.tensor_reduce(small[:, 8:28], sqt.rearrange("p (h d) -> p h d", h=20), AX.X, ALU.add),
                     reads=[g + "sqt"], writes=["ssq"])
                S.op("act", lambda e: e.activation(small[:, 8:28], small[:, 8:28], AF.Ln, scale=1.0 / 64, bias=EPS),
                     reads=["ssq"], writes=["ssq"])
                S.op("act", lambda e: e.activation(small[:, 8:28], small[:, 8:28], AF.Exp, scale=-0.5), reads=["ssq"], writes=["ssq"])
                S.op("dve", lambda e, P=P: e.tensor_tensor(sqt.rearrange("p (h d) -> p h d", h=20),
                                                          P[:, 0:1280].rearrange("p (h d) -> p h d", h=20),
                                                          small[:, 8:28].unsqueeze(2).to_broadcast([128, 20, 64]), ALU.mult),
                     reads=[Pk, "ssq", g + "sqt"], writes=[g + "sqt"])
                S.op("pool", lambda e: e.tensor_tensor(qkn.rearrange("p (h d) -> p h d", h=20),
                                                      sqt.rearrange("p (h d) -> p h d", h=20), gqk[:], ALU.mult),
                     reads=[g + "sqt", "gqk_q", "gqk_k"], writes=[g + "qkn"])
                S.op("act", lambda e, P=P, pb=pb: e.copy(Vaug[pb][:, :, 0:64], P[:, 1280:1536].rearrange("p (j d) -> p j d", j=4)),
                     reads=[Pk], writes=[("Vaug", pb)])

            def swa_main(tb, pb, P, Pk):
                pbb = psb[PB][:].bitcast(BF16)
                for j in range(4):
                    S.op("pe", lambda e, j=j: e.transpose(pbb[0:64, j * 128:(j + 1) * 128], qkn[:, 1024 + j * 64:1024 + (j + 1) * 64], ident[:]),
                         reads=[g + "qkn", "ident"], writes=[pk(PB)])
                S.op("act", lambda e, pb=pb: e.copy(KT[pb][:], pbb[0:64, 0:512]), reads=[pk(PB)], writes=[("KT", pb)])
                for h in range(16):
                    S.op("pe", lambda e, h=h: e.transpose(psA[0:64, h * 128:(h + 1) * 128], qkn[:, h * 64:(h + 1) * 64], ident[:]),
                         reads=[g + "qkn", "ident"], writes=[("psA", h // 8)])
                S.op("dve", lambda e: e.tensor_copy(QTs[0:64, 0:1024], psA[0:64, 0:1024]), reads=[("psA", 0)], writes=[g + "QT0"])
                S.op("act", lambda e: e.copy(QTs[0:64, 1024:2048], psA[0:64, 1024:2048]), reads=[("psA", 1)], writes=[g + "QT1"])
                first_blk = (first_group or init_payload) and tb == 0
                def scores(j):
                    for c in range(2):
                        cnt = 2 * j + c
                        src = 1 - pb if c == 0 else pb
                        pi = PB + (cnt % 2)
                        bsel = (2 if first_blk else 0) if c == 0 else 1
                        S.op("pe", lambda e, j=j, src=src, pi=pi: e.matmul(psb[pi][:], KT[src][0:64, j * 128:(j + 1) * 128],
                                                                         QTs[0:64, j * 512:(j + 1) * 512], start=True, stop=True),
                             reads=[("KT", src), g + "QT0", g + "QT1"], writes=[pk(pi)])
                        S.op("dve", lambda e, j=j, pi=pi, bsel=bsel, cnt=cnt: e.scalar_tensor_tensor(
                            spt[cnt % 2], psb[pi][:], 0.125, bias[:, bsel * 2048 + j * 512: bsel * 2048 + (j + 1) * 512], ALU.mult, ALU.add),
                            reads=[pk(pi), g + "bias"], writes=[(g + "spt", cnt % 2)])
                        S.op("act", lambda e, j=j, c=c, cnt=cnt: e.activation(PT[j % 2][c], spt[cnt % 2], AF.Exp),
                             reads=[(g + "spt", cnt % 2)], writes=[(g + "PT", j % 2, c)])

                def pv(j):
                    pdv = psb[PD][:, 0:260].rearrange("p (h d) -> p h d", h=4)
                    for hl in range(4):
                        for c in range(2):
                            src = 1 - pb if c == 0 else pb
                            S.op("pe", lambda e, j=j, hl=hl, c=c, src=src: e.matmul(
                                psb[PD][:, hl * 65:(hl + 1) * 65], PT[j % 2][c][:, hl * 128:(hl + 1) * 128], Vaug[src][:, j, :],
                                start=(c == 0), stop=(c == 1)),
                                reads=[(g + "PT", j % 2, c), ("Vaug", src)], writes=[pk(PD)], atom=("pv", tb, j, hl))
                    S.op("dve", lambda e, j=j: e.tensor_tensor(small[:, 32:36], pdv[:, :, 64], esink[:, 4 * j:4 * j + 4], ALU.add),
                         reads=[pk(PD), "esink"], writes=["den"])
                    S.op("dve", lambda e: e.reciprocal(small[:, 32:36], small[:, 32:36]), reads=["den"], writes=["den"])
                    S.op("dve", lambda e, j=j, pb=pb: e.tensor_tensor(
                        mix[pb][:, j * 256:(j + 1) * 256].rearrange("p (h d) -> p h d", h=4), pdv[:, :, 0:64],
                        small[:, 32:36].unsqueeze(2).to_broadcast([128, 4, 64]), ALU.mult),
                        reads=[pk(PD), "den"], writes=[(g + "mixa", pb)])

                scores(0)
                for j in range(4):
                    if j + 1 < 4:
                        scores(j + 1)
                    pv(j)

            def gla(tb, part):
                pb = tb % 2
                P = Pt[pb]
                Pk = (g + "P", pb)
                WE = [pk(PE_)]
                TBK, ABK, UBK = GLA_BANKS
                WF = [pk(PF)]
                WG = [pk(PG)]
                S.op("sp", lambda e, tb=tb, pb=pb: e.dma_start(out=gzTa[pb][0:16, :], in_=gzT_h[:, tb * 128:(tb + 1) * 128]),
                     reads=["gzT_h"], writes=[("gzTa", pb)], dma=gd + f"gz{pb}")
                S.op("pe", lambda e, pb=pb: e.matmul(psb[PE_][:], gzTa[pb][:], gatew[:], start=True, stop=True),
                     reads=[("gzTa", pb), "gatew0", "gatew1"], writes=WE)
                S.op("act", lambda e: e.activation(et, psb[PE_][:], AF.Exp, scale=-1.0), reads=[pk(PE_)], writes=[g + "et"])
                S.op("act", lambda e: e.activation(lat, et, AF.Ln, bias=1.0), reads=[g + "et"], writes=[g + "lat"])
                S.op("pe", lambda e: e.matmul(psb[PF][:], triu, lat, start=True, stop=True), reads=["mats", g + "lat"], writes=WF)
                for h in range(4):
                    S.op("pe", lambda e, h=h: e.matmul(psb[PE_][:, 2 * h:2 * h + 2], lat[:, h * 128:(h + 1) * 128], chunksel, start=True, stop=True),
                         reads=["mats", g + "lat"], writes=WE)
                S.op("act", lambda e: e.activation(small[:, 40:48], psb[PE_][:, 0:8], AF.Exp), reads=WE, writes=["dec"])
                S.op("pe", lambda e: e.matmul(psb[PE_][:], strictl, lat, start=True, stop=True), reads=["mats", g + "lat"], writes=WE)
                S.op("act", lambda e: e.activation(Eb, psb[PF][:], AF.Exp), reads=WF, writes=[g + "Eb"])
                S.op("act", lambda e: e.activation(Enb, psb[PF][:], AF.Exp, scale=-1.0), reads=WF, writes=[g + "Enb"])
                S.op("act", lambda e: e.activation(Ebl, psb[PE_][:], AF.Exp), reads=[pk(PE_)], writes=[g + "Ebl"])
                S.op("dve", lambda e, P=P: e.scalar_tensor_tensor(qin, P[:, 1536:2048], 128.0 ** -0.5, Eb, ALU.mult, ALU.mult),
                     reads=[Pk, g + "Eb"], writes=[g + "qin"])
                S.op("pool", lambda e, P=P: e.tensor_tensor(kin, P[:, 2048:2560], Enb, ALU.mult), reads=[Pk, g + "Enb"], writes=[g + "kin"])
                S.op("pool", lambda e, P=P: e.tensor_tensor(kst, P[:, 2048:2560], Ebl, ALU.mult), reads=[Pk, g + "Ebl"], writes=[g + "kst"])
                S.op("act", lambda e, P=P: e.copy(gvb, P[:, 2560:3584]), reads=[Pk], writes=[g + "gvb"])
                S.op("act", lambda e, P=P: e.activation(sgt, P[:, 3584:4608], AF.Exp, scale=-1.0), reads=[Pk], writes=[g + "sgt"])
                S.op("dve", lambda e: e.tensor_scalar_add(sgt, sgt, 1.0), reads=[g + "sgt"], writes=[g + "sgt"])
                S.op("dve", lambda e: e.reciprocal(sgt, sgt), reads=[g + "sgt"], writes=[g + "sgt"])
                S.op("pool", lambda e, P=P: e.tensor_tensor(sgt, sgt, P[:, 3584:4608], ALU.mult), reads=[g + "sgt", Pk], writes=[g + "sgt"])
                if part == "head":
                    return
                pfb = psb[TBK][:].bitcast(BF16)
                for h in range(4):
                    S.op("pe", lambda e, h=h: e.transpose(pfb[:, h * 128:(h + 1) * 128], qin[:, h * 128:(h + 1) * 128], ident[:]),
                         reads=[g + "qin", "ident"], writes=[pk(TBK)])
                for h in range(4):
                    S.op("pe", lambda e, h=h: e.transpose(pfb[:, 512 + h * 128:512 + (h + 1) * 128], kin[:, h * 128:(h + 1) * 128], ident[:]),
                         reads=[g + "kin", "ident"], writes=[pk(TBK)])
                v3 = lambda ap: ap.rearrange("p (h t) -> p h t", h=4)
                S.op("act", lambda e: e.copy(v3(qT0)[:, :, 0:64], v3(pfb[:, 0:512])[:, :, 0:64]), reads=[pk(TBK)], writes=[g + "qT0"])
                S.op("dve", lambda e: e.tensor_copy(v3(qT1)[:, :, 64:128], v3(pfb[:, 0:512])[:, :, 64:128]), reads=[pk(TBK)], writes=[g + "qT1"])
                S.op("act", lambda e: e.copy(kinT, pfb[:, 512:1024]), reads=[pk(TBK)], writes=[g + "kinT"])
                for h in range(4):
                    S.op("pe", lambda e, h=h: e.matmul(psb[ABK][:, h * 128:h * 128 + 64], kinT[:, h * 128:(h + 1) * 128],
                                                       qT0[:, h * 128:h * 128 + 64], start=True, stop=True),
                         reads=[g + "kinT", g + "qT0"], writes=[pk(ABK)])
                    S.op("pe", lambda e, h=h: e.matmul(psb[ABK][:, h * 128 + 64:(h + 1) * 128], kinT[:, h * 128:(h + 1) * 128],
                                                       qT1[:, h * 128 + 64:(h + 1) * 128], start=True, stop=True),
                         reads=[g + "kinT", g + "qT1"], writes=[pk(ABK)])
                S.op("dve", lambda e: e.tensor_tensor(v3(attm), v3(psb[ABK][:]), mask01.unsqueeze(1).to_broadcast([128, 4, 128]), ALU.mult),
                     reads=[pk(ABK), "mats"], writes=[g + "attm"])
                s0 = Sbf0[pb]
                s0n = Sbf0[1 - pb]
                for h in range(4):
                    S.op("pe", lambda e, h=h: e.matmul(psb[UBK][:, 0:256], kst[0:64, h * 128:(h + 1) * 128], gvb[0:64, h * 256:(h + 1) * 256],
                                                       start=True, stop=True),
                         reads=[g + "kst", g + "gvb"], writes=[pk(UBK)])
                    S.op("dve", lambda e, h=h: e.scalar_tensor_tensor(Smid[:, h, :], Sst[:, h, :], small[:, 40 + 2 * h:41 + 2 * h],
                                                                      psb[UBK][:, 0:256], ALU.mult, ALU.add),
                         reads=[("Sst", h), "dec", pk(UBK)], writes=[("Smid", h)])
                    S.op("act", lambda e, h=h: e.copy(Sbf1[:, h, :], Smid[:, h, :]), reads=[("Smid", h)], writes=[("Sbf1", h)])
                    S.op("pe", lambda e, h=h: e.matmul(psb[PF][:, 0:256], kst[64:128, h * 128:(h + 1) * 128], gvb[64:128, h * 256:(h + 1) * 256],
                                                       start=True, stop=True),
                         reads=[g + "kst", g + "gvb"], writes=[pk(PF)])
                    og = psb[PG][:, 0:256]
                    ogk = pk(PG)
                    S.op("pe", lambda e, h=h, og=og: e.matmul(og, attm[:, h * 128:(h + 1) * 128], gvb[:, h * 256:(h + 1) * 256], start=True, stop=False),
                         reads=[g + "attm", g + "gvb"], writes=[ogk], atom=("og", tb, h))
                    S.op("pe", lambda e, h=h, og=og, s0=s0: e.matmul(og, qT0[:, h * 128:(h + 1) * 128], s0[:, h, :], start=False, stop=False),
                         reads=[g + "qT0", ("Sbf0", pb, h)], writes=[ogk], atom=("og", tb, h))
                    S.op("pe", lambda e, h=h, og=og: e.matmul(og, qT1[:, h * 128:(h + 1) * 128], Sbf1[:, h, :], start=False, stop=True),
                         reads=[g + "qT1", ("Sbf1", h)], writes=[ogk], atom=("og", tb, h))
                    S.op("dve", lambda e, h=h: e.scalar_tensor_tensor(Sst[:, h, :], Smid[:, h, :], small[:, 41 + 2 * h:42 + 2 * h],
                                                                      psb[PF][:, 0:256], ALU.mult, ALU.add),
                         reads=[("Smid", h), "dec", pk(PF), ("Sst", h)], writes=[("Sst", h)])
                    S.op("act", lambda e, h=h, s0n=s0n: e.copy(s0n[:, h, :], Sst[:, h, :]), reads=[("Sst", h)], writes=[("Sbf0", 1 - pb, h)])
                    S.op("act", lambda e, h=h, og=og: e.activation(junk, og, AF.Square, accum_out=small[:, 48 + h:49 + h]),
                         reads=[ogk], writes=[g + "junk", ("sso", h)])
                    S.op("act", lambda e, h=h: e.activation(small[:, 52 + h:53 + h], small[:, 48 + h:49 + h], AF.Ln, scale=1.0 / 256, bias=EPS),
                         reads=[("sso", h)], writes=[("sso2", h)])
                    S.op("act", lambda e, h=h: e.activation(small[:, 56 + h:57 + h], small[:, 52 + h:53 + h], AF.Exp, scale=-0.5),
                         reads=[("sso2", h)], writes=[("sso3", h)])
                    S.op("dve", lambda e, h=h, og=og: e.scalar_tensor_tensor(tmpo[h % 2], og, small[:, 56 + h:57 + h], ggla[:], ALU.mult, ALU.mult),
                         reads=[ogk, ("sso3", h), "ggla"], writes=[(g + "tmpo", h % 2)])
                    S.op("pool", lambda e, h=h, pb=pb: e.tensor_tensor(mix[pb][:, 1024 + h * 256:1024 + (h + 1) * 256], tmpo[h % 2],
                                                                      sgt[:, h * 256:(h + 1) * 256], ALU.mult),
                         reads=[(g + "tmpo", h % 2), g + "sgt"], writes=[(g + "mixg", pb)])

            def mixt(tb):
                pb = tb % 2
                for kc in range(16):
                    S.op("pe", lambda e, kc=kc, pb=pb: e.transpose(psA[:, kc * 128:(kc + 1) * 128], mix[pb][:, kc * 128:(kc + 1) * 128], ident[:]),
                         reads=[(g + "mixa", pb), (g + "mixg", pb), "ident"], writes=[("psA", kc // 8)])
                S.op("act", lambda e, pb=pb: e.copy(mTs[pb][:, 0:1024], psA[:, 0:1024]), reads=[("psA", 0)], writes=[(g + "mTs", pb)])
                S.op("dve", lambda e, pb=pb: e.tensor_copy(mTs[pb][:, 1024:2048], psA[:, 1024:2048]), reads=[("psA", 1)], writes=[(g + "mTs", pb)])
                S.op("sp", lambda e, pb=pb, tb=tb: e.dma_start(out=mixT_h[tb], in_=mTs[pb]), reads=[(g + "mTs", pb), (g + "mixa", pb), (g + "mixg", pb)],
                     writes=[("mixT_h", tb)], dma=gd + f"mT{pb}")

            def cap(fn, *a):
                n0 = len(S.ops)
                fn(*a)
                lst = S.ops[n0:]
                del S.ops[n0:]
                return lst

            def units(lst):
                u = []
                for o in lst:
                    if u and o.get("atom") is not None and u[-1][-1].get("atom") == o["atom"]:
                        u[-1].append(o)
                    else:
                        u.append([o])
                return u

            def merge(a, b):
                a, b = units(a), units(b)
                out, i, j = [], 0, 0
                while i < len(a) or j < len(b):
                    if j >= len(b) or (i < len(a) and i * len(b) <= j * len(a)):
                        out.extend(a[i]); i += 1
                    else:
                        out.extend(b[j]); j += 1
                return out

            def gla_tail(tb):
                n_head = len(cap(gla, tb, "head"))
                full = cap(gla, tb, "tail")
                S.ops.extend(full[n_head:])

            swa(0, "prep")
            swa(0, "main")
            swa(1, "prep")
            for tb in range(NTB):
                if tb >= 1:
                    mixt(tb - 1)
                if tb + 1 < NTB:
                    swa(tb + 1, "main")
                gla(tb, "head")
                if tb + 2 < NTB:
                    swa(tb + 2, "prep")
                gla_tail(tb)
            mixt(NTB - 1)
            S.barrier()

        def phase_Bs(l):
            ar.reset()
            g = f"S{ar.gen}."
            gd = "S."
            Pg = [ar.alloc(1536) for _ in range(2)]
            Pkv = ar.alloc(512)
            et = ar.alloc(512)
            lat = ar.alloc(512)
            Ebl = ar.alloc(512)
            kst = ar.alloc(512, BF16)
            gvb = ar.alloc(1024, BF16)
            Dl = ar.alloc(8)
            sq = ar.alloc(256)
            kn = ar.alloc(256, BF16)
            KTo = ar.alloc(512, BF16)
            Vo = ar.alloc(256, BF16)
            S.op("pool", lambda e: e.memset(Sst[:], 0.0), writes=["Sst"])
            S.op("pool", lambda e: e.memset(Dl, 0.0), writes=[g + "Dl"])
            for tb in range(NTB):
                pb = tb % 2
                P = Pg[pb]
                Pk = (g + "P", pb)
                S.op("sp", lambda e, P=P, tb=tb: e.dma_start(out=P, in_=proj[tb * 128:(tb + 1) * 128, 2048:3584]),
                     reads=[("proj", tb)], writes=[Pk], dma=gd + f"P{pb}")
                S.op("sp", lambda e, tb=tb, pb=pb: e.dma_start(out=gzTa[pb][0:16, :], in_=gzT_h[:, tb * 128:(tb + 1) * 128]),
                     reads=["gzT_h"], writes=[("gzTa", pb)], dma=gd + f"gz{pb}")
                S.op("pe", lambda e, pb=pb: e.matmul(psb[PE_][:], gzTa[pb][:], gatew[:], start=True, stop=True),
                     reads=[("gzTa", pb), "gatew0", "gatew1"], writes=[pk(PE_)])
                S.op("act", lambda e: e.activation(et, psb[PE_][:], AF.Exp, scale=-1.0), reads=[pk(PE_)], writes=[g + "et"])
                S.op("act", lambda e: e.activation(lat, et, AF.Ln, bias=1.0), reads=[g + "et"], writes=[g + "lat"])
                S.op("pe", lambda e: e.matmul(psb[PE_][:], strictl, lat, start=True, stop=True), reads=["mats", g + "lat"], writes=[pk(PE_)])
                for h in range(4):
                    S.op("pe", lambda e, h=h: e.matmul(psb[PG][:, 2 * h:2 * h + 2], lat[:, h * 128:(h + 1) * 128], chunksel, start=True, stop=True),
                         reads=["mats", g + "lat"], writes=[pk(PG)])
                S.op("act", lambda e: e.activation(Ebl, psb[PE_][:], AF.Exp), reads=[pk(PE_)], writes=[g + "Ebl"])
                S.op("act", lambda e: e.activation(small[:, 40:48], psb[PG][:, 0:8], AF.Exp), reads=[pk(PG)], writes=["dec"])
                S.op("dve", lambda e: e.tensor_tensor(Dl, psb[PG][:, 0:8], Dl, ALU.add), reads=[pk(PG), g + "Dl"], writes=[g + "Dl"])
                S.op("pool", lambda e, P=P: e.tensor_tensor(kst, P[:, 0:512], Ebl, ALU.mult), reads=[Pk, g + "Ebl"], writes=[g + "kst"])
                S.op("act", lambda e, P=P: e.copy(gvb, P[:, 512:1536]), reads=[Pk], writes=[g + "gvb"])
                for h in range(4):
                    S.op("pe", lambda e, h=h: e.matmul(psb[PD][:, 0:256], kst[0:64, h * 128:(h + 1) * 128], gvb[0:64, h * 256:(h + 1) * 256],
                                                       start=True, stop=True),
                         reads=[g + "kst", g + "gvb"], writes=[pk(PD)])
                    S.op("dve", lambda e, h=h: e.scalar_tensor_tensor(Smid[:, h, :], Sst[:, h, :], small[:, 40 + 2 * h:41 + 2 * h],
                                                                      psb[PD][:, 0:256], ALU.mult, ALU.add),
                         reads=["Sst", "dec", pk(PD)], writes=[("Smid", h)])
                    S.op("pe", lambda e, h=h: e.matmul(psb[PF][:, 0:256], kst[64:128, h * 128:(h + 1) * 128], gvb[64:128, h * 256:(h + 1) * 256],
                                                       start=True, stop=True),
                         reads=[g + "kst", g + "gvb"], writes=[pk(PF)])
                    S.op("dve", lambda e, h=h: e.scalar_tensor_tensor(Sst[:, h, :], Smid[:, h, :], small[:, 41 + 2 * h:42 + 2 * h],
                                                                      psb[PF][:, 0:256], ALU.mult, ALU.add),
                         reads=[("Smid", h), "dec", pk(PF), "Sst"], writes=["Sst"])
            tb = NTB - 1
            S.op("sp", lambda e: e.dma_start(out=Pkv, in_=proj[tb * 128:(tb + 1) * 128, 1024:1536]),
                 reads=[("proj", tb)], writes=[g + "Pkv"], dma=gd + "kv")
            S.op("pool", lambda e: e.tensor_tensor(sq, Pkv[:, 0:256], Pkv[:, 0:256], ALU.mult), reads=[g + "Pkv"], writes=[g + "sq"])
            S.op("dve", lambda e: e.tensor_reduce(small[:, 8:12], sq.rearrange("p (h d) -> p h d", h=4), AX.X, ALU.add),
                 reads=[g + "sq"], writes=["ssq"])
            S.op("act", lambda e: e.activation(small[:, 8:12], small[:, 8:12], AF.Ln, scale=1.0 / 64, bias=EPS), reads=["ssq"], writes=["ssq"])
            S.op("act", lambda e: e.activation(small[:, 8:12], small[:, 8:12], AF.Exp, scale=-0.5), reads=["ssq"], writes=["ssq"])
            S.op("dve", lambda e: e.tensor_tensor(sq.rearrange("p (h d) -> p h d", h=4), Pkv[:, 0:256].rearrange("p (h d) -> p h d", h=4),
                                                  small[:, 8:12].unsqueeze(2).to_broadcast([128, 4, 64]), ALU.mult),
                 reads=[g + "Pkv", "ssq", g + "sq"], writes=[g + "sq"])
            S.op("pool", lambda e: e.tensor_tensor(kn.rearrange("p (h d) -> p h d", h=4), sq.rearrange("p (h d) -> p h d", h=4),
                                                   gqk[:, 16:20, :], ALU.mult), reads=[g + "sq", "gqk_k"], writes=[g + "kn"])
            pgb = psb[PG][:].bitcast(BF16)
            for j in range(4):
                S.op("pe", lambda e, j=j: e.transpose(pgb[0:64, j * 128:(j + 1) * 128], kn[:, j * 64:(j + 1) * 64], ident[:]),
                     reads=[g + "kn", "ident"], writes=[pk(PG)])
            S.op("act", lambda e: e.copy(KTo[0:64, :], pgb[0:64, 0:512]), reads=[pk(PG)], writes=[g + "KTo"])
            S.op("act", lambda e: e.copy(Vo, Pkv[:, 256:512]), reads=[g + "Pkv"], writes=[g + "Vo"])
            S.op("dve", lambda e: e.tensor_tensor(small[:, 12:16], Dl.rearrange("p (h c) -> p h c", c=2)[:, :, 0],
                                                  Dl.rearrange("p (h c) -> p h c", c=2)[:, :, 1], ALU.add),
                 reads=[g + "Dl"], writes=["dl4"])
            S.op("sp", lambda e: e.dma_start(out=pl_S[:, 0:1024], in_=Sst[:].rearrange("p h d -> p (h d)")), reads=["Sst"], dma=gd + "o0")
            S.op("sp", lambda e: e.dma_start(out=pl_S[:, 1024:1028], in_=small[:, 12:16]), reads=["dl4"], dma=gd + "o1")
            S.op("sp", lambda e: e.dma_start(out=pl_KT, in_=KTo[0:64, :]), reads=[g + "KTo"], dma=gd + "o2")
            S.op("sp", lambda e: e.dma_start(out=pl_V, in_=Vo), reads=[g + "Vo"], dma=gd + "o3")
            S.barrier()

        def phase_C(l, x_src, r0):
            ar.reset()
            g = f"C{ar.gen}."
            gd = "C."
            mT = ar.alloc(NTB * 2048, BF16).rearrange("p (b k t) -> p b k t", b=NTB, k=16)
            wt = [ar.alloc(16 * 512, BF16).rearrange("p (k n) -> p k n", k=16) for _ in range(2)]
            xs = [ar.alloc(512) for _ in range(4)]
            for tb in range(NTB):
                S.op("sp", lambda e, tb=tb: e.dma_start(out=mT[:, tb].rearrange("p k t -> p (k t)"), in_=mixT_h[tb]),
                     reads=[("mixT_h", tb)], writes=[(g + "mT", tb)], dma=gd + f"m{tb % 4}")
            cnt = 0
            for nt in range(4):
                w = wt[nt % 2]
                wk = (g + "wt", nt % 2)
                S.op("pool", lambda e, nt=nt, w=w: e.dma_start(
                    out=w, in_=w_out[l, :, nt * 512:(nt + 1) * 512].rearrange("(kc p) n -> p kc n", p=128)),
                    writes=[wk], dma=gd + f"w{nt % 2}")
                for tb in range(NTB):
                    pi = PB + (cnt % 2)
                    x_ = xs[cnt % 4]
                    xk = (g + "xs", cnt % 4)
                    S.op("sp", lambda e, x_=x_, tb=tb, nt=nt: e.dma_start(
                        out=x_, in_=x_src[r0 + tb * 128:r0 + (tb + 1) * 128, nt * 512:(nt + 1) * 512]),
                        writes=[xk], dma=gd + f"xl{cnt % 4}")
                    for kc in range(16):
                        S.op("pe", lambda e, kc=kc, tb=tb, w=w, pi=pi: e.matmul(
                            psb[pi][:], mT[:, tb, kc, :], w[:, kc, :], start=(kc == 0), stop=(kc == 15)),
                            reads=[(g + "mT", tb), wk], writes=[pk(pi)])
                    S.op("dve", lambda e, x_=x_, pi=pi: e.tensor_tensor(x_, psb[pi][:], x_, ALU.add), reads=[pk(pi), xk], writes=[xk])
                    S.op("sp", lambda e, x_=x_, tb=tb, nt=nt: e.dma_start(
                        out=xmid[tb * 128:(tb + 1) * 128, nt * 512:(nt + 1) * 512], in_=x_),
                        reads=[xk], writes=[("xmid", tb)], dma=gd + f"xs{cnt % 4}")
                    cnt += 1
            S.barrier()

        def phase_D(l):
            ar.reset()
            g = f"D{ar.gen}."
            gd = "D."
            xt = [ar.alloc(D) for _ in range(2)]
            hb = [ar.alloc(D, BF16) for _ in range(2)]
            hs = [ar.alloc(2048, BF16) for _ in range(2)]
            for tb in range(NTB):
                i = tb % 2
                hv = hs[i].rearrange("p (k t) -> p k t", k=16)
                norm_block(g + f"{i}", gd + f"{i}", xmid[tb * 128:(tb + 1) * 128, :], g2b, "g2b", xt[i], hb[i],
                           lambda a, b, hv=hv: hv[:, a:b, :], [(g + "hs", i, 0), (g + "hs", i, 1)], 4 * i)
                S.op("sp", lambda e, i=i, tb=tb: e.dma_start(out=h2T_h[tb], in_=hs[i]),
                     reads=[(g + "hs", i, 0), (g + "hs", i, 1), ("xmid", tb)], writes=[("h2T_h", tb)], dma=gd + f"h{i}")
            S.barrier()

        def phase_CD(l, x_src, r0):
            ar.reset()
            g = f"F{ar.gen}."
            gd = "F."
            wo = ar.alloc(4 * 16 * 512, BF16).rearrange("p (n k c) -> p n k c", n=4, k=16)
            mTb = [ar.alloc(2048, BF16).rearrange("p (k t) -> p k t", k=16) for _ in range(2)]
            xr = [ar.alloc(D) for _ in range(2)]
            hb = [ar.alloc(D, BF16) for _ in range(2)]
            hs = [ar.alloc(2048, BF16) for _ in range(2)]
            for nt in range(4):
                S.op("pool", lambda e, nt=nt: e.dma_start(
                    out=wo[:, nt], in_=w_out[l, :, nt * 512:(nt + 1) * 512].rearrange("(kc p) n -> p kc n", p=128)),
                    writes=[(g + "wo", nt)], dma=gd + f"w{nt % 2}")
            cnt = 0
            for tb in range(NTB):
                i = tb % 2
                tag = g + f"{i}"
                S.op("sp", lambda e, i=i, tb=tb: e.dma_start(out=mTb[i].rearrange("p k t -> p (k t)"), in_=mixT_h[tb]),
                     reads=[("mixT_h", tb)], writes=[(g + "mT", i)], dma=gd + f"m{i}")
                S.op("sp", lambda e, i=i, tb=tb: e.dma_start(out=xr[i], in_=x_src[r0 + tb * 128:r0 + (tb + 1) * 128, :]),
                     writes=[tag + "xt"], dma=gd + f"x{i}")
                for nt in range(4):
                    pi = cnt % 6
                    cnt += 1
                    for kc in range(16):
                        S.op("pe", lambda e, kc=kc, nt=nt, i=i, pi=pi: e.matmul(
                            psb[pi][:], mTb[i][:, kc, :], wo[:, nt, kc, :], start=(kc == 0), stop=(kc == 15)),
                            reads=[(g + "mT", i), (g + "wo", nt)], writes=[pk(pi)])
                    S.op("dve", lambda e, nt=nt, i=i, pi=pi: e.tensor_tensor(
                        xr[i][:, nt * 512:(nt + 1) * 512], psb[pi][:], xr[i][:, nt * 512:(nt + 1) * 512], ALU.add),
                        reads=[pk(pi), tag + "xt"], writes=[tag + "xt"])
                S.op("sp", lambda e, i=i, tb=tb: e.dma_start(out=xmid[tb * 128:(tb + 1) * 128, :], in_=xr[i]),
                     reads=[tag + "xt"], writes=[("xmid", tb)], dma=gd + f"s{i}")
                hv = hs[i].rearrange("p (k t) -> p k t", k=16)
                norm_block(tag, gd + f"{i}", None, g2b, "g2b", xr[i], hb[i],
                           lambda a, b, hv=hv: hv[:, a:b, :], [(g + "hs", i, 0), (g + "hs", i, 1)], 4 * i)
                S.op("sp", lambda e, i=i, tb=tb: e.dma_start(out=h2T_h[tb], in_=hs[i]),
                     reads=[(g + "hs", i, 0), (g + "hs", i, 1)], writes=[("h2T_h", tb)], dma=gd + f"h{i}")
            S.barrier()

        def phase_E(l, y_dst, r0):
            ar.reset()
            g = f"E{ar.gen}."
            for half in range(2):
                ar.off = 0
                gd = "E."
                HT = GT // 2
                hT = ar.alloc(16 * HT, BF16).rearrange("p (k t) -> p k t", k=16)
                acc = ar.alloc(8 * D).rearrange("p (b d) -> p b d", b=8)
                wu = ar.alloc(16 * 512, BF16).rearrange("p (k f) -> p k f", k=16)
                wd = ar.alloc(4 * D, BF16).rearrange("p (f n) -> p f n", f=4)
                uT = ar.alloc(4 * HT, BF16).rearrange("p (f t) -> p f t", f=4)
                rt = [ar.alloc(512) for _ in range(2)]
                for b in range(8):
                    tb = half * 8 + b
                    S.op("sp", lambda e, b=b, tb=tb: e.dma_start(out=hT[:, :, b * 128:(b + 1) * 128],
                                                                 in_=h2T_h[tb].rearrange("p (k t) -> p k t", k=16)),
                         reads=[("h2T_h", tb)], writes=[(g + "hT", b)], dma=gd + f"h{b % 4}")
                for b in range(8):
                    tb = half * 8 + b
                    S.op("sp", lambda e, b=b, tb=tb: e.dma_start(out=acc[:, b, :], in_=xmid[tb * 128:(tb + 1) * 128, :]),
                         reads=[("xmid", tb)], writes=[(g + "acc", b)], dma=gd + f"a{b % 4}")
                cu = 0
                cd = 0
                for fcg in range(DFF // 512):
                    S.op("pool", lambda e, fcg=fcg: e.dma_start(
                        out=wu, in_=w_up[l, :, fcg * 512:(fcg + 1) * 512].rearrange("(kc p) f -> p kc f", p=128)),
                        writes=[g + "wu"], dma=gd + "wu")
                    S.op("pool", lambda e, fcg=fcg: e.dma_start(
                        out=wd, in_=w_down[l, fcg * 512:(fcg + 1) * 512, :].rearrange("(fb p) n -> p fb n", p=128)),
                        writes=[g + "wd"], dma=gd + "wd")
                    for fb in range(4):
                        for tt in range(HT // 512):
                            pi = PB + (cu % 2)
                            for kc in range(16):
                                S.op("pe", lambda e, kc=kc, fb=fb, tt=tt, pi=pi: e.matmul(
                                    psb[pi][:], wu[:, kc, fb * 128:(fb + 1) * 128], hT[:, kc, tt * 512:(tt + 1) * 512],
                                    start=(kc == 0), stop=(kc == 15)),
                                    reads=[g + "wu"] + [(g + "hT", b) for b in range(tt * 4, tt * 4 + 4)], writes=[pk(pi)])
                            r = rt[cu % 2]
                            rk = (g + "rt", cu % 2)
                            S.op("act", lambda e, r=r, pi=pi: e.activation(r, psb[pi][:], AF.Relu), reads=[pk(pi)], writes=[rk])
                            S.op("pool", lambda e, r=r, fb=fb, tt=tt: e.tensor_tensor(uT[:, fb, tt * 512:(tt + 1) * 512], r, r, ALU.mult),
                                 reads=[rk], writes=[(g + "uT", tt)])
                            cu += 1
                    for b in range(8):
                        for nt in range(4):
                            pi = PD + (cd % 4)
                            for fb in range(4):
                                S.op("pe", lambda e, fb=fb, b=b, nt=nt, pi=pi: e.matmul(
                                    psb[pi][:], uT[:, fb, b * 128:(b + 1) * 128], wd[:, fb, nt * 512:(nt + 1) * 512],
                                    start=(fb == 0), stop=(fb == 3)),
                                    reads=[(g + "uT", b // 4), g + "wd"], writes=[pk(pi)])
                            S.op("dve", lambda e, b=b, nt=nt, pi=pi: e.tensor_tensor(
                                acc[:, b, nt * 512:(nt + 1) * 512], psb[pi][:], acc[:, b, nt * 512:(nt + 1) * 512], ALU.add),
                                reads=[pk(pi), (g + "acc", b)], writes=[(g + "acc", b)])
                            cd += 1
                for b in range(8):
                    tb = half * 8 + b
                    S.op("sp", lambda e, b=b, tb=tb: e.dma_start(out=y_dst[r0 + tb * 128:r0 + (tb + 1) * 128, :], in_=acc[:, b, :]),
                         reads=[(g + "acc", b)], writes=[("ydst", r0, tb)], dma=gd + f"o{b % 4}")
            S.barrier()

        for step, l in (plan or []):
            layer_consts(l)
            src = x1_out if (mid_out and step == "pre") else x_in
            phase_A(l, src, 0)
            if step == "pre":
                phase_Bs(l)
            else:
                phase_B(l, 0, False, init_payload=True)
                phase_CD(l, src, 0)
                phase_E(l, x1_out if mid_out else y_out, 0)
        for l in range(n_layers if plan is None else 0):
            layer_consts(l)
            src = x_in if l == 0 else xa
            dst = y_out if l == n_layers - 1 else xa
            for gi in range(n_groups):
                r0 = gi * GT
                phase_A(l, src, r0)
                phase_B(l, gi * NTB, gi == 0)
                phase_CD(l, src, r0)
                phase_E(l, dst, r0)
        S.emit(nc)
    return nc, S


_CACHE = {}
MODE = "fused"
PLANS = ([("pre", 0)], [("main", 0), ("pre", 1)], [("main", 1)])


def kernel_unfused(x, **w):
    x = np.asarray(x, dtype=np.float32)
    B = x.shape[0]
    NP = SEQ // GT
    ncores = B * NP
    if "plans" not in _CACHE:
        _CACHE["plans"] = [build_program(GT, DEPTH, True, plan=list(p))[0] for p in PLANS]
    progs = _CACHE["plans"]
    consts = host_consts()
    shared = {k: np.ascontiguousarray(np.asarray(v, dtype=np.float32)) for k, v in w.items()}
    cb_first = consts["c_bias"]
    cb_rest = cb_first.copy()
    cb_rest[:, 2 * 2048:3 * 2048] = cb_first[:, 0:2048]
    cms = []
    for p in range(NP):
        cmk = np.zeros((128, 8), np.float32)
        for j in range(3):
            cmk[:, j] = 1.0 if j < p else 0.0
            cmk[:, 4 + j] = 1.0 - cmk[:, j]
        cms.append(cmk)
    lite_keys = ("norm1_g", "w_in", "q_norm_g", "k_norm_g", "attn_sinks", "gla_gate_w", "gla_gate_b", "gla_norm_g")

    def base(c, lite=False):
        p = c % NP
        d = {k: shared[k] for k in (lite_keys if lite else shared)}
        d.update(c_mats=consts["c_mats"], c_ident=consts["c_ident"], c_bias=cb_first if p == 0 else cb_rest)
        return d

    def halo(res):
        out = []
        zk = np.zeros((64, 512), ml_dtypes.bfloat16)
        zv = np.zeros((128, 256), ml_dtypes.bfloat16)
        for c in range(ncores):
            b, p = divmod(c, NP)
            gs = np.stack([np.asarray(res[b * NP + j]["pl_S"], np.float32) for j in range(NP)], 0)
            out.append(dict(g_S=gs, h_KT=res[c - 1]["pl_KT"] if p > 0 else zk, h_V=res[c - 1]["pl_V"] if p > 0 else zv,
                            cmask=cms[p]))
        return out

    xs = [np.ascontiguousarray(x[c // NP, (c % NP) * GT:(c % NP + 1) * GT]) for c in range(ncores)]
    r1 = run_bass_kernel_spmd(progs[0], [dict(base(c, True), x=xs[c]) for c in range(ncores)], core_ids=list(range(ncores))).results
    h1 = halo(r1)
    r2 = run_bass_kernel_spmd(progs[1], [dict(base(c), x=xs[c], **h1[c]) for c in range(ncores)], core_ids=list(range(ncores))).results
    h2 = halo(r2)
    r3 = run_bass_kernel_spmd(progs[2], [dict(base(c), x=np.asarray(r2[c]["x1"], np.float32), **h2[c]) for c in range(ncores)],
                              core_ids=list(range(ncores))).results
    out = np.empty((B, SEQ, D), np.float32)
    for c in range(ncores):
        out[c // NP, (c % NP) * GT:(c % NP + 1) * GT] = r3[c]["out"]
    return out


def kernel(x, norm1_g, w_in, q_norm_g, k_norm_g, attn_sinks, gla_gate_w, gla_gate_b,
           gla_norm_g, w_out, norm2_g, w_up, w_down):
    if MODE == "unfused":
        return kernel_unfused(x, norm1_g=norm1_g, w_in=w_in, q_norm_g=q_norm_g, k_norm_g=k_norm_g, attn_sinks=attn_sinks,
                              gla_gate_w=gla_gate_w, gla_gate_b=gla_gate_b, gla_norm_g=gla_norm_g, w_out=w_out,
                              norm2_g=norm2_g, w_up=w_up, w_down=w_down)
    x = np.asarray(x, dtype=np.float32)
    B = x.shape[0]
    if "nc" not in _CACHE:
        _CACHE["nc"] = build_program(SEQ, DEPTH)[0]
    nc = _CACHE["nc"]
    consts = host_consts()
    shared = dict(norm1_g=norm1_g, w_in=w_in, q_norm_g=q_norm_g, k_norm_g=k_norm_g, attn_sinks=attn_sinks,
                  gla_gate_w=gla_gate_w, gla_gate_b=gla_gate_b, gla_norm_g=gla_norm_g, w_out=w_out,
                  norm2_g=norm2_g, w_up=w_up, w_down=w_down)
    shared = {k: np.ascontiguousarray(np.asarray(v, dtype=np.float32)) for k, v in shared.items()}
    shared.update(consts)
    in_maps = [dict(shared, x=np.ascontiguousarray(x[b])) for b in range(B)]
    res = run_bass_kernel_spmd(nc, in_maps, core_ids=list(range(B)))
    return np.stack([res.results[b]["out"] for b in range(B)], 0).astype(np.float32)
```

```python
import numpy as np
import ml_dtypes
from contextlib import ExitStack
import concourse.bass as bass
import concourse.mybir as mybir
from concourse.bass_utils import run_bass_kernel_spmd

F32 = mybir.dt.float32
BF16 = mybir.dt.bfloat16
AF = mybir.ActivationFunctionType
ALU = mybir.AluOpType
AX = mybir.AxisListType

D = 2048
SEQ = 8192
DEPTH = 2
INW = 4624
DFF = 8192
PIPE_B = False
GLA_BANKS_PIPE = None
GT = 2048
NTB = GT // 128
EPS = 1e-6
NEG = -30000.0
ENGS = ("pe", "act", "dve", "pool", "sp")


class Sched:
    def __init__(self, same_engine_sync=True):
        self.ops = []
        self.ginc = {}
        self.same = (set(ENGS) if same_engine_sync is True else set() if not same_engine_sync else set(same_engine_sync)) - {'pe'}

    def op(self, eng, fn, reads=(), writes=(), dma=None, inc=16, atom=None):
        self.ops.append(dict(eng=eng, fn=fn, reads=tuple(reads), writes=tuple(writes), dma=dma, inc=inc, bar=False, atom=atom))
        if dma is not None:
            assert self.ginc.setdefault(dma, inc) == inc

    def barrier(self):
        for e in ENGS:
            self.ops.append(dict(eng=e, fn=None, reads=(), writes=(), dma=None, inc=0, bar=True))

    def resolve(self):
        last_w, readers = {}, {}
        eng_count = {e: 0 for e in ENGS}
        eng_last = {}
        dma_count, dma_last = {}, {}
        signal = set()
        for o in self.ops:
            e = o["eng"]
            if o["bar"]:
                deps = set(eng_last.values()) | set(dma_last.values())
                o["ev"] = None
                o["deps"] = deps
                for d in deps:
                    if d[0] == "eng":
                        signal.add((d[1], d[2]))
                continue
            if o["dma"] is None:
                ev = ("eng", e, eng_count[e])
                eng_count[e] += 1
            else:
                g = o["dma"]
                dma_count[g] = dma_count.get(g, 0) + 1
                ev = ("dma", g, dma_count[g])
            deps = set()
            for k in o["reads"]:
                if k in last_w:
                    deps.add(last_w[k])
            for k in o["writes"]:
                if k in last_w:
                    deps.add(last_w[k])
                deps.update(readers.get(k, ()))
            if o["dma"] is not None and o["dma"] in dma_last:
                deps.add(dma_last[o["dma"]])
            deps.discard(ev)
            red = {}
            for d in deps:
                kk = (d[0], d[1])
                if kk not in red or red[kk][2] < d[2]:
                    red[kk] = d
            deps = set(red.values())
            o["ev"], o["deps"] = ev, deps
            for d in deps:
                if d[0] == "eng":
                    if d[1] == e and e not in self.same:
                        continue
                    signal.add((d[1], d[2]))
            for k in o["reads"]:
                readers.setdefault(k, []).append(ev)
            for k in o["writes"]:
                last_w[k] = ev
                readers[k] = []
            if o["dma"] is not None:
                dma_last[o["dma"]] = ev
            else:
                eng_last[e] = ev
        self.signal = signal
        self.dma_groups = sorted(dma_count)

    def emit(self, nc):
        self.resolve()
        with ExitStack() as st:
            EPOCH = 30000
            cnt0 = {e: 0 for e in ENGS}
            idx0 = {e: 0 for e in ENGS}
            for o in self.ops:
                if o["dma"] is None and not o["bar"]:
                    e = o["eng"]
                    if (e, idx0[e]) in self.signal:
                        cnt0[e] += 1
                    idx0[e] += 1
            esem = {(e, k): st.enter_context(nc.semaphore(f"s_{e}{k}")) for e in ("pe", "act", "dve", "pool")
                    for k in range(cnt0[e] // EPOCH + 1)}
            dsem = {g: st.enter_context(nc.semaphore("d_" + str(g))) for g in self.dma_groups}
            block = st.enter_context(nc.Block())
            sigval, cnt, idxc = {}, {e: 0 for e in ENGS}, {e: 0 for e in ENGS}
            for o in self.ops:
                if o["dma"] is None and not o["bar"]:
                    e = o["eng"]
                    i = idxc[e]
                    idxc[e] += 1
                    if (e, i) in self.signal:
                        sigval[(e, i)] = (cnt[e] // EPOCH, cnt[e] % EPOCH + 1)
                        cnt[e] += 1
            self.sig_counts = cnt
            per = {e: [o for o in self.ops if o["eng"] == e] for e in ENGS}
            same = self.same

            def run_engine(e, engobj):
                seen = {}
                myidx = 0
                for o in per[e]:
                    need = {}
                    for d in o["deps"]:
                        if d[0] == "eng":
                            if d[1] == e and e not in same:
                                continue
                            key, val = ("eng", d[1]), sigval[(d[1], d[2])]
                        else:
                            key, val = ("dma", d[1]), (0, self.ginc[d[1]] * d[2])
                        if seen.get(key, (0, 0)) >= val:
                            continue
                        need[key] = max(need.get(key, (0, 0)), val)
                    for key, val in need.items():
                        engobj.wait_ge(esem[(key[1], val[0])] if key[0] == "eng" else dsem[key[1]], val[1])
                        seen[key] = val
                    if o["bar"]:
                        continue
                    ins = o["fn"](engobj)
                    if o["dma"] is not None:
                        ins.then_inc(dsem[o["dma"]], o["inc"])
                    else:
                        if (e, myidx) in self.signal:
                            ins.then_inc(esem[(e, sigval[(e, myidx)][0])], 1)
                        myidx += 1
                final = {}
                for o in per[e]:
                    if o["dma"] is not None:
                        final[o["dma"]] = max(final.get(o["dma"], 0), o["inc"] * o["ev"][2])
                for g, val in final.items():
                    if seen.get(("dma", g), (0, 0)) < (0, val):
                        engobj.wait_ge(dsem[g], val)

            for e, deco in (("sp", block.sync), ("pe", block.tensor), ("act", block.scalar),
                            ("dve", block.vector), ("pool", block.gpsimd)):
                if per[e]:
                    deco(lambda eng, e=e: run_engine(e, eng))


class Arena:
    def __init__(self, tile, ncols):
        self.tile, self.ncols, self.off, self.gen = tile, ncols, 0, 0

    def reset(self):
        self.off = 0
        self.gen += 1

    def alloc(self, cols, dtype=F32):
        n4 = cols if dtype == F32 else (cols + 1) // 2
        assert self.off + n4 <= self.ncols, (self.off, n4, self.ncols)
        v = self.tile[:, self.off:self.off + n4]
        self.off += n4
        return v if dtype == F32 else v.bitcast(BF16)


def host_consts():
    hs = np.arange(16)
    slopes = (2.0 ** (-8.0 * (hs + 1) / 16)).astype(np.float64)
    tk = np.arange(128)[:, None, None]
    tq = np.arange(128)[None, None, :]
    sl = slopes[None, :, None]
    dist_cur = tq - tk
    bias_cur = np.where(dist_cur >= 0, -sl * dist_cur, NEG)
    dist_prev = tq + 128 - tk
    bias_prev = np.where(dist_prev < 128, -sl * dist_prev, NEG)
    bias_first = np.full_like(bias_prev, NEG)
    s = np.arange(128)[:, None]
    t = np.arange(128)[None, :]
    same = (s // 64) == (t // 64)
    triu = np.where(same & (s <= t), -1.0 / 16, 0.0)
    strictl = np.where(same & (s > t), -1.0 / 16, 0.0)
    mask01 = np.where(same & (s <= t), 1.0, 0.0)
    chunksel = np.zeros((128, 2))
    chunksel[:64, 0] = -1.0 / 16
    chunksel[64:, 1] = -1.0 / 16
    c = dict(
        c_bias=np.stack([bias_prev, bias_cur, bias_first], 1).reshape(128, 3 * 2048).astype(np.float32),
        c_mats=np.concatenate([triu, strictl, mask01, chunksel], 1).astype(np.float32),
        c_ident=np.eye(128).astype(ml_dtypes.bfloat16),
    )
    return c


def build_program(n_tok, n_layers=DEPTH, same_sync=True, plan=None):
    n_groups = n_tok // GT
    nc = bass.Bass("TRN2", target_bir_lowering=False)
    dt = lambda name, shape, dtp=F32, kind="ExternalInput": nc.dram_tensor(name, shape, dtp, kind=kind).ap()
    lite = plan is not None and plan == [("pre", 0)]
    x_in = dt("x", [n_tok, D])
    norm1_g = dt("norm1_g", [DEPTH, D])
    w_in = dt("w_in", [DEPTH, D, INW])
    q_norm_g = dt("q_norm_g", [DEPTH, 64])
    k_norm_g = dt("k_norm_g", [DEPTH, 64])
    attn_sinks = dt("attn_sinks", [DEPTH, 16])
    gate_w = dt("gla_gate_w", [DEPTH, 16, 512])
    gate_b = dt("gla_gate_b", [DEPTH, 512])
    gla_norm_g = dt("gla_norm_g", [DEPTH, 256])
    if not lite:
        w_out = dt("w_out", [DEPTH, D, D])
        norm2_g = dt("norm2_g", [DEPTH, D])
        w_up = dt("w_up", [DEPTH, D, DFF])
        w_down = dt("w_down", [DEPTH, DFF, D])
    c_bias = dt("c_bias", [128, 3 * 2048])
    c_mats = dt("c_mats", [128, 386])
    c_ident = dt("c_ident", [128, 128], BF16)
    has_pre = plan is not None and any(p[0] == "pre" for p in plan)
    has_main = plan is not None and any(p[0] == "main" for p in plan)
    mid_out = plan is not None and len(plan) == 2
    y_out = x1_out = None
    if plan is None or plan == [("main", 1)]:
        y_out = dt("out", [n_tok, D], F32, "ExternalOutput")
    if mid_out:
        x1_out = dt("x1", [n_tok, D], F32, "ExternalOutput")
    if has_pre:
        pl_S = dt("pl_S", [128, 1028], F32, "ExternalOutput")
        pl_KT = dt("pl_KT", [64, 512], BF16, "ExternalOutput")
        pl_V = dt("pl_V", [128, 256], BF16, "ExternalOutput")
    if has_main:
        g_S = dt("g_S", [4, 128, 1028])
        h_KT = dt("h_KT", [64, 512], BF16)
        h_V = dt("h_V", [128, 256], BF16)
        cmask_d = dt("cmask", [128, 8])
    xa = nc.dram_tensor("xa", [n_tok, D], F32).ap()
    xmid = nc.dram_tensor("xmid", [GT, D], F32).ap()
    proj = nc.dram_tensor("proj", [GT, 4608], F32).ap()
    gzT_h = nc.dram_tensor("gzT_h", [16, GT], F32).ap()
    mixT_h = nc.dram_tensor("mixT_h", [NTB, 128, 16 * 128], BF16).ap()
    h2T_h = nc.dram_tensor("h2T_h", [NTB, 128, 16 * 128], BF16).ap()

    S = Sched(same_sync)
    with ExitStack() as st:
        sb = lambda name, shape, dtp=F32: st.enter_context(nc.sbuf_tensor(name, shape, dtp))
        ident = sb("ident", [128, 128], BF16)
        mats = sb("mats", [128, 386])
        triu, strictl, mask01, chunksel = mats[:, 0:128], mats[:, 128:256], mats[:, 256:384], mats[:, 384:386]
        g1b = sb("g1b", [128, D])
        g2b = sb("g2b", [128, D])
        g64 = sb("g64", [128, 128])
        gqk = sb("gqk", [128, 20, 64])
        esink = sb("esink", [128, 16])
        ggla = sb("ggla", [128, 256])
        gatew = sb("gatew", [17, 512])
        wgz = sb("wgz", [128, 16, 16], BF16)
        Sst = sb("Sst", [128, 4, 256])
        Smid = sb("Smid", [128, 4, 256])
        Sbf0 = [sb(f"Sbf0_{i}", [128, 4, 256], BF16) for i in range(2)]
        Sbf1 = sb("Sbf1", [128, 4, 256], BF16)
        KT = [sb(f"KT{i}", [64, 512], BF16) for i in range(2)]
        Vaug = [sb(f"Vaug{i}", [128, 4, 65], BF16) for i in range(2)]
        gzTa = [sb(f"gzTa{i}", [17, 128]) for i in range(2)]
        small = sb("small", [128, 64])
        cm = sb("cm", [128, 8])
        ARC = 38000
        arena_t = sb("arena", [128, ARC])
        ar = Arena(arena_t, ARC)
        psA = st.enter_context(nc.psum_tensor("psA", [128, 2048], BF16))
        psb = [st.enter_context(nc.psum_tensor(f"ps{i}", [128, 512], F32)) for i in range(6)]
        PB, PC, PD, PE_, PF, PG = range(6)
        pk = lambda i: ("ps", i)

        S.op("sp", lambda e: e.dma_start(out=ident[:], in_=c_ident), writes=["ident"], dma="c0")
        S.op("sp", lambda e: e.dma_start(out=mats[:], in_=c_mats), writes=["mats"], dma="c1")
        for i in range(2):
            S.op("pool", lambda e, i=i: e.memset(KT[i][:], 0.0), writes=[("KT", i)])
            S.op("pool", lambda e, i=i: e.memset(Vaug[i][:], 1.0), writes=[("Vaug", i)])
            S.op("pool", lambda e, i=i: e.memset(gzTa[i][:], 1.0), writes=[("gzTa", i)])

        def layer_consts(l):
            S.op("sp", lambda e: e.dma_start(out=g1b[:], in_=norm1_g[l].partition_broadcast(128)), writes=["g1b"], dma="c0")
            if not lite:
                S.op("sp", lambda e: e.dma_start(out=g2b[:], in_=norm2_g[l].partition_broadcast(128)), writes=["g2b"], dma="c1")
            S.op("sp", lambda e: e.dma_start(out=g64[:, 0:64], in_=q_norm_g[l].partition_broadcast(128)), writes=["g64a"], dma="c2")
            S.op("sp", lambda e: e.dma_start(out=g64[:, 64:128], in_=k_norm_g[l].partition_broadcast(128)), writes=["g64b"], dma="c3")
            S.op("dve", lambda e: e.tensor_copy(gqk[:, 0:16, :], g64[:, 0:64].unsqueeze(1).to_broadcast([128, 16, 64])),
                 reads=["g64a"], writes=["gqk_q"])
            S.op("dve", lambda e: e.tensor_copy(gqk[:, 16:20, :], g64[:, 64:128].unsqueeze(1).to_broadcast([128, 4, 64])),
                 reads=["g64b"], writes=["gqk_k"])
            S.op("sp", lambda e: e.dma_start(out=esink[:], in_=attn_sinks[l].partition_broadcast(128)), writes=["esink"], dma="c2")
            S.op("act", lambda e: e.activation(esink[:], esink[:], AF.Exp), reads=["esink"], writes=["esink"])
            S.op("sp", lambda e: e.dma_start(out=ggla[:], in_=gla_norm_g[l].partition_broadcast(128)), writes=["ggla"], dma="c3")
            S.op("sp", lambda e: e.dma_start(out=gatew[0:16, :], in_=gate_w[l]), writes=["gatew0"], dma="c0")
            S.op("sp", lambda e: e.dma_start(out=gatew[16:17, :], in_=gate_b[l:l + 1, :]), writes=["gatew1"], dma="c1")
            S.op("pool", lambda e: e.dma_start(out=wgz[:], in_=w_in[l, :, 4608:4624].rearrange("(kc p) n -> p kc n", p=128)),
                 writes=["wgz"], dma="c4")

        def norm_block(tag, dtag, src_rows, gb, gkey, xt, hb, dstT_fn, dst_keys, ssi):
            if src_rows is not None:
                S.op("sp", lambda e: e.dma_start(out=xt, in_=src_rows), writes=[tag + "xt"], dma=dtag + "x")
            S.op("act", lambda e: e.activation(hb, xt, AF.Square, accum_out=small[:, ssi:ssi + 1]),
                 reads=[tag + "xt"], writes=[tag + "hb", ("small", ssi)])
            S.op("act", lambda e: e.activation(small[:, ssi + 1:ssi + 2], small[:, ssi:ssi + 1], AF.Ln, scale=1.0 / D, bias=EPS),
                 reads=[("small", ssi)], writes=[("small", ssi + 1)])
            S.op("act", lambda e: e.activation(small[:, ssi + 2:ssi + 3], small[:, ssi + 1:ssi + 2], AF.Exp, scale=-0.5),
                 reads=[("small", ssi + 1)], writes=[("small", ssi + 2)])
            S.op("dve", lambda e: e.scalar_tensor_tensor(hb, xt, small[:, ssi + 2:ssi + 3], gb[:], ALU.mult, ALU.mult),
                 reads=[tag + "xt", ("small", ssi + 2), gkey, tag + "hb"], writes=[tag + "hb"])
            for kc in range(16):
                S.op("pe", lambda e, kc=kc: e.transpose(psA[:, kc * 128:(kc + 1) * 128], hb[:, kc * 128:(kc + 1) * 128], ident[:]),
                     reads=[tag + "hb", "ident"], writes=[("psA", kc // 8)])
            pv = psA[:].rearrange("p (k t) -> p k t", k=16)
            S.op("act", lambda e: e.copy(dstT_fn(0, 8), pv[:, 0:8, :]), reads=[("psA", 0)], writes=[dst_keys[0]])
            S.op("dve", lambda e: e.tensor_copy(dstT_fn(8, 16), pv[:, 8:16, :]), reads=[("psA", 1)], writes=[dst_keys[1]])

        def phase_A(l, x_src, r0):
            ar.reset()
            g = f"A{ar.gen}."
            gd = "A."
            hT = ar.alloc(16 * GT, BF16).rearrange("p (k t) -> p k t", k=16)
            wt = [ar.alloc(16 * 512, BF16).rearrange("p (k n) -> p k n", k=16) for _ in range(2)]
            xt = [ar.alloc(D) for _ in range(2)]
            hb = [ar.alloc(D, BF16) for _ in range(2)]
            stg = [ar.alloc(512) for _ in range(4)]
            gzs = ar.alloc(GT)
            for tb in range(NTB):
                i = tb % 2
                norm_block(g + f"{i}", gd + f"{i}", x_src[r0 + tb * 128: r0 + (tb + 1) * 128, :], g1b, "g1b", xt[i], hb[i],
                           lambda a, b, tb=tb: hT[:, a:b, tb * 128:(tb + 1) * 128],
                           [(g + "hT", tb, 0), (g + "hT", tb, 1)], 4 * i)
            hkeys = lambda tbs: [(g + "hT", tb, i) for tb in tbs for i in range(2)]
            for tt in range(GT // 512):
                for kc in range(16):
                    S.op("pe", lambda e, kc=kc, tt=tt: e.matmul(psb[PB][0:16, :], wgz[:, kc, :], hT[:, kc, tt * 512:(tt + 1) * 512],
                                                                start=(kc == 0), stop=(kc == 15)),
                         reads=hkeys(range(tt * 4, tt * 4 + 4)) + ["wgz"], writes=[pk(PB)])
                S.op("act", lambda e, tt=tt: e.copy(gzs[0:16, tt * 512:(tt + 1) * 512], psb[PB][0:16, :]),
                     reads=[pk(PB)], writes=[g + "gzs"])
            S.op("sp", lambda e: e.dma_start(out=gzT_h, in_=gzs[0:16, :]), reads=[g + "gzs"], writes=["gzT_h"], dma=gd + "gz")
            cnt = 0
            for nt in range(9):
                w = wt[nt % 2]
                wk = (g + "wt", nt % 2)
                S.op("pool", lambda e, nt=nt, w=w: e.dma_start(
                    out=w, in_=w_in[l, :, nt * 512:(nt + 1) * 512].rearrange("(kc p) n -> p kc n", p=128)),
                    writes=[wk], dma=gd + f"w{nt % 2}")
                for tb in range(NTB):
                    pi = PB + (cnt % 2)
                    for kc in range(16):
                        S.op("pe", lambda e, kc=kc, tb=tb, w=w, pi=pi: e.matmul(
                            psb[pi][:], hT[:, kc, tb * 128:(tb + 1) * 128], w[:, kc, :], start=(kc == 0), stop=(kc == 15)),
                            reads=hkeys([tb]) + [wk], writes=[pk(pi)])
                    sg = stg[cnt % 4]
                    sk = (g + "stg", cnt % 4)
                    if cnt % 2 == 0:
                        S.op("act", lambda e, sg=sg, pi=pi: e.copy(sg, psb[pi][:]), reads=[pk(pi)], writes=[sk])
                    else:
                        S.op("dve", lambda e, sg=sg, pi=pi: e.tensor_copy(sg, psb[pi][:]), reads=[pk(pi)], writes=[sk])
                    S.op("sp", lambda e, sg=sg, tb=tb, nt=nt: e.dma_start(
                        out=proj[tb * 128:(tb + 1) * 128, nt * 512:(nt + 1) * 512], in_=sg),
                        reads=[sk], writes=[("proj", tb)], dma=gd + f"st{cnt % 4}")
                    cnt += 1
            S.barrier()

        def phase_B(l, gblk0, first_group, init_payload=False):
            ar.reset()
            SST_ALL = [("Sst", h) for h in range(4)]
            GLA_BANKS = (PG, PE_, PE_)
            g = f"B{ar.gen}."
            gd = "B."
            bias = ar.alloc(3 * 2048)
            Pt = [ar.alloc(4608) for _ in range(2)]
            sqt = ar.alloc(1280)
            qkn = ar.alloc(1280, BF16)
            QTs = ar.alloc(2048, BF16)
            spt = [ar.alloc(512) for _ in range(2)]
            PT = [[ar.alloc(512, BF16) for _ in range(2)] for _ in range(2)]
            mix = [ar.alloc(D, BF16) for _ in range(2)]
            et = ar.alloc(512)
            lat = ar.alloc(512)
            Eb = ar.alloc(512)
            Enb = ar.alloc(512)
            Ebl = ar.alloc(512)
            qin = ar.alloc(512, BF16)
            kin = ar.alloc(512, BF16)
            kst = ar.alloc(512, BF16)
            gvb = ar.alloc(1024, BF16)
            sgt = ar.alloc(1024)
            qT0 = ar.alloc(512, BF16)
            qT1 = ar.alloc(512, BF16)
            kinT = ar.alloc(512, BF16)
            attm = ar.alloc(512, BF16)
            tmpo = [ar.alloc(256) for _ in range(2)]
            junk = ar.alloc(256)
            mTs = [ar.alloc(2048, BF16) for _ in range(2)]
            S.op("sp", lambda e: e.dma_start(out=bias, in_=c_bias), writes=[g + "bias"], dma=gd + "bias")
            S.op("pool", lambda e: e.memset(qT0, 0.0), writes=[g + "qT0"])
            S.op("pool", lambda e: e.memset(qT1, 0.0), writes=[g + "qT1"])
            if init_payload:
                Sj = ar.alloc(1028)
                tmpc = ar.alloc(256)
                fac = ar.alloc(4)
                S.op("sp", lambda e: e.dma_start(out=cm[:], in_=cmask_d), writes=["cm"], dma=gd + "cm")
                S.op("pool", lambda e: e.memset(Sst[:], 0.0), writes=SST_ALL)
                for j in range(3):
                    S.op("sp", lambda e, j=j: e.dma_start(out=Sj, in_=g_S[j]), writes=[g + "Sj"], dma=gd + "Sj")
                    S.op("act", lambda e: e.activation(fac, Sj[:, 1024:1028], AF.Exp), reads=[g + "Sj"], writes=[g + "fac"])
                    S.op("dve", lambda e, j=j: e.tensor_scalar(fac, fac, cm[:, j:j + 1], cm[:, 4 + j:5 + j], ALU.mult, ALU.add),
                         reads=[g + "fac", "cm"], writes=[g + "fac"])
                    for h in range(4):
                        S.op("dve", lambda e, j=j, h=h: e.tensor_scalar_mul(tmpc, Sj[:, h * 256:(h + 1) * 256], cm[:, j:j + 1]),
                             reads=[g + "Sj", "cm"], writes=[g + "tmpc"])
                        S.op("dve", lambda e, h=h: e.scalar_tensor_tensor(Sst[:, h, :], Sst[:, h, :], fac[:, h:h + 1], tmpc, ALU.mult, ALU.add),
                             reads=[("Sst", h), g + "fac", g + "tmpc"], writes=[("Sst", h)])
                S.op("act", lambda e: e.copy(Sbf0[0][:], Sst[:]), reads=SST_ALL, writes=[("Sbf0", 0, h) for h in range(4)])
                S.op("sp", lambda e: e.dma_start(out=KT[1][:], in_=h_KT), writes=[("KT", 1)], dma=gd + "hk")
                S.op("sp", lambda e: e.dma_start(out=Vaug[1][:, :, 0:64], in_=h_V.rearrange("p (j d) -> p j d", j=4)),
                     writes=[("Vaug", 1)], dma=gd + "hv")
            if first_group:
                for i in range(2):
                    S.op("pool", lambda e, i=i: e.memset(KT[i][:], 0.0), writes=[("KT", i)])
                    S.op("pool", lambda e, i=i: e.memset(Vaug[i][:, :, 0:64], 0.0), writes=[("Vaug", i)])
                S.op("pool", lambda e: e.memset(Sst[:], 0.0), writes=SST_ALL)
                S.op("pool", lambda e: e.memset(Sbf0[0][:], 0.0), writes=[("Sbf0", 0, h) for h in range(4)])
            def swa(tb, part):
                pb = tb % 2
                P = Pt[pb]
                Pk = (g + "P", pb)
                if part == "prep":
                    swa_prep(tb, pb, P, Pk)
                else:
                    swa_main(tb, pb, P, Pk)

            def swa_prep(tb, pb, P, Pk):
                S.op("sp", lambda e, P=P, tb=tb: e.dma_start(out=P, in_=proj[tb * 128:(tb + 1) * 128, :]),
                     reads=[("proj", tb)], writes=[Pk], dma=gd + f"P{pb}")
                S.op("pool", lambda e, P=P: e.tensor_tensor(sqt, P[:, 0:1280], P[:, 0:1280], ALU.mult), reads=[Pk], writes=[g + "sqt"])
                S.op("dve", lambda e: e.tensor_reduce(small[:, 8:28], sqt.rearrange("p (h d) -> p h d", h=20), AX.X, ALU.add),
                     reads=[g + "sqt"], writes=["ssq"])
                S.op("act", lambda e: e.activation(small[:, 8:28], small[:, 8:28], AF.Ln, scale=1.0 / 64, bias=EPS),
                     reads=["ssq"], writes=["ssq"])
                S.op("act", lambda e: e.activation(small[:, 8:28], small[:, 8:28], AF.Exp, scale=-0.5), reads=["ssq"], writes=["ssq"])
                S.op("dve", lambda e, P=P: e.tensor_tensor(sqt.rearrange("p (h d) -> p h d", h=20),
                                                          P[:, 0:1280].rearrange("p (h d) -> p h d", h=20),
                                                          small[:, 8:28].unsqueeze(2).to_broadcast([128, 20, 64]), ALU.mult),
                     reads=[Pk, "ssq", g + "sqt"], writes=[g + "sqt"])
                S.op("pool", lambda e: e.tensor_tensor(qkn.rearrange("p (h d) -> p h d", h=20),
                                                      sqt.rearrange("p (h d) -> p h d", h=20), gqk[:], ALU.mult),
                     reads=[g + "sqt", "gqk_q", "gqk_k"], writes=[g + "qkn"])
                S.op("act", lambda e, P=P, pb=pb: e.copy(Vaug[pb][:, :, 0:64], P[:, 1280:1536].rearrange("p (j d) -> p j d", j=4)),
                     reads=[Pk], writes=[("Vaug", pb)])

            def swa_main(tb, pb, P, Pk):
                pbb = psb[PB][:].bitcast(BF16)
                for j in range(4):
                    S.op("pe", lambda e, j=j: e.transpose(pbb[0:64, j * 128:(j + 1) * 128], qkn[:, 1024 + j * 64:1024 + (j + 1) * 64], ident[:]),
                         reads=[g + "qkn", "ident"], writes=[pk(PB)])
                S.op("act", lambda e, pb=pb: e.copy(KT[pb][:], pbb[0:64, 0:512]), reads=[pk(PB)], writes=[("KT", pb)])
                for h in range(16):
                    S.op("pe", lambda e, h=h: e.transpose(psA[0:64, h * 128:(h + 1) * 128], qkn[:, h * 64:(h + 1) * 64], ident[:]),
                         reads=[g + "qkn", "ident"], writes=[("psA", h // 8)])
                S.op("dve", lambda e: e.tensor_copy(QTs[0:64, 0:1024], psA[0:64, 0:1024]), reads=[("psA", 0)], writes=[g + "QT0"])
                S.op("act", lambda e: e.copy(QTs[0:64, 1024:2048], psA[0:64, 1024:2048]), reads=[("psA", 1)], writes=[g + "QT1"])
                first_blk = (first_group or init_payload) and tb == 0
                def scores(j):
                    for c in range(2):
                        cnt = 2 * j + c
                        src = 1 - pb if c == 0 else pb
                        pi = PB + (cnt % 2)
                        bsel = (2 if first_blk else 0) if c == 0 else 1
                        S.op("pe", lambda e, j=j, src=src, pi=pi: e.matmul(psb[pi][:], KT[src][0:64, j * 128:(j + 1) * 128],
                                                                         QTs[0:64, j * 512:(j + 1) * 512], start=True, stop=True),
                             reads=[("KT", src), g + "QT0", g + "QT1"], writes=[pk(pi)])
                        S.op("dve", lambda e, j=j, pi=pi, bsel=bsel, cnt=cnt: e.scalar_tensor_tensor(
                            spt[cnt % 2], psb[pi][:], 0.125, bias[:, bsel * 2048 + j * 512: bsel * 2048 + (j + 1) * 512], ALU.mult, ALU.add),
                            reads=[pk(pi), g + "bias"], writes=[(g + "spt", cnt % 2)])
                        S.op("act", lambda e, j=j, c=c, cnt=cnt: e.activation(PT[j % 2][c], spt[cnt % 2], AF.Exp),
                             reads=[(g + "spt", cnt % 2)], writes=[(g + "PT", j % 2, c)])

                def pv(j):
                    pdv = psb[PD][:, 0:260].rearrange("p (h d) -> p h d", h=4)
                    for hl in range(4):
                        for c in range(2):
                            src = 1 - pb if c == 0 else pb
                            S.op("pe", lambda e, j=j, hl=hl, c=c, src=src: e.matmul(
                                psb[PD][:, hl * 65:(hl + 1) * 65], PT[j % 2][c][:, hl * 128:(hl + 1) * 128], Vaug[src][:, j, :],
                                start=(c == 0), stop=(c == 1)),
                                reads=[(g + "PT", j % 2, c), ("Vaug", src)], writes=[pk(PD)], atom=("pv", tb, j, hl))
                    S.op("dve", lambda e, j=j: e.tensor_tensor(small[:, 32:36], pdv[:, :, 64], esink[:, 4 * j:4 * j + 4], ALU.add),
                         reads=[pk(PD), "esink"], writes=["den"])
                    S.op("dve", lambda e: e.reciprocal(small[:, 32:36], small[:, 32:36]), reads=["den"], writes=["den"])
                    S.op("dve", lambda e, j=j, pb=pb: e.tensor_tensor(
                        mix[pb][:, j * 256:(j + 1) * 256].rearrange("p (h d) -> p h d", h=4), pdv[:, :, 0:64],
                        small[:, 32:36].unsqueeze(2).to_broadcast([128, 4, 64]), ALU.mult),
                        reads=[pk(PD), "den"], writes=[(g + "mixa", pb)])

                scores(0)
                for j in range(4):
                    if j + 1 < 4:
                        scores(j + 1)
                    pv(j)

            def gla(tb, part):
                pb = tb % 2
                P = Pt[pb]
                Pk = (g + "P", pb)
                WE = [pk(PE_)]
                TBK, ABK, UBK = GLA_BANKS
                WF = [pk(PF)]
                WG = [pk(PG)]
                S.op("sp", lambda e, tb=tb, pb=pb: e.dma_start(out=gzTa[pb][0:16, :], in_=gzT_h[:, tb * 128:(tb + 1) * 128]),
                     reads=["gzT_h"], writes=[("gzTa", pb)], dma=gd + f"gz{pb}")
                S.op("pe", lambda e, pb=pb: e.matmul(psb[PE_][:], gzTa[pb][:], gatew[:], start=True, stop=True),
                     reads=[("gzTa", pb), "gatew0", "gatew1"], writes=WE)
                S.op("act", lambda e: e.activation(et, psb[PE_][:], AF.Exp, scale=-1.0), reads=[pk(PE_)], writes=[g + "et"])
                S.op("act", lambda e: e.activation(lat, et, AF.Ln, bias=1.0), reads=[g + "et"], writes=[g + "lat"])
                S.op("pe", lambda e: e.matmul(psb[PF][:], triu, lat, start=True, stop=True), reads=["mats", g + "lat"], writes=WF)
                for h in range(4):
                    S.op("pe", lambda e, h=h: e.matmul(psb[PE_][:, 2 * h:2 * h + 2], lat[:, h * 128:(h + 1) * 128], chunksel, start=True, stop=True),
                         reads=["mats", g + "lat"], writes=WE)
                S.op("act", lambda e: e.activation(small[:, 40:48], psb[PE_][:, 0:8], AF.Exp), reads=WE, writes=["dec"])
                S.op("pe", lambda e: e.matmul(psb[PE_][:], strictl, lat, start=True, stop=True), reads=["mats", g + "lat"], writes=WE)
                S.op("act", lambda e: e.activation(Eb, psb[PF][:], AF.Exp), reads=WF, writes=[g + "Eb"])
                S.op("act", lambda e: e.activation(Enb, psb[PF][:], AF.Exp, scale=-1.0), reads=WF, writes=[g + "Enb"])
                S.op("act", lambda e: e.activation(Ebl, psb[PE_][:], AF.Exp), reads=[pk(PE_)], writes=[g + "Ebl"])
                S.op("dve", lambda e, P=P: e.scalar_tensor_tensor(qin, P[:, 1536:2048], 128.0 ** -0.5, Eb, ALU.mult, ALU.mult),
                     reads=[Pk, g + "Eb"], writes=[g + "qin"])
                S.op("pool", lambda e, P=P: e.tensor_tensor(kin, P[:, 2048:2560], Enb, ALU.mult), reads=[Pk, g + "Enb"], writes=[g + "kin"])
                S.op("pool", lambda e, P=P: e.tensor_tensor(kst, P[:, 2048:2560], Ebl, ALU.mult), reads=[Pk, g + "Ebl"], writes=[g + "kst"])
                S.op("act", lambda e, P=P: e.copy(gvb, P[:, 2560:3584]), reads=[Pk], writes=[g + "gvb"])
                S.op("act", lambda e, P=P: e.activation(sgt, P[:, 3584:4608], AF.Exp, scale=-1.0), reads=[Pk], writes=[g + "sgt"])
                S.op("dve", lambda e: e.tensor_scalar_add(sgt, sgt, 1.0), reads=[g + "sgt"], writes=[g + "sgt"])
                S.op("dve", lambda e: e.reciprocal(sgt, sgt), reads=[g + "sgt"], writes=[g + "sgt"])
                S.op("pool", lambda e, P=P: e.tensor_tensor(sgt, sgt, P[:, 3584:4608], ALU.mult), reads=[g + "sgt", Pk], writes=[g + "sgt"])
                if part == "head":
                    return
                pfb = psb[TBK][:].bitcast(BF16)
                for h in range(4):
                    S.op("pe", lambda e, h=h: e.transpose(pfb[:, h * 128:(h + 1) * 128], qin[:, h * 128:(h + 1) * 128], ident[:]),
                         reads=[g + "qin", "ident"], writes=[pk(TBK)])
                for h in range(4):
                    S.op("pe", lambda e, h=h: e.transpose(pfb[:, 512 + h * 128:512 + (h + 1) * 128], kin[:, h * 128:(h + 1) * 128], ident[:]),
                         reads=[g + "kin", "ident"], writes=[pk(TBK)])
                v3 = lambda ap: ap.rearrange("p (h t) -> p h t", h=4)
                S.op("act", lambda e: e.copy(v3(qT0)[:, :, 0:64], v3(pfb[:, 0:512])[:, :, 0:64]), reads=[pk(TBK)], writes=[g + "qT0"])
                S.op("dve", lambda e: e.tensor_copy(v3(qT1)[:, :, 64:128], v3(pfb[:, 0:512])[:, :, 64:128]), reads=[pk(TBK)], writes=[g + "qT1"])
                S.op("act", lambda e: e.copy(kinT, pfb[:, 512:1024]), reads=[pk(TBK)], writes=[g + "kinT"])
                for h in range(4):
                    S.op("pe", lambda e, h=h: e.matmul(psb[ABK][:, h * 128:h * 128 + 64], kinT[:, h * 128:(h + 1) * 128],
                                                       qT0[:, h * 128:h * 128 + 64], start=True, stop=True),
                         reads=[g + "kinT", g + "qT0"], writes=[pk(ABK)])
                    S.op("pe", lambda e, h=h: e.matmul(psb[ABK][:, h * 128 + 64:(h + 1) * 128], kinT[:, h * 128:(h + 1) * 128],
                                                       qT1[:, h * 128 + 64:(h + 1) * 128], start=True, stop=True),
                         reads=[g + "kinT", g + "qT1"], writes=[pk(ABK)])
                S.op("dve", lambda e: e.tensor_tensor(v3(attm), v3(psb[ABK][:]), mask01.unsqueeze(1).to_broadcast([128, 4, 128]), ALU.mult),
                     reads=[pk(ABK), "mats"], writes=[g + "attm"])
                s0 = Sbf0[pb]
                s0n = Sbf0[1 - pb]
                for h in range(4):
                    S.op("pe", lambda e, h=h: e.matmul(psb[UBK][:, 0:256], kst[0:64, h * 128:(h + 1) * 128], gvb[0:64, h * 256:(h + 1) * 256],
                                                       start=True, stop=True),
                         reads=[g + "kst", g + "gvb"], writes=[pk(UBK)])
                    S.op("dve", lambda e, h=h: e.scalar_tensor_tensor(Smid[:, h, :], Sst[:, h, :], small[:, 40 + 2 * h:41 + 2 * h],
                                                                      psb[UBK][:, 0:256], ALU.mult, ALU.add),
                         reads=[("Sst", h), "dec", pk(UBK)], writes=[("Smid", h)])
                    S.op("act", lambda e, h=h: e.copy(Sbf1[:, h, :], Smid[:, h, :]), reads=[("Smid", h)], writes=[("Sbf1", h)])
                    S.op("pe", lambda e, h=h: e.matmul(psb[PF][:, 0:256], kst[64:128, h * 128:(h + 1) * 128], gvb[64:128, h * 256:(h + 1) * 256],
                                                       start=True, stop=True),
                         reads=[g + "kst", g + "gvb"], writes=[pk(PF)])
                    og = psb[PG][:, 0:256]
                    ogk = pk(PG)
                    S.op("pe", lambda e, h=h, og=og: e.matmul(og, attm[:, h * 128:(h + 1) * 128], gvb[:, h * 256:(h + 1) * 256], start=True, stop=False),
                         reads=[g + "attm", g + "gvb"], writes=[ogk], atom=("og", tb, h))
                    S.op("pe", lambda e, h=h, og=og, s0=s0: e.matmul(og, qT0[:, h * 128:(h + 1) * 128], s0[:, h, :], start=False, stop=False),
                         reads=[g + "qT0", ("Sbf0", pb, h)], writes=[ogk], atom=("og", tb, h))
                    S.op("pe", lambda e, h=h, og=og: e.matmul(og, qT1[:, h * 128:(h + 1) * 128], Sbf1[:, h, :], start=False, stop=True),
                         reads=[g + "qT1", ("Sbf1", h)], writes=[ogk], atom=("og", tb, h))
                    S.op("dve", lambda e, h=h: e.scalar_tensor_tensor(Sst[:, h, :], Smid[:, h, :], small[:, 41 + 2 * h:42 + 2 * h],
                                                                      psb[PF][:, 0:256], ALU.mult, ALU.add),
                         reads=[("Smid", h), "dec", pk(PF), ("Sst", h)], writes=[("Sst", h)])
                    S.op("act", lambda e, h=h, s0n=s0n: e.copy(s0n[:, h, :], Sst[:, h, :]), reads=[("Sst", h)], writes=[("Sbf0", 1 - pb, h)])
                    S.op("act", lambda e, h=h, og=og: e.activation(junk, og, AF.Square, accum_out=small[:, 48 + h:49 + h]),
                         reads=[ogk], writes=[g + "junk", ("sso", h)])
                    S.op("act", lambda e, h=h: e.activation(small[:, 52 + h:53 + h], small[:, 48 + h:49 + h], AF.Ln, scale=1.0 / 256, bias=EPS),
                         reads=[("sso", h)], writes=[("sso2", h)])
                    S.op("act", lambda e, h=h: e.activation(small[:, 56 + h:57 + h], small[:, 52 + h:53 + h], AF.Exp, scale=-0.5),
                         reads=[("sso2", h)], writes=[("sso3", h)])
                    S.op("dve", lambda e, h=h, og=og: e.scalar_tensor_tensor(tmpo[h % 2], og, small[:, 56 + h:57 + h], ggla[:], ALU.mult, ALU.mult),
                         reads=[ogk, ("sso3", h), "ggla"], writes=[(g + "tmpo", h % 2)])
                    S.op("pool", lambda e, h=h, pb=pb: e.tensor_tensor(mix[pb][:, 1024 + h * 256:1024 + (h + 1) * 256], tmpo[h % 2],
                                                                      sgt[:, h * 256:(h + 1) * 256], ALU.mult),
                         reads=[(g + "tmpo", h % 2), g + "sgt"], writes=[(g + "mixg", pb)])

            def mixt(tb):
                pb = tb % 2
                for kc in range(16):
                    S.op("pe", lambda e, kc=kc, pb=pb: e.transpose(psA[:, kc * 128:(kc + 1) * 128], mix[pb][:, kc * 128:(kc + 1) * 128], ident[:]),
                         reads=[(g + "mixa", pb), (g + "mixg", pb), "ident"], writes=[("psA", kc // 8)])
                S.op("act", lambda e, pb=pb: e.copy(mTs[pb][:, 0:1024], psA[:, 0:1024]), reads=[("psA", 0)], writes=[(g + "mTs", pb)])
                S.op("dve", lambda e, pb=pb: e.tensor_copy(mTs[pb][:, 1024:2048], psA[:, 1024:2048]), reads=[("psA", 1)], writes=[(g + "mTs", pb)])
                S.op("sp", lambda e, pb=pb, tb=tb: e.dma_start(out=mixT_h[tb], in_=mTs[pb]), reads=[(g + "mTs", pb), (g + "mixa", pb), (g + "mixg", pb)],
                     writes=[("mixT_h", tb)], dma=gd + f"mT{pb}")

            def cap(fn, *a):
                n0 = len(S.ops)
                fn(*a)
                lst = S.ops[n0:]
                del S.ops[n0:]
                return lst

            def units(lst):
                u = []
                for o in lst:
                    if u and o.get("atom") is not None and u[-1][-1].get("atom") == o["atom"]:
                        u[-1].append(o)
                    else:
                        u.append([o])
                return u

            def merge(a, b):
                a, b = units(a), units(b)
                out, i, j = [], 0, 0
                while i < len(a) or j < len(b):
                    if j >= len(b) or (i < len(a) and i * len(b) <= j * len(a)):
                        out.extend(a[i]); i += 1
                    else:
                        out.extend(b[j]); j += 1
                return out

            def gla_tail(tb, prep_ops):
                n_head = len(cap(gla, tb, "head"))
                tail = cap(gla, tb, "tail")[n_head:]
                ia = max(i for i, o in enumerate(tail) if (g + "attm") in o["writes"]) + 1
                S.ops.extend(tail[:ia])
                rest = tail[ia:]
                assert len(rest) % 4 == 0
                nh = len(rest) // 4
                npc = (len(prep_ops) + 3) // 4
                for h in range(4):
                    S.ops.extend(rest[h * nh:(h + 1) * nh])
                    S.ops.extend(prep_ops[h * npc:(h + 1) * npc])

            swa(0, "prep")
            swa(0, "main")
            swa(1, "prep")
            for tb in range(NTB):
                if tb >= 1:
                    mixt(tb - 1)
                if tb + 1 < NTB:
                    swa(tb + 1, "main")
                gla(tb, "head")
                gla_tail(tb, cap(swa, tb + 2, "prep") if tb + 2 < NTB else [])
            mixt(NTB - 1)
            S.barrier()

        def phase_Bs(l):
            ar.reset()
            g = f"S{ar.gen}."
            gd = "S."
            Pg = [ar.alloc(1536) for _ in range(2)]
            Pkv = ar.alloc(512)
            et = ar.alloc(512)
            lat = ar.alloc(512)
            Ebl = ar.alloc(512)
            kst = ar.alloc(512, BF16)
            gvb = ar.alloc(1024, BF16)
            Dl = ar.alloc(8)
            sq = ar.alloc(256)
            kn = ar.alloc(256, BF16)
            KTo = ar.alloc(512, BF16)
            Vo = ar.alloc(256, BF16)
            S.op("pool", lambda e: e.memset(Sst[:], 0.0), writes=["Sst"])
            S.op("pool", lambda e: e.memset(Dl, 0.0), writes=[g + "Dl"])
            for tb in range(NTB):
                pb = tb % 2
                P = Pg[pb]
                Pk = (g + "P", pb)
                S.op("sp", lambda e, P=P, tb=tb: e.dma_start(out=P, in_=proj[tb * 128:(tb + 1) * 128, 2048:3584]),
                     reads=[("proj", tb)], writes=[Pk], dma=gd + f"P{pb}")
                S.op("sp", lambda e, tb=tb, pb=pb: e.dma_start(out=gzTa[pb][0:16, :], in_=gzT_h[:, tb * 128:(tb + 1) * 128]),
                     reads=["gzT_h"], writes=[("gzTa", pb)], dma=gd + f"gz{pb}")
                S.op("pe", lambda e, pb=pb: e.matmul(psb[PE_][:], gzTa[pb][:], gatew[:], start=True, stop=True),
                     reads=[("gzTa", pb), "gatew0", "gatew1"], writes=[pk(PE_)])
                S.op("act", lambda e: e.activation(et, psb[PE_][:], AF.Exp, scale=-1.0), reads=[pk(PE_)], writes=[g + "et"])
                S.op("act", lambda e: e.activation(lat, et, AF.Ln, bias=1.0), reads=[g + "et"], writes=[g + "lat"])
                S.op("pe", lambda e: e.matmul(psb[PE_][:], strictl, lat, start=True, stop=True), reads=["mats", g + "lat"], writes=[pk(PE_)])
                for h in range(4):
                    S.op("pe", lambda e, h=h: e.matmul(psb[PG][:, 2 * h:2 * h + 2], lat[:, h * 128:(h + 1) * 128], chunksel, start=True, stop=True),
                         reads=["mats", g + "lat"], writes=[pk(PG)])
                S.op("act", lambda e: e.activation(Ebl, psb[PE_][:], AF.Exp), reads=[pk(PE_)], writes=[g + "Ebl"])
                S.op("act", lambda e: e.activation(small[:, 40:48], psb[PG][:, 0:8], AF.Exp), reads=[pk(PG)], writes=["dec"])
                S.op("dve", lambda e: e.tensor_tensor(Dl, psb[PG][:, 0:8], Dl, ALU.add), reads=[pk(PG), g + "Dl"], writes=[g + "Dl"])
                S.op("pool", lambda e, P=P: e.tensor_tensor(kst, P[:, 0:512], Ebl, ALU.mult), reads=[Pk, g + "Ebl"], writes=[g + "kst"])
                S.op("act", lambda e, P=P: e.copy(gvb, P[:, 512:1536]), reads=[Pk], writes=[g + "gvb"])
                for h in range(4):
                    S.op("pe", lambda e, h=h: e.matmul(psb[PD][:, 0:256], kst[0:64, h * 128:(h + 1) * 128], gvb[0:64, h * 256:(h + 1) * 256],
                                                       start=True, stop=True),
                         reads=[g + "kst", g + "gvb"], writes=[pk(PD)])
                    S.op("dve", lambda e, h=h: e.scalar_tensor_tensor(Smid[:, h, :], Sst[:, h, :], small[:, 40 + 2 * h:41 + 2 * h],
                                                                      psb[PD][:, 0:256], ALU.mult, ALU.add),
                         reads=["Sst", "dec", pk(PD)], writes=[("Smid", h)])
                    S.op("pe", lambda e, h=h: e.matmul(psb[PF][:, 0:256], kst[64:128, h * 128:(h + 1) * 128], gvb[64:128, h * 256:(h + 1) * 256],
                                                       start=True, stop=True),
                         reads=[g + "kst", g + "gvb"], writes=[pk(PF)])
                    S.op("dve", lambda e, h=h: e.scalar_tensor_tensor(Sst[:, h, :], Smid[:, h, :], small[:, 41 + 2 * h:42 + 2 * h],
                                                                      psb[PF][:, 0:256], ALU.mult, ALU.add),
                         reads=[("Smid", h), "dec", pk(PF), "Sst"], writes=["Sst"])
            tb = NTB - 1
            S.op("sp", lambda e: e.dma_start(out=Pkv, in_=proj[tb * 128:(tb + 1) * 128, 1024:1536]),
                 reads=[("proj", tb)], writes=[g + "Pkv"], dma=gd + "kv")
            S.op("pool", lambda e: e.tensor_tensor(sq, Pkv[:, 0:256], Pkv[:, 0:256], ALU.mult), reads=[g + "Pkv"], writes=[g + "sq"])
            S.op("dve", lambda e: e.tensor_reduce(small[:, 8:12], sq.rearrange("p (h d) -> p h d", h=4), AX.X, ALU.add),
                 reads=[g + "sq"], writes=["ssq"])
            S.op("act", lambda e: e.activation(small[:, 8:12], small[:, 8:12], AF.Ln, scale=1.0 / 64, bias=EPS), reads=["ssq"], writes=["ssq"])
            S.op("act", lambda e: e.activation(small[:, 8:12], small[:, 8:12], AF.Exp, scale=-0.5), reads=["ssq"], writes=["ssq"])
            S.op("dve", lambda e: e.tensor_tensor(sq.rearrange("p (h d) -> p h d", h=4), Pkv[:, 0:256].rearrange("p (h d) -> p h d", h=4),
                                                  small[:, 8:12].unsqueeze(2).to_broadcast([128, 4, 64]), ALU.mult),
                 reads=[g + "Pkv", "ssq", g + "sq"], writes=[g + "sq"])
            S.op("pool", lambda e: e.tensor_tensor(kn.rearrange("p (h d) -> p h d", h=4), sq.rearrange("p (h d) -> p h d", h=4),
                                                   gqk[:, 16:20, :], ALU.mult), reads=[g + "sq", "gqk_k"], writes=[g + "kn"])
            pgb = psb[PG][:].bitcast(BF16)
            for j in range(4):
                S.op("pe", lambda e, j=j: e.transpose(pgb[0:64, j * 128:(j + 1) * 128], kn[:, j * 64:(j + 1) * 64], ident[:]),
                     reads=[g + "kn", "ident"], writes=[pk(PG)])
            S.op("act", lambda e: e.copy(KTo[0:64, :], pgb[0:64, 0:512]), reads=[pk(PG)], writes=[g + "KTo"])
            S.op("act", lambda e: e.copy(Vo, Pkv[:, 256:512]), reads=[g + "Pkv"], writes=[g + "Vo"])
            S.op("dve", lambda e: e.tensor_tensor(small[:, 12:16], Dl.rearrange("p (h c) -> p h c", c=2)[:, :, 0],
                                                  Dl.rearrange("p (h c) -> p h c", c=2)[:, :, 1], ALU.add),
                 reads=[g + "Dl"], writes=["dl4"])
            S.op("sp", lambda e: e.dma_start(out=pl_S[:, 0:1024], in_=Sst[:].rearrange("p h d -> p (h d)")), reads=["Sst"], dma=gd + "o0")
            S.op("sp", lambda e: e.dma_start(out=pl_S[:, 1024:1028], in_=small[:, 12:16]), reads=["dl4"], dma=gd + "o1")
            S.op("sp", lambda e: e.dma_start(out=pl_KT, in_=KTo[0:64, :]), reads=[g + "KTo"], dma=gd + "o2")
            S.op("sp", lambda e: e.dma_start(out=pl_V, in_=Vo), reads=[g + "Vo"], dma=gd + "o3")
            S.barrier()

        def phase_C(l, x_src, r0):
            ar.reset()
            g = f"C{ar.gen}."
            gd = "C."
            mT = ar.alloc(NTB * 2048, BF16).rearrange("p (b k t) -> p b k t", b=NTB, k=16)
            wt = [ar.alloc(16 * 512, BF16).rearrange("p (k n) -> p k n", k=16) for _ in range(2)]
            xs = [ar.alloc(512) for _ in range(4)]
            for tb in range(NTB):
                S.op("sp", lambda e, tb=tb: e.dma_start(out=mT[:, tb].rearrange("p k t -> p (k t)"), in_=mixT_h[tb]),
                     reads=[("mixT_h", tb)], writes=[(g + "mT", tb)], dma=gd + f"m{tb % 4}")
            cnt = 0
            for nt in range(4):
                w = wt[nt % 2]
                wk = (g + "wt", nt % 2)
                S.op("pool", lambda e, nt=nt, w=w: e.dma_start(
                    out=w, in_=w_out[l, :, nt * 512:(nt + 1) * 512].rearrange("(kc p) n -> p kc n", p=128)),
                    writes=[wk], dma=gd + f"w{nt % 2}")
                for tb in range(NTB):
                    pi = PB + (cnt % 2)
                    x_ = xs[cnt % 4]
                    xk = (g + "xs", cnt % 4)
                    S.op("sp", lambda e, x_=x_, tb=tb, nt=nt: e.dma_start(
                        out=x_, in_=x_src[r0 + tb * 128:r0 + (tb + 1) * 128, nt * 512:(nt + 1) * 512]),
                        writes=[xk], dma=gd + f"xl{cnt % 4}")
                    for kc in range(16):
                        S.op("pe", lambda e, kc=kc, tb=tb, w=w, pi=pi: e.matmul(
                            psb[pi][:], mT[:, tb, kc, :], w[:, kc, :], start=(kc == 0), stop=(kc == 15)),
                            reads=[(g + "mT", tb), wk], writes=[pk(pi)])
                    S.op("dve", lambda e, x_=x_, pi=pi: e.tensor_tensor(x_, psb[pi][:], x_, ALU.add), reads=[pk(pi), xk], writes=[xk])
                    S.op("sp", lambda e, x_=x_, tb=tb, nt=nt: e.dma_start(
                        out=xmid[tb * 128:(tb + 1) * 128, nt * 512:(nt + 1) * 512], in_=x_),
                        reads=[xk], writes=[("xmid", tb)], dma=gd + f"xs{cnt % 4}")
                    cnt += 1
            S.barrier()

        def phase_D(l):
            ar.reset()
            g = f"D{ar.gen}."
            gd = "D."
            xt = [ar.alloc(D) for _ in range(2)]
            hb = [ar.alloc(D, BF16) for _ in range(2)]
            hs = [ar.alloc(2048, BF16) for _ in range(2)]
            for tb in range(NTB):
                i = tb % 2
                hv = hs[i].rearrange("p (k t) -> p k t", k=16)
                norm_block(g + f"{i}", gd + f"{i}", xmid[tb * 128:(tb + 1) * 128, :], g2b, "g2b", xt[i], hb[i],
                           lambda a, b, hv=hv: hv[:, a:b, :], [(g + "hs", i, 0), (g + "hs", i, 1)], 4 * i)
                S.op("sp", lambda e, i=i, tb=tb: e.dma_start(out=h2T_h[tb], in_=hs[i]),
                     reads=[(g + "hs", i, 0), (g + "hs", i, 1), ("xmid", tb)], writes=[("h2T_h", tb)], dma=gd + f"h{i}")
            S.barrier()

        def phase_CD(l, x_src, r0):
            ar.reset()
            g = f"F{ar.gen}."
            gd = "F."
            wo = ar.alloc(4 * 16 * 512, BF16).rearrange("p (n k c) -> p n k c", n=4, k=16)
            mTb = [ar.alloc(2048, BF16).rearrange("p (k t) -> p k t", k=16) for _ in range(2)]
            xr = [ar.alloc(D) for _ in range(2)]
            hb = [ar.alloc(D, BF16) for _ in range(2)]
            hs = [ar.alloc(2048, BF16) for _ in range(2)]
            for nt in range(4):
                S.op("pool", lambda e, nt=nt: e.dma_start(
                    out=wo[:, nt], in_=w_out[l, :, nt * 512:(nt + 1) * 512].rearrange("(kc p) n -> p kc n", p=128)),
                    writes=[(g + "wo", nt)], dma=gd + f"w{nt % 2}")
            cnt = [0]

            def mm_part(tb):
                i = tb % 2
                tag = g + f"{i}"
                S.op("sp", lambda e, i=i, tb=tb: e.dma_start(out=mTb[i].rearrange("p k t -> p (k t)"), in_=mixT_h[tb]),
                     reads=[("mixT_h", tb)], writes=[(g + "mT", i)], dma=gd + f"m{i}")
                S.op("sp", lambda e, i=i, tb=tb: e.dma_start(out=xr[i], in_=x_src[r0 + tb * 128:r0 + (tb + 1) * 128, :]),
                     writes=[tag + "xt"], dma=gd + f"x{i}")
                for nt in range(4):
                    pi = cnt[0] % 6
                    cnt[0] += 1
                    for kc in range(16):
                        S.op("pe", lambda e, kc=kc, nt=nt, i=i, pi=pi: e.matmul(
                            psb[pi][:], mTb[i][:, kc, :], wo[:, nt, kc, :], start=(kc == 0), stop=(kc == 15)),
                            reads=[(g + "mT", i), (g + "wo", nt)], writes=[pk(pi)])
                    S.op("dve", lambda e, nt=nt, i=i, pi=pi: e.tensor_tensor(
                        xr[i][:, nt * 512:(nt + 1) * 512], psb[pi][:], xr[i][:, nt * 512:(nt + 1) * 512], ALU.add),
                        reads=[pk(pi), tag + "xt"], writes=[tag + "xt"])

            def norm_part(tb):
                i = tb % 2
                tag = g + f"{i}"
                S.op("sp", lambda e, i=i, tb=tb: e.dma_start(out=xmid[tb * 128:(tb + 1) * 128, :], in_=xr[i]),
                     reads=[tag + "xt"], writes=[("xmid", tb)], dma=gd + f"s{i}")
                hv = hs[i].rearrange("p (k t) -> p k t", k=16)
                norm_block(tag, gd + f"{i}", None, g2b, "g2b", xr[i], hb[i],
                           lambda a, b, hv=hv: hv[:, a:b, :], [(g + "hs", i, 0), (g + "hs", i, 1)], 4 * i)
                S.op("sp", lambda e, i=i, tb=tb: e.dma_start(out=h2T_h[tb], in_=hs[i]),
                     reads=[(g + "hs", i, 0), (g + "hs", i, 1)], writes=[("h2T_h", tb)], dma=gd + f"h{i}")

            mm_part(0)
            for tb in range(NTB):
                if tb + 1 < NTB:
                    mm_part(tb + 1)
                norm_part(tb)
            S.barrier()

        def phase_E(l, y_dst, r0):
            ar.reset()
            g = f"E{ar.gen}."
            for half in range(2):
                ar.off = 0
                gd = "E."
                HT = GT // 2
                hT = ar.alloc(16 * HT, BF16).rearrange("p (k t) -> p k t", k=16)
                acc = ar.alloc(8 * D).rearrange("p (b d) -> p b d", b=8)
                wu = ar.alloc(16 * 512, BF16).rearrange("p (k f) -> p k f", k=16)
                wd = ar.alloc(4 * D, BF16).rearrange("p (f n) -> p f n", f=4)
                uT = ar.alloc(4 * HT, BF16).rearrange("p (f t) -> p f t", f=4)
                rt = [ar.alloc(512) for _ in range(2)]
                for b in range(8):
                    tb = half * 8 + b
                    S.op("sp", lambda e, b=b, tb=tb: e.dma_start(out=hT[:, :, b * 128:(b + 1) * 128],
                                                                 in_=h2T_h[tb].rearrange("p (k t) -> p k t", k=16)),
                         reads=[("h2T_h", tb)], writes=[(g + "hT", b)], dma=gd + f"h{b % 4}")
                for b in range(8):
                    tb = half * 8 + b
                    S.op("sp", lambda e, b=b, tb=tb: e.dma_start(out=acc[:, b, :], in_=xmid[tb * 128:(tb + 1) * 128, :]),
                         reads=[("xmid", tb)], writes=[(g + "acc", b)], dma=gd + f"a{b % 4}")
                cu = 0
                cd = 0
                for fcg in range(DFF // 512):
                    S.op("pool", lambda e, fcg=fcg: e.dma_start(
                        out=wu, in_=w_up[l, :, fcg * 512:(fcg + 1) * 512].rearrange("(kc p) f -> p kc f", p=128)),
                        writes=[g + "wu"], dma=gd + "wu")
                    S.op("pool", lambda e, fcg=fcg: e.dma_start(
                        out=wd, in_=w_down[l, fcg * 512:(fcg + 1) * 512, :].rearrange("(fb p) n -> p fb n", p=128)),
                        writes=[g + "wd"], dma=gd + "wd")
                    for fb in range(4):
                        for tt in range(HT // 512):
                            pi = PB + (cu % 2)
                            for kc in range(16):
                                S.op("pe", lambda e, kc=kc, fb=fb, tt=tt, pi=pi: e.matmul(
                                    psb[pi][:], wu[:, kc, fb * 128:(fb + 1) * 128], hT[:, kc, tt * 512:(tt + 1) * 512],
                                    start=(kc == 0), stop=(kc == 15)),
                                    reads=[g + "wu"] + [(g + "hT", b) for b in range(tt * 4, tt * 4 + 4)], writes=[pk(pi)])
                            r = rt[cu % 2]
                            rk = (g + "rt", cu % 2)
                            S.op("act", lambda e, r=r, pi=pi: e.activation(r, psb[pi][:], AF.Relu), reads=[pk(pi)], writes=[rk])
                            S.op("pool", lambda e, r=r, fb=fb, tt=tt: e.tensor_tensor(uT[:, fb, tt * 512:(tt + 1) * 512], r, r, ALU.mult),
                                 reads=[rk], writes=[(g + "uT", tt)])
                            cu += 1
                    for b in range(8):
                        for nt in range(4):
                            pi = PD + (cd % 4)
                            for fb in range(4):
                                S.op("pe", lambda e, fb=fb, b=b, nt=nt, pi=pi: e.matmul(
                                    psb[pi][:], uT[:, fb, b * 128:(b + 1) * 128], wd[:, fb, nt * 512:(nt + 1) * 512],
                                    start=(fb == 0), stop=(fb == 3)),
                                    reads=[(g + "uT", b // 4), g + "wd"], writes=[pk(pi)])
                            S.op("dve", lambda e, b=b, nt=nt, pi=pi: e.tensor_tensor(
                                acc[:, b, nt * 512:(nt + 1) * 512], psb[pi][:], acc[:, b, nt * 512:(nt + 1) * 512], ALU.add),
                                reads=[pk(pi), (g + "acc", b)], writes=[(g + "acc", b)])
                            cd += 1
                for b in range(8):
                    tb = half * 8 + b
                    S.op("sp", lambda e, b=b, tb=tb: e.dma_start(out=y_dst[r0 + tb * 128:r0 + (tb + 1) * 128, :], in_=acc[:, b, :]),
                         reads=[(g + "acc", b)], writes=[("ydst", r0, tb)], dma=gd + f"o{b % 4}")
            S.barrier()

        for step, l in (plan or []):
            layer_consts(l)
            src = x1_out if (mid_out and step == "pre") else x_in
            phase_A(l, src, 0)
            if step == "pre":
                phase_Bs(l)
            else:
                phase_B(l, 0, False, init_payload=True)
                phase_CD(l, src, 0)
                phase_E(l, x1_out if mid_out else y_out, 0)
        for l in range(n_layers if plan is None else 0):
            layer_consts(l)
            src = x_in if l == 0 else xa
            dst = y_out if l == n_layers - 1 else xa
            for gi in range(n_groups):
                r0 = gi * GT
                phase_A(l, src, r0)
                phase_B(l, gi * NTB, gi == 0)
                phase_CD(l, src, r0)
                phase_E(l, dst, r0)
        S.emit(nc)
    return nc, S


_CACHE = {}
MODE = "fused"
PLANS = ([("pre", 0)], [("main", 0), ("pre", 1)], [("main", 1)])


def kernel_unfused(x, **w):
    x = np.asarray(x, dtype=np.float32)
    B = x.shape[0]
    NP = SEQ // GT
    ncores = B * NP
    if "plans" not in _CACHE:
        _CACHE["plans"] = [build_program(GT, DEPTH, True, plan=list(p))[0] for p in PLANS]
    progs = _CACHE["plans"]
    consts = host_consts()
    shared = {k: np.ascontiguousarray(np.asarray(v, dtype=np.float32)) for k, v in w.items()}
    cb_first = consts["c_bias"]
    cb_rest = cb_first.copy()
    cb_rest[:, 2 * 2048:3 * 2048] = cb_first[:, 0:2048]
    cms = []
    for p in range(NP):
        cmk = np.zeros((128, 8), np.float32)
        for j in range(3):
            cmk[:, j] = 1.0 if j < p else 0.0
            cmk[:, 4 + j] = 1.0 - cmk[:, j]
        cms.append(cmk)
    lite_keys = ("norm1_g", "w_in", "q_norm_g", "k_norm_g", "attn_sinks", "gla_gate_w", "gla_gate_b", "gla_norm_g")

    def base(c, lite=False):
        p = c % NP
        d = {k: shared[k] for k in (lite_keys if lite else shared)}
        d.update(c_mats=consts["c_mats"], c_ident=consts["c_ident"], c_bias=cb_first if p == 0 else cb_rest)
        return d

    def halo(res):
        out = []
        zk = np.zeros((64, 512), ml_dtypes.bfloat16)
        zv = np.zeros((128, 256), ml_dtypes.bfloat16)
        for c in range(ncores):
            b, p = divmod(c, NP)
            gs = np.stack([np.asarray(res[b * NP + j]["pl_S"], np.float32) for j in range(NP)], 0)
            out.append(dict(g_S=gs, h_KT=res[c - 1]["pl_KT"] if p > 0 else zk, h_V=res[c - 1]["pl_V"] if p > 0 else zv,
                            cmask=cms[p]))
        return out

    xs = [np.ascontiguousarray(x[c // NP, (c % NP) * GT:(c % NP + 1) * GT]) for c in range(ncores)]
    r1 = run_bass_kernel_spmd(progs[0], [dict(base(c, True), x=xs[c]) for c in range(ncores)], core_ids=list(range(ncores))).results
    h1 = halo(r1)
    r2 = run_bass_kernel_spmd(progs[1], [dict(base(c), x=xs[c], **h1[c]) for c in range(ncores)], core_ids=list(range(ncores))).results
    h2 = halo(r2)
    r3 = run_bass_kernel_spmd(progs[2], [dict(base(c), x=np.asarray(r2[c]["x1"], np.float32), **h2[c]) for c in range(ncores)],
                              core_ids=list(range(ncores))).results
    out = np.empty((B, SEQ, D), np.float32)
    for c in range(ncores):
        out[c // NP, (c % NP) * GT:(c % NP + 1) * GT] = r3[c]["out"]
    return out


def kernel(x, norm1_g, w_in, q_norm_g, k_norm_g, attn_sinks, gla_gate_w, gla_gate_b,
           gla_norm_g, w_out, norm2_g, w_up, w_down):
    if MODE == "unfused":
        return kernel_unfused(x, norm1_g=norm1_g, w_in=w_in, q_norm_g=q_norm_g, k_norm_g=k_norm_g, attn_sinks=attn_sinks,
                              gla_gate_w=gla_gate_w, gla_gate_b=gla_gate_b, gla_norm_g=gla_norm_g, w_out=w_out,
                              norm2_g=norm2_g, w_up=w_up, w_down=w_down)
    x = np.asarray(x, dtype=np.float32)
    B = x.shape[0]
    if "nc" not in _CACHE:
        _CACHE["nc"] = build_program(SEQ, DEPTH)[0]
    nc = _CACHE["nc"]
    consts = host_consts()
    shared = dict(norm1_g=norm1_g, w_in=w_in, q_norm_g=q_norm_g, k_norm_g=k_norm_g, attn_sinks=attn_sinks,
                  gla_gate_w=gla_gate_w, gla_gate_b=gla_gate_b, gla_norm_g=gla_norm_g, w_out=w_out,
                  norm2_g=norm2_g, w_up=w_up, w_down=w_down)
    shared = {k: np.ascontiguousarray(np.asarray(v, dtype=np.float32)) for k, v in shared.items()}
    shared.update(consts)
    in_maps = [dict(shared, x=np.ascontiguousarray(x[b])) for b in range(B)]
    res = run_bass_kernel_spmd(nc, in_maps, core_ids=list(range(B)))
    return np.stack([res.results[b]["out"] for b in range(B)], 0).astype(np.float32)
```

```python
import numpy as np
import ml_dtypes
from contextlib import ExitStack
import concourse.bass as bass
import concourse.mybir as mybir
from concourse.bass_utils import run_bass_kernel_spmd

F32 = mybir.dt.float32
BF16 = mybir.dt.bfloat16
AF = mybir.ActivationFunctionType
ALU = mybir.AluOpType
AX = mybir.AxisListType

D = 2048
SEQ = 8192
DEPTH = 2
INW = 4624
DFF = 8192
PIPE_B = False
GLA_BANKS_PIPE = None
GT = 2048
NTB = GT // 128
EPS = 1e-6
NEG = -30000.0
ENGS = ("pe", "act", "dve", "pool", "sp")


class Sched:
    def __init__(self, same_engine_sync=True):
        self.ops = []
        self.ginc = {}
        self.same = (set(ENGS) if same_engine_sync is True else set() if not same_engine_sync else set(same_engine_sync)) - {'pe'}

    def op(self, eng, fn, reads=(), writes=(), dma=None, inc=16, atom=None):
        self.ops.append(dict(eng=eng, fn=fn, reads=tuple(reads), writes=tuple(writes), dma=dma, inc=inc, bar=False, atom=atom))
        if dma is not None:
            assert self.ginc.setdefault(dma, inc) == inc

    def barrier(self):
        for e in ENGS:
            self.ops.append(dict(eng=e, fn=None, reads=(), writes=(), dma=None, inc=0, bar=True))

    def resolve(self):
        last_w, readers = {}, {}
        eng_count = {e: 0 for e in ENGS}
        eng_last = {}
        dma_count, dma_last = {}, {}
        signal = set()
        for o in self.ops:
            e = o["eng"]
            if o["bar"]:
                deps = set(eng_last.values()) | set(dma_last.values())
                o["ev"] = None
                o["deps"] = deps
                for d in deps:
                    if d[0] == "eng":
                        signal.add((d[1], d[2]))
                continue
            if o["dma"] is None:
                ev = ("eng", e, eng_count[e])
                eng_count[e] += 1
            else:
                g = o["dma"]
                dma_count[g] = dma_count.get(g, 0) + 1
                ev = ("dma", g, dma_count[g])
            deps = set()
            for k in o["reads"]:
                if k in last_w:
                    deps.add(last_w[k])
            for k in o["writes"]:
                if k in last_w:
                    deps.add(last_w[k])
                deps.update(readers.get(k, ()))
            if o["dma"] is not None and o["dma"] in dma_last:
                deps.add(dma_last[o["dma"]])
            deps.discard(ev)
            red = {}
            for d in deps:
                kk = (d[0], d[1])
                if kk not in red or red[kk][2] < d[2]:
                    red[kk] = d
            deps = set(red.values())
            o["ev"], o["deps"] = ev, deps
            for d in deps:
                if d[0] == "eng":
                    if d[1] == e and e not in self.same:
                        continue
                    signal.add((d[1], d[2]))
            for k in o["reads"]:
                readers.setdefault(k, []).append(ev)
            for k in o["writes"]:
                last_w[k] = ev
                readers[k] = []
            if o["dma"] is not None:
                dma_last[o["dma"]] = ev
            else:
                eng_last[e] = ev
        self.signal = signal
        self.dma_groups = sorted(dma_count)

    def emit(self, nc):
        self.resolve()
        with ExitStack() as st:
            EPOCH = 30000
            cnt0 = {e: 0 for e in ENGS}
            idx0 = {e: 0 for e in ENGS}
            for o in self.ops:
                if o["dma"] is None and not o["bar"]:
                    e = o["eng"]
                    if (e, idx0[e]) in self.signal:
                        cnt0[e] += 1
                    idx0[e] += 1
            esem = {(e, k): st.enter_context(nc.semaphore(f"s_{e}{k}")) for e in ("pe", "act", "dve", "pool")
                    for k in range(cnt0[e] // EPOCH + 1)}
            dsem = {g: st.enter_context(nc.semaphore("d_" + str(g))) for g in self.dma_groups}
            block = st.enter_context(nc.Block())
            sigval, cnt, idxc = {}, {e: 0 for e in ENGS}, {e: 0 for e in ENGS}
            for o in self.ops:
                if o["dma"] is None and not o["bar"]:
                    e = o["eng"]
                    i = idxc[e]
                    idxc[e] += 1
                    if (e, i) in self.signal:
                        sigval[(e, i)] = (cnt[e] // EPOCH, cnt[e] % EPOCH + 1)
                        cnt[e] += 1
            self.sig_counts = cnt
            per = {e: [o for o in self.ops if o["eng"] == e] for e in ENGS}
            same = self.same

            def run_engine(e, engobj):
                seen = {}
                myidx = 0
                for o in per[e]:
                    need = {}
                    for d in o["deps"]:
                        if d[0] == "eng":
                            if d[1] == e and e not in same:
                                continue
                            key, val = ("eng", d[1]), sigval[(d[1], d[2])]
                        else:
                            key, val = ("dma", d[1]), (0, self.ginc[d[1]] * d[2])
                        if seen.get(key, (0, 0)) >= val:
                            continue
                        need[key] = max(need.get(key, (0, 0)), val)
                    for key, val in need.items():
                        engobj.wait_ge(esem[(key[1], val[0])] if key[0] == "eng" else dsem[key[1]], val[1])
                        seen[key] = val
                    if o["bar"]:
                        continue
                    ins = o["fn"](engobj)
                    if o["dma"] is not None:
                        ins.then_inc(dsem[o["dma"]], o["inc"])
                    else:
                        if (e, myidx) in self.signal:
                            ins.then_inc(esem[(e, sigval[(e, myidx)][0])], 1)
                        myidx += 1
                final = {}
                for o in per[e]:
                    if o["dma"] is not None:
                        final[o["dma"]] = max(final.get(o["dma"], 0), o["inc"] * o["ev"][2])
                for g, val in final.items():
                    if seen.get(("dma", g), (0, 0)) < (0, val):
                        engobj.wait_ge(dsem[g], val)

            for e, deco in (("sp", block.sync), ("pe", block.tensor), ("act", block.scalar),
                            ("dve", block.vector), ("pool", block.gpsimd)):
                if per[e]:
                    deco(lambda eng, e=e: run_engine(e, eng))


class Arena:
    def __init__(self, tile, ncols):
        self.tile, self.ncols, self.off, self.gen = tile, ncols, 0, 0

    def reset(self):
        self.off = 0
        self.gen += 1

    def alloc(self, cols, dtype=F32):
        n4 = cols if dtype == F32 else (cols + 1) // 2
        assert self.off + n4 <= self.ncols, (self.off, n4, self.ncols)
        v = self.tile[:, self.off:self.off + n4]
        self.off += n4
        return v if dtype == F32 else v.bitcast(BF16)


def host_consts():
    hs = np.arange(16)
    slopes = (2.0 ** (-8.0 * (hs + 1) / 16)).astype(np.float64)
    tk = np.arange(128)[:, None, None]
    tq = np.arange(128)[None, None, :]
    sl = slopes[None, :, None]
    dist_cur = tq - tk
    bias_cur = np.where(dist_cur >= 0, -sl * dist_cur, NEG)
    dist_prev = tq + 128 - tk
    bias_prev = np.where(dist_prev < 128, -sl * dist_prev, NEG)
    bias_first = np.full_like(bias_prev, NEG)
    s = np.arange(128)[:, None]
    t = np.arange(128)[None, :]
    same = (s // 64) == (t // 64)
    triu = np.where(same & (s <= t), -1.0 / 16, 0.0)
    strictl = np.where(same & (s > t), -1.0 / 16, 0.0)
    mask01 = np.where(same & (s <= t), 1.0, 0.0)
    chunksel = np.zeros((128, 2))
    chunksel[:64, 0] = -1.0 / 16
    chunksel[64:, 1] = -1.0 / 16
    c = dict(
        c_bias=np.stack([bias_prev, bias_cur, bias_first], 1).reshape(128, 3 * 2048).astype(np.float32),
        c_mats=np.concatenate([triu, strictl, mask01, chunksel], 1).astype(np.float32),
        c_ident=np.eye(128).astype(ml_dtypes.bfloat16),
    )
    return c


def build_program(n_tok, n_layers=DEPTH, same_sync=True, plan=None):
    n_groups = n_tok // GT
    nc = bass.Bass("TRN2", target_bir_lowering=False)
    dt = lambda name, shape, dtp=F32, kind="ExternalInput": nc.dram_tensor(name, shape, dtp, kind=kind).ap()
    lite = plan is not None and plan == [("pre", 0)]
    x_in = dt("x", [n_tok, D])
    norm1_g = dt("norm1_g", [DEPTH, D])
    w_in = dt("w_in", [DEPTH, D, INW])
    q_norm_g = dt("q_norm_g", [DEPTH, 64])
    k_norm_g = dt("k_norm_g", [DEPTH, 64])
    attn_sinks = dt("attn_sinks", [DEPTH, 16])
    gate_w = dt("gla_gate_w", [DEPTH, 16, 512])
    gate_b = dt("gla_gate_b", [DEPTH, 512])
    gla_norm_g = dt("gla_norm_g", [DEPTH, 256])
    if not lite:
        w_out = dt("w_out", [DEPTH, D, D])
        norm2_g = dt("norm2_g", [DEPTH, D])
        w_up = dt("w_up", [DEPTH, D, DFF])
        w_down = dt("w_down", [DEPTH, DFF, D])
    c_bias = dt("c_bias", [128, 3 * 2048])
    c_mats = dt("c_mats", [128, 386])
    c_ident = dt("c_ident", [128, 128], BF16)
    has_pre = plan is not None and any(p[0] == "pre" for p in plan)
    has_main = plan is not None and any(p[0] == "main" for p in plan)
    mid_out = plan is not None and len(plan) == 2
    y_out = x1_out = None
    if plan is None or plan == [("main", 1)]:
        y_out = dt("out", [n_tok, D], F32, "ExternalOutput")
    if mid_out:
        x1_out = dt("x1", [n_tok, D], F32, "ExternalOutput")
    if has_pre:
        pl_S = dt("pl_S", [128, 1028], F32, "ExternalOutput")
        pl_KT = dt("pl_KT", [64, 512], BF16, "ExternalOutput")
        pl_V = dt("pl_V", [128, 256], BF16, "ExternalOutput")
    if has_main:
        g_S = dt("g_S", [4, 128, 1028])
        h_KT = dt("h_KT", [64, 512], BF16)
        h_V = dt("h_V", [128, 256], BF16)
        cmask_d = dt("cmask", [128, 8])
    xa = nc.dram_tensor("xa", [n_tok, D], F32).ap()
    xmid = nc.dram_tensor("xmid", [GT, D], F32).ap()
    proj = nc.dram_tensor("proj", [GT, 4608], F32).ap()
    gzT_h = nc.dram_tensor("gzT_h", [16, GT], F32).ap()
    mixT_h = nc.dram_tensor("mixT_h", [NTB, 128, 16 * 128], BF16).ap()
    h2T_h = nc.dram_tensor("h2T_h", [NTB, 128, 16 * 128], BF16).ap()

    S = Sched(same_sync)
    with ExitStack() as st:
        sb = lambda name, shape, dtp=F32: st.enter_context(nc.sbuf_tensor(name, shape, dtp))
        ident = sb("ident", [128, 128], BF16)
        mats = sb("mats", [128, 386])
        triu, strictl, mask01, chunksel = mats[:, 0:128], mats[:, 128:256], mats[:, 256:384], mats[:, 384:386]
        g1b = sb("g1b", [128, D])
        g2b = sb("g2b", [128, D])
        g64 = sb("g64", [128, 128])
        gqk = sb("gqk", [128, 20, 64])
        esink = sb("esink", [128, 16])
        ggla = sb("ggla", [128, 256])
        gatew = sb("gatew", [17, 512])
        wgz = sb("wgz", [128, 16, 16], BF16)
        Sst = sb("Sst", [128, 4, 256])
        Smid = sb("Smid", [128, 4, 256])
        Sbf0 = [sb(f"Sbf0_{i}", [128, 4, 256], BF16) for i in range(2)]
        Sbf1 = sb("Sbf1", [128, 4, 256], BF16)
        KT = [sb(f"KT{i}", [64, 512], BF16) for i in range(2)]
        Vaug = [sb(f"Vaug{i}", [128, 4, 65], BF16) for i in range(2)]
        gzTa = [sb(f"gzTa{i}", [17, 128]) for i in range(2)]
        small = sb("small", [128, 64])
        cm = sb("cm", [128, 8])
        ARC = 38000
        arena_t = sb("arena", [128, ARC])
        ar = Arena(arena_t, ARC)
        psA = st.enter_context(nc.psum_tensor("psA", [128, 2048], BF16))
        psb = [st.enter_context(nc.psum_tensor(f"ps{i}", [128, 512], F32)) for i in range(6)]
        PB, PC, PD, PE_, PF, PG = range(6)
        pk = lambda i: ("ps", i)

        S.op("sp", lambda e: e.dma_start(out=ident[:], in_=c_ident), writes=["ident"], dma="c0")
        S.op("sp", lambda e: e.dma_start(out=mats[:], in_=c_mats), writes=["mats"], dma="c1")
        for i in range(2):
            S.op("pool", lambda e, i=i: e.memset(KT[i][:], 0.0), writes=[("KT", i)])
            S.op("pool", lambda e, i=i: e.memset(Vaug[i][:], 1.0), writes=[("Vaug", i)])
            S.op("pool", lambda e, i=i: e.memset(gzTa[i][:], 1.0), writes=[("gzTa", i)])

        def layer_consts(l):
            S.op("sp", lambda e: e.dma_start(out=g1b[:], in_=norm1_g[l].partition_broadcast(128)), writes=["g1b"], dma="c0")
            if not lite:
                S.op("sp", lambda e: e.dma_start(out=g2b[:], in_=norm2_g[l].partition_broadcast(128)), writes=["g2b"], dma="c1")
            S.op("sp", lambda e: e.dma_start(out=g64[:, 0:64], in_=q_norm_g[l].partition_broadcast(128)), writes=["g64a"], dma="c2")
            S.op("sp", lambda e: e.dma_start(out=g64[:, 64:128], in_=k_norm_g[l].partition_broadcast(128)), writes=["g64b"], dma="c3")
            S.op("dve", lambda e: e.tensor_copy(gqk[:, 0:16, :], g64[:, 0:64].unsqueeze(1).to_broadcast([128, 16, 64])),
                 reads=["g64a"], writes=["gqk_q"])
            S.op("dve", lambda e: e.tensor_copy(gqk[:, 16:20, :], g64[:, 64:128].unsqueeze(1).to_broadcast([128, 4, 64])),
                 reads=["g64b"], writes=["gqk_k"])
            S.op("sp", lambda e: e.dma_start(out=esink[:], in_=attn_sinks[l].partition_broadcast(128)), writes=["esink"], dma="c2")
            S.op("act", lambda e: e.activation(esink[:], esink[:], AF.Exp), reads=["esink"], writes=["esink"])
            S.op("sp", lambda e: e.dma_start(out=ggla[:], in_=gla_norm_g[l].partition_broadcast(128)), writes=["ggla"], dma="c3")
            S.op("sp", lambda e: e.dma_start(out=gatew[0:16, :], in_=gate_w[l]), writes=["gatew0"], dma="c0")
            S.op("sp", lambda e: e.dma_start(out=gatew[16:17, :], in_=gate_b[l:l + 1, :]), writes=["gatew1"], dma="c1")
            S.op("pool", lambda e: e.dma_start(out=wgz[:], in_=w_in[l, :, 4608:4624].rearrange("(kc p) n -> p kc n", p=128)),
                 writes=["wgz"], dma="c4")

        def norm_block(tag, dtag, src_rows, gb, gkey, xt, hb, dstT_fn, dst_keys, ssi):
            if src_rows is not None:
                S.op("sp", lambda e: e.dma_start(out=xt, in_=src_rows), writes=[tag + "xt"], dma=dtag + "x")
            S.op("act", lambda e: e.activation(hb, xt, AF.Square, accum_out=small[:, ssi:ssi + 1]),
                 reads=[tag + "xt"], writes=[tag + "hb", ("small", ssi)])
            S.op("act", lambda e: e.activation(small[:, ssi + 1:ssi + 2], small[:, ssi:ssi + 1], AF.Ln, scale=1.0 / D, bias=EPS),
                 reads=[("small", ssi)], writes=[("small", ssi + 1)])
            S.op("act", lambda e: e.activation(small[:, ssi + 2:ssi + 3], small[:, ssi + 1:ssi + 2], AF.Exp, scale=-0.5),
                 reads=[("small", ssi + 1)], writes=[("small", ssi + 2)])
            S.op("dve", lambda e: e.scalar_tensor_tensor(hb, xt, small[:, ssi + 2:ssi + 3], gb[:], ALU.mult, ALU.mult),
                 reads=[tag + "xt", ("small", ssi + 2), gkey, tag + "hb"], writes=[tag + "hb"])
            for kc in range(16):
                S.op("pe", lambda e, kc=kc: e.transpose(psA[:, kc * 128:(kc + 1) * 128], hb[:, kc * 128:(kc + 1) * 128], ident[:]),
                     reads=[tag + "hb", "ident"], writes=[("psA", kc // 8)])
            pv = psA[:].rearrange("p (k t) -> p k t", k=16)
            S.op("act", lambda e: e.copy(dstT_fn(0, 8), pv[:, 0:8, :]), reads=[("psA", 0)], writes=[dst_keys[0]])
            S.op("dve", lambda e: e.tensor_copy(dstT_fn(8, 16), pv[:, 8:16, :]), reads=[("psA", 1)], writes=[dst_keys[1]])

        def phase_A(l, x_src, r0):
            ar.reset()
            g = f"A{ar.gen}."
            gd = "A."
            hT = ar.alloc(16 * GT, BF16).rearrange("p (k t) -> p k t", k=16)
            wt = [ar.alloc(16 * 512, BF16).rearrange("p (k n) -> p k n", k=16) for _ in range(2)]
            xt = [ar.alloc(D) for _ in range(2)]
            hb = [ar.alloc(D, BF16) for _ in range(2)]
            stg = [ar.alloc(512) for _ in range(4)]
            gzs = ar.alloc(GT)
            for tb in range(NTB):
                i = tb % 2
                norm_block(g + f"{i}", gd + f"{i}", x_src[r0 + tb * 128: r0 + (tb + 1) * 128, :], g1b, "g1b", xt[i], hb[i],
                           lambda a, b, tb=tb: hT[:, a:b, tb * 128:(tb + 1) * 128],
                           [(g + "hT", tb, 0), (g + "hT", tb, 1)], 4 * i)
            hkeys = lambda tbs: [(g + "hT", tb, i) for tb in tbs for i in range(2)]
            for tt in range(GT // 512):
                for kc in range(16):
                    S.op("pe", lambda e, kc=kc, tt=tt: e.matmul(psb[PB][0:16, :], wgz[:, kc, :], hT[:, kc, tt * 512:(tt + 1) * 512],
                                                                start=(kc == 0), stop=(kc == 15)),
                         reads=hkeys(range(tt * 4, tt * 4 + 4)) + ["wgz"], writes=[pk(PB)])
                S.op("act", lambda e, tt=tt: e.copy(gzs[0:16, tt * 512:(tt + 1) * 512], psb[PB][0:16, :]),
                     reads=[pk(PB)], writes=[g + "gzs"])
            S.op("sp", lambda e: e.dma_start(out=gzT_h, in_=gzs[0:16, :]), reads=[g + "gzs"], writes=["gzT_h"], dma=gd + "gz")
            cnt = 0
            for nt in range(9):
                w = wt[nt % 2]
                wk = (g + "wt", nt % 2)
                S.op("pool", lambda e, nt=nt, w=w: e.dma_start(
                    out=w, in_=w_in[l, :, nt * 512:(nt + 1) * 512].rearrange("(kc p) n -> p kc n", p=128)),
                    writes=[wk], dma=gd + f"w{nt % 2}")
                for tb in range(NTB):
                    pi = PB + (cnt % 2)
                    for kc in range(16):
                        S.op("pe", lambda e, kc=kc, tb=tb, w=w, pi=pi: e.matmul(
                            psb[pi][:], hT[:, kc, tb * 128:(tb + 1) * 128], w[:, kc, :], start=(kc == 0), stop=(kc == 15)),
                            reads=hkeys([tb]) + [wk], writes=[pk(pi)])
                    sg = stg[cnt % 4]
                    sk = (g + "stg", cnt % 4)
                    if cnt % 2 == 0:
                        S.op("act", lambda e, sg=sg, pi=pi: e.copy(sg, psb[pi][:]), reads=[pk(pi)], writes=[sk])
                    else:
                        S.op("dve", lambda e, sg=sg, pi=pi: e.tensor_copy(sg, psb[pi][:]), reads=[pk(pi)], writes=[sk])
                    S.op("sp", lambda e, sg=sg, tb=tb, nt=nt: e.dma_start(
                        out=proj[tb * 128:(tb + 1) * 128, nt * 512:(nt + 1) * 512], in_=sg),
                        reads=[sk], writes=[("proj", tb)], dma=gd + f"st{cnt % 4}")
                    cnt += 1
            S.barrier()

        def phase_B(l, gblk0, first_group, init_payload=False):
            ar.reset()
            SST_ALL = [("Sst", h) for h in range(4)]
            GLA_BANKS = (PG, PE_, PE_)
            g = f"B{ar.gen}."
            gd = "B."
            bias = ar.alloc(3 * 2048)
            Pt = [ar.alloc(4608) for _ in range(2)]
            sqt = ar.alloc(1280)
            qkn = ar.alloc(1280, BF16)
            QTs = ar.alloc(2048, BF16)
            spt = [ar.alloc(512) for _ in range(2)]
            PT = [[ar.alloc(512, BF16) for _ in range(2)] for _ in range(2)]
            mix = [ar.alloc(D, BF16) for _ in range(2)]
            et = ar.alloc(512)
            lat = ar.alloc(512)
            Eb = ar.alloc(512)
            Enb = ar.alloc(512)
            Ebl = ar.alloc(512)
            qin = ar.alloc(512, BF16)
            kin = ar.alloc(512, BF16)
            kst = ar.alloc(512, BF16)
            gvb = ar.alloc(1024, BF16)
            sgt = ar.alloc(1024)
            qT0 = ar.alloc(512, BF16)
            qT1 = ar.alloc(512, BF16)
            kinT = ar.alloc(512, BF16)
            attm = ar.alloc(512, BF16)
            tmpo = [ar.alloc(256) for _ in range(2)]
            junk = ar.alloc(256)
            mTs = [ar.alloc(2048, BF16) for _ in range(2)]
            S.op("sp", lambda e: e.dma_start(out=bias, in_=c_bias), writes=[g + "bias"], dma=gd + "bias")
            S.op("pool", lambda e: e.memset(qT0, 0.0), writes=[g + "qT0"])
            S.op("pool", lambda e: e.memset(qT1, 0.0), writes=[g + "qT1"])
            if init_payload:
                Sj = ar.alloc(1028)
                tmpc = ar.alloc(256)
                fac = ar.alloc(4)
                S.op("sp", lambda e: e.dma_start(out=cm[:], in_=cmask_d), writes=["cm"], dma=gd + "cm")
                S.op("pool", lambda e: e.memset(Sst[:], 0.0), writes=SST_ALL)
                for j in range(3):
                    S.op("sp", lambda e, j=j: e.dma_start(out=Sj, in_=g_S[j]), writes=[g + "Sj"], dma=gd + "Sj")
                    S.op("act", lambda e: e.activation(fac, Sj[:, 1024:1028], AF.Exp), reads=[g + "Sj"], writes=[g + "fac"])
                    S.op("dve", lambda e, j=j: e.tensor_scalar(fac, fac, cm[:, j:j + 1], cm[:, 4 + j:5 + j], ALU.mult, ALU.add),
                         reads=[g + "fac", "cm"], writes=[g + "fac"])
                    for h in range(4):
                        S.op("dve", lambda e, j=j, h=h: e.tensor_scalar_mul(tmpc, Sj[:, h * 256:(h + 1) * 256], cm[:, j:j + 1]),
                             reads=[g + "Sj", "cm"], writes=[g + "tmpc"])
                        S.op("dve", lambda e, h=h: e.scalar_tensor_tensor(Sst[:, h, :], Sst[:, h, :], fac[:, h:h + 1], tmpc, ALU.mult, ALU.add),
                             reads=[("Sst", h), g + "fac", g + "tmpc"], writes=[("Sst", h)])
                S.op("act", lambda e: e.copy(Sbf0[0][:], Sst[:]), reads=SST_ALL, writes=[("Sbf0", 0, h) for h in range(4)])
                S.op("sp", lambda e: e.dma_start(out=KT[1][:], in_=h_KT), writes=[("KT", 1)], dma=gd + "hk")
                S.op("sp", lambda e: e.dma_start(out=Vaug[1][:, :, 0:64], in_=h_V.rearrange("p (j d) -> p j d", j=4)),
                     writes=[("Vaug", 1)], dma=gd + "hv")
            if first_group:
                for i in range(2):
                    S.op("pool", lambda e, i=i: e.memset(KT[i][:], 0.0), writes=[("KT", i)])
                    S.op("pool", lambda e, i=i: e.memset(Vaug[i][:, :, 0:64], 0.0), writes=[("Vaug", i)])
                S.op("pool", lambda e: e.memset(Sst[:], 0.0), writes=SST_ALL)
                S.op("pool", lambda e: e.memset(Sbf0[0][:], 0.0), writes=[("Sbf0", 0, h) for h in range(4)])
            def swa(tb, part):
                pb = tb % 2
                P = Pt[pb]
                Pk = (g + "P", pb)
                if part == "prep":
                    swa_prep(tb, pb, P, Pk)
                else:
                    swa_main(tb, pb, P, Pk)

            def swa_prep(tb, pb, P, Pk):
                S.op("sp", lambda e, P=P, tb=tb: e.dma_start(out=P, in_=proj[tb * 128:(tb + 1) * 128, :]),
                     reads=[("proj", tb)], writes=[Pk], dma=gd + f"P{pb}")
                S.op("pool", lambda e, P=P: e.tensor_tensor(sqt, P[:, 0:1280], P[:, 0:1280], ALU.mult), reads=[Pk], writes=[g + "sqt"])
                S.op("dve", lambda e: e.tensor_reduce(small[:, 8:28], sqt.rearrange("p (h d) -> p h d", h=20), AX.X, ALU.add),
                     reads=[g + "sqt"], writes=["ssq"])
                S.op("act", lambda e: e.activation(small[:, 8:28], small[:, 8:28], AF.Ln, scale=1.0 / 64, bias=EPS),
                     reads=["ssq"], writes=["ssq"])
                S.op("act", lambda e: e.activation(small[:, 8:28], small[:, 8:28], AF.Exp, scale=-0.5), reads=["ssq"], writes=["ssq"])
                S.op("dve", lambda e, P=P: e.tensor_tensor(sqt.rearrange("p (h d) -> p h d", h=20),
                                                          P[:, 0:1280].rearrange("p (h d) -> p h d", h=20),
                                                          small[:, 8:28].unsqueeze(2).to_broadcast([128, 20, 64]), ALU.mult),
                     reads=[Pk, "ssq", g + "sqt"], writes=[g + "sqt"])
                S.op("pool", lambda e: e.tensor_tensor(qkn.rearrange("p (h d) -> p h d", h=20),
                                                      sqt.rearrange("p (h d) -> p h d", h=20), gqk[:], ALU.mult),
                     reads=[g + "sqt", "gqk_q", "gqk_k"], writes=[g + "qkn"])
                S.op("act", lambda e, P=P, pb=pb: e.copy(Vaug[pb][:, :, 0:64], P[:, 1280:1536].rearrange("p (j d) -> p j d", j=4)),
                     reads=[Pk], writes=[("Vaug", pb)])

            def swa_main(tb, pb, P, Pk):
                pbb = psb[PB][:].bitcast(BF16)
                for j in range(4):
                    S.op("pe", lambda e, j=j: e.transpose(pbb[0:64, j * 128:(j + 1) * 128], qkn[:, 1024 + j * 64:1024 + (j + 1) * 64], ident[:]),
                         reads=[g + "qkn", "ident"], writes=[pk(PB)])
                S.op("act", lambda e, pb=pb: e.copy(KT[pb][:], pbb[0:64, 0:512]), reads=[pk(PB)], writes=[("KT", pb)])
                for h in range(16):
                    S.op("pe", lambda e, h=h: e.transpose(psA[0:64, h * 128:(h + 1) * 128], qkn[:, h * 64:(h + 1) * 64], ident[:]),
                         reads=[g + "qkn", "ident"], writes=[("psA", h // 8)])
                S.op("dve", lambda e: e.tensor_copy(QTs[0:64, 0:1024], psA[0:64, 0:1024]), reads=[("psA", 0)], writes=[g + "QT0"])
                S.op("act", lambda e: e.copy(QTs[0:64, 1024:2048], psA[0:64, 1024:2048]), reads=[("psA", 1)], writes=[g + "QT1"])
                first_blk = (first_group or init_payload) and tb == 0
                def scores(j):
                    for c in range(2):
                        cnt = 2 * j + c
                        src = 1 - pb if c == 0 else pb
                        pi = PB + (cnt % 2)
                        bsel = (2 if first_blk else 0) if c == 0 else 1
                        S.op("pe", lambda e, j=j, src=src, pi=pi: e.matmul(psb[pi][:], KT[src][0:64, j * 128:(j + 1) * 128],
                                                                         QTs[0:64, j * 512:(j + 1) * 512], start=True, stop=True),
                             reads=[("KT", src), g + "QT0", g + "QT1"], writes=[pk(pi)])
                        S.op("dve", lambda e, j=j, pi=pi, bsel=bsel, cnt=cnt: e.scalar_tensor_tensor(
                            spt[cnt % 2], psb[pi][:], 0.125, bias[:, bsel * 2048 + j * 512: bsel * 2048 + (j + 1) * 512], ALU.mult, ALU.add),
                            reads=[pk(pi), g + "bias"], writes=[(g + "spt", cnt % 2)])
                        S.op("act", lambda e, j=j, c=c, cnt=cnt: e.activation(PT[j % 2][c], spt[cnt % 2], AF.Exp),
                             reads=[(g + "spt", cnt % 2)], writes=[(g + "PT", j % 2, c)])

                def pv(j):
                    pdv = psb[PD][:, 0:260].rearrange("p (h d) -> p h d", h=4)
                    for hl in range(4):
                        for c in range(2):
                            src = 1 - pb if c == 0 else pb
                            S.op("pe", lambda e, j=j, hl=hl, c=c, src=src: e.matmul(
                                psb[PD][:, hl * 65:(hl + 1) * 65], PT[j % 2][c][:, hl * 128:(hl + 1) * 128], Vaug[src][:, j, :],
                                start=(c == 0), stop=(c == 1)),
                                reads=[(g + "PT", j % 2, c), ("Vaug", src)], writes=[pk(PD)], atom=("pv", tb, j, hl))
                    S.op("dve", lambda e, j=j: e.tensor_tensor(small[:, 32:36], pdv[:, :, 64], esink[:, 4 * j:4 * j + 4], ALU.add),
                         reads=[pk(PD), "esink"], writes=["den"])
                    S.op("dve", lambda e: e.reciprocal(small[:, 32:36], small[:, 32:36]), reads=["den"], writes=["den"])
                    S.op("dve", lambda e, j=j, pb=pb: e.tensor_tensor(
                        mix[pb][:, j * 256:(j + 1) * 256].rearrange("p (h d) -> p h d", h=4), pdv[:, :, 0:64],
                        small[:, 32:36].unsqueeze(2).to_broadcast([128, 4, 64]), ALU.mult),
                        reads=[pk(PD), "den"], writes=[(g + "mixa", pb)])

                scores(0)
                for j in range(4):
                    if j + 1 < 4:
                        scores(j + 1)
                    pv(j)

            def gla(tb, part):
                pb = tb % 2
                P = Pt[pb]
                Pk = (g + "P", pb)
                WE = [pk(PE_)]
                TBK, ABK, UBK = GLA_BANKS
                WF = [pk(PF)]
                WG = [pk(PG)]
                S.op("sp", lambda e, tb=tb, pb=pb: e.dma_start(out=gzTa[pb][0:16, :], in_=gzT_h[:, tb * 128:(tb + 1) * 128]),
                     reads=["gzT_h"], writes=[("gzTa", pb)], dma=gd + f"gz{pb}")
                S.op("pe", lambda e, pb=pb: e.matmul(psb[PE_][:], gzTa[pb][:], gatew[:], start=True, stop=True),
                     reads=[("gzTa", pb), "gatew0", "gatew1"], writes=WE)
                S.op("act", lambda e: e.activation(et, psb[PE_][:], AF.Exp, scale=-1.0), reads=[pk(PE_)], writes=[g + "et"])
                S.op("act", lambda e: e.activation(lat, et, AF.Ln, bias=1.0), reads=[g + "et"], writes=[g + "lat"])
                S.op("pe", lambda e: e.matmul(psb[PF][:], triu, lat, start=True, stop=True), reads=["mats", g + "lat"], writes=WF)
                for h in range(4):
                    S.op("pe", lambda e, h=h: e.matmul(psb[PE_][:, 2 * h:2 * h + 2], lat[:, h * 128:(h + 1) * 128], chunksel, start=True, stop=True),
                         reads=["mats", g + "lat"], writes=WE)
                S.op("act", lambda e: e.activation(small[:, 40:48], psb[PE_][:, 0:8], AF.Exp), reads=WE, writes=["dec"])
                S.op("pe", lambda e: e.matmul(psb[PE_][:], strictl, lat, start=True, stop=True), reads=["mats", g + "lat"], writes=WE)
                S.op("act", lambda e: e.activation(Eb, psb[PF][:], AF.Exp), reads=WF, writes=[g + "Eb"])
                S.op("act", lambda e: e.activation(Enb, psb[PF][:], AF.Exp, scale=-1.0), reads=WF, writes=[g + "Enb"])
                S.op("act", lambda e: e.activation(Ebl, psb[PE_][:], AF.Exp), reads=[pk(PE_)], writes=[g + "Ebl"])
                S.op("dve", lambda e, P=P: e.scalar_tensor_tensor(qin, P[:, 1536:2048], 128.0 ** -0.5, Eb, ALU.mult, ALU.mult),
                     reads=[Pk, g + "Eb"], writes=[g + "qin"])
                S.op("pool", lambda e, P=P: e.tensor_tensor(kin, P[:, 2048:2560], Enb, ALU.mult), reads=[Pk, g + "Enb"], writes=[g + "kin"])
                S.op("pool", lambda e, P=P: e.tensor_tensor(kst, P[:, 2048:2560], Ebl, ALU.mult), reads=[Pk, g + "Ebl"], writes=[g + "kst"])
                S.op("act", lambda e, P=P: e.copy(gvb, P[:, 2560:3584]), reads=[Pk], writes=[g + "gvb"])
                S.op("act", lambda e, P=P: e.activation(sgt, P[:, 3584:4608], AF.Exp, scale=-1.0), reads=[Pk], writes=[g + "sgt"])
                S.op("dve", lambda e: e.tensor_scalar_add(sgt, sgt, 1.0), reads=[g + "sgt"], writes=[g + "sgt"])
                S.op("dve", lambda e: e.reciprocal(sgt, sgt), reads=[g + "sgt"], writes=[g + "sgt"])
                S.op("pool", lambda e, P=P: e.tensor_tensor(sgt, sgt, P[:, 3584:4608], ALU.mult), reads=[g + "sgt", Pk], writes=[g + "sgt"])
                if part == "head":
                    return
                pfb = psb[TBK][:].bitcast(BF16)
                for h in range(4):
                    S.op("pe", lambda e, h=h: e.transpose(pfb[:, h * 128:(h + 1) * 128], qin[:, h * 128:(h + 1) * 128], ident[:]),
                         reads=[g + "qin", "ident"], writes=[pk(TBK)])
                for h in range(4):
                    S.op("pe", lambda e, h=h: e.transpose(pfb[:, 512 + h * 128:512 + (h + 1) * 128], kin[:, h * 128:(h + 1) * 128], ident[:]),
                         reads=[g + "kin", "ident"], writes=[pk(TBK)])
                v3 = lambda ap: ap.rearrange("p (h t) -> p h t", h=4)
                S.op("act", lambda e: e.copy(v3(qT0)[:, :, 0:64], v3(pfb[:, 0:512])[:, :, 0:64]), reads=[pk(TBK)], writes=[g + "qT0"])
                S.op("dve", lambda e: e.tensor_copy(v3(qT1)[:, :, 64:128], v3(pfb[:, 0:512])[:, :, 64:128]), reads=[pk(TBK)], writes=[g + "qT1"])
                S.op("act", lambda e: e.copy(kinT, pfb[:, 512:1024]), reads=[pk(TBK)], writes=[g + "kinT"])
                for h in range(4):
                    S.op("pe", lambda e, h=h: e.matmul(psb[ABK][:, h * 128:h * 128 + 64], kinT[:, h * 128:(h + 1) * 128],
                                                       qT0[:, h * 128:h * 128 + 64], start=True, stop=True),
                         reads=[g + "kinT", g + "qT0"], writes=[pk(ABK)])
                    S.op("pe", lambda e, h=h: e.matmul(psb[ABK][:, h * 128 + 64:(h + 1) * 128], kinT[:, h * 128:(h + 1) * 128],
                                                       qT1[:, h * 128 + 64:(h + 1) * 128], start=True, stop=True),
                         reads=[g + "kinT", g + "qT1"], writes=[pk(ABK)])
                S.op("dve", lambda e: e.tensor_tensor(v3(attm), v3(psb[ABK][:]), mask01.unsqueeze(1).to_broadcast([128, 4, 128]), ALU.mult),
                     reads=[pk(ABK), "mats"], writes=[g + "attm"])
                s0 = Sbf0[pb]
                s0n = Sbf0[1 - pb]
                for h in range(4):
                    S.op("pe", lambda e, h=h: e.matmul(psb[UBK][:, 0:256], kst[0:64, h * 128:(h + 1) * 128], gvb[0:64, h * 256:(h + 1) * 256],
                                                       start=True, stop=True),
                         reads=[g + "kst", g + "gvb"], writes=[pk(UBK)])
                    S.op("dve", lambda e, h=h: e.scalar_tensor_tensor(Smid[:, h, :], Sst[:, h, :], small[:, 40 + 2 * h:41 + 2 * h],
                                                                      psb[UBK][:, 0:256], ALU.mult, ALU.add),
                         reads=[("Sst", h), "dec", pk(UBK)], writes=[("Smid", h)])
                    S.op("act", lambda e, h=h: e.copy(Sbf1[:, h, :], Smid[:, h, :]), reads=[("Smid", h)], writes=[("Sbf1", h)])
                    S.op("pe", lambda e, h=h: e.matmul(psb[PF][:, 0:256], kst[64:128, h * 128:(h + 1) * 128], gvb[64:128, h * 256:(h + 1) * 256],
                                                       start=True, stop=True),
                         reads=[g + "kst", g + "gvb"], writes=[pk(PF)])
                    og = psb[PG][:, 0:256]
                    ogk = pk(PG)
                    S.op("pe", lambda e, h=h, og=og: e.matmul(og, attm[:, h * 128:(h + 1) * 128], gvb[:, h * 256:(h + 1) * 256], start=True, stop=False),
                         reads=[g + "attm", g + "gvb"], writes=[ogk], atom=("og", tb, h))
                    S.op("pe", lambda e, h=h, og=og, s0=s0: e.matmul(og, qT0[:, h * 128:(h + 1) * 128], s0[:, h, :], start=False, stop=False),
                         reads=[g + "qT0", ("Sbf0", pb, h)], writes=[ogk], atom=("og", tb, h))
                    S.op("pe", lambda e, h=h, og=og: e.matmul(og, qT1[:, h * 128:(h + 1) * 128], Sbf1[:, h, :], start=False, stop=True),
                         reads=[g + "qT1", ("Sbf1", h)], writes=[ogk], atom=("og", tb, h))
                    S.op("dve", lambda e, h=h: e.scalar_tensor_tensor(Sst[:, h, :], Smid[:, h, :], small[:, 41 + 2 * h:42 + 2 * h],
                                                                      psb[PF][:, 0:256], ALU.mult, ALU.add),
                         reads=[("Smid", h), "dec", pk(PF), ("Sst", h)], writes=[("Sst", h)])
                    S.op("act", lambda e, h=h, s0n=s0n: e.copy(s0n[:, h, :], Sst[:, h, :]), reads=[("Sst", h)], writes=[("Sbf0", 1 - pb, h)])
                    S.op("act", lambda e, h=h, og=og: e.activation(junk, og, AF.Square, accum_out=small[:, 48 + h:49 + h]),
                         reads=[ogk], writes=[g + "junk", ("sso", h)])
                    S.op("act", lambda e, h=h: e.activation(small[:, 52 + h:53 + h], small[:, 48 + h:49 + h], AF.Ln, scale=1.0 / 256, bias=EPS),
                         reads=[("sso", h)], writes=[("sso2", h)])
                    S.op("act", lambda e, h=h: e.activation(small[:, 56 + h:57 + h], small[:, 52 + h:53 + h], AF.Exp, scale=-0.5),
                         reads=[("sso2", h)], writes=[("sso3", h)])
                    S.op("dve", lambda e, h=h, og=og: e.scalar_tensor_tensor(tmpo[h % 2], og, small[:, 56 + h:57 + h], ggla[:], ALU.mult, ALU.mult),
                         reads=[ogk, ("sso3", h), "ggla"], writes=[(g + "tmpo", h % 2)])
                    S.op("pool", lambda e, h=h, pb=pb: e.tensor_tensor(mix[pb][:, 1024 + h * 256:1024 + (h + 1) * 256], tmpo[h % 2],
                                                                      sgt[:, h * 256:(h + 1) * 256], ALU.mult),
                         reads=[(g + "tmpo", h % 2), g + "sgt"], writes=[(g + "mixg", pb)])

            def mixt(tb):
                pb = tb % 2
                for kc in range(16):
                    S.op("pe", lambda e, kc=kc, pb=pb: e.transpose(psA[:, kc * 128:(kc + 1) * 128], mix[pb][:, kc * 128:(kc + 1) * 128], ident[:]),
                         reads=[(g + "mixa", pb), (g + "mixg", pb), "ident"], writes=[("psA", kc // 8)])
                S.op("act", lambda e, pb=pb: e.copy(mTs[pb][:, 0:1024], psA[:, 0:1024]), reads=[("psA", 0)], writes=[(g + "mTs", pb)])
                S.op("dve", lambda e, pb=pb: e.tensor_copy(mTs[pb][:, 1024:2048], psA[:, 1024:2048]), reads=[("psA", 1)], writes=[(g + "mTs", pb)])
                S.op("sp", lambda e, pb=pb, tb=tb: e.dma_start(out=mixT_h[tb], in_=mTs[pb]), reads=[(g + "mTs", pb), (g + "mixa", pb), (g + "mixg", pb)],
                     writes=[("mixT_h", tb)], dma=gd + f"mT{pb}")

            def cap(fn, *a):
                n0 = len(S.ops)
                fn(*a)
                lst = S.ops[n0:]
                del S.ops[n0:]
                return lst

            def units(lst):
                u = []
                for o in lst:
                    if u and o.get("atom") is not None and u[-1][-1].get("atom") == o["atom"]:
                        u[-1].append(o)
                    else:
                        u.append([o])
                return u

            def merge(a, b):
                a, b = units(a), units(b)
                out, i, j = [], 0, 0
                while i < len(a) or j < len(b):
                    if j >= len(b) or (i < len(a) and i * len(b) <= j * len(a)):
                        out.extend(a[i]); i += 1
                    else:
                        out.extend(b[j]); j += 1
                return out

            def gla_tail(tb, prep_ops):
                n_head = len(cap(gla, tb, "head"))
                tail = cap(gla, tb, "tail")[n_head:]
                ia = max(i for i, o in enumerate(tail) if (g + "attm") in o["writes"]) + 1
                S.ops.extend(tail[:ia])
                rest = tail[ia:]
                assert len(rest) % 4 == 0
                nh = len(rest) // 4
                npc = (len(prep_ops) + 3) // 4
                for h in range(4):
                    S.ops.extend(rest[h * nh:(h + 1) * nh])
                    S.ops.extend(prep_ops[h * npc:(h + 1) * npc])

            swa(0, "prep")
            swa(0, "main")
            swa(1, "prep")
            for tb in range(NTB):
                if tb >= 1:
                    mixt(tb - 1)
                if tb + 1 < NTB:
                    swa(tb + 1, "main")
                gla(tb, "head")
                gla_tail(tb, cap(swa, tb + 2, "prep") if tb + 2 < NTB else [])
            mixt(NTB - 1)
            S.barrier()

        def phase_Bs(l):
            ar.reset()
            g = f"S{ar.gen}."
            gd = "S."
            Pg = [ar.alloc(1536) for _ in range(2)]
            Pkv = ar.alloc(512)
            et = ar.alloc(512)
            lat = ar.alloc(512)
            Ebl = ar.alloc(512)
            kst = ar.alloc(512, BF16)
            gvb = ar.alloc(1024, BF16)
            Dl = ar.alloc(8)
            sq = ar.alloc(256)
            kn = ar.alloc(256, BF16)
            KTo = ar.alloc(512, BF16)
            Vo = ar.alloc(256, BF16)
            S.op("pool", lambda e: e.memset(Sst[:], 0.0), writes=["Sst"])
            S.op("pool", lambda e: e.memset(Dl, 0.0), writes=[g + "Dl"])
            for tb in range(NTB):
                pb = tb % 2
                P = Pg[pb]
                Pk = (g + "P", pb)
                S.op("sp", lambda e, P=P, tb=tb: e.dma_start(out=P, in_=proj[tb * 128:(tb + 1) * 128, 2048:3584]),
                     reads=[("proj", tb)], writes=[Pk], dma=gd + f"P{pb}")
                S.op("sp", lambda e, tb=tb, pb=pb: e.dma_start(out=gzTa[pb][0:16, :], in_=gzT_h[:, tb * 128:(tb + 1) * 128]),
                     reads=["gzT_h"], writes=[("gzTa", pb)], dma=gd + f"gz{pb}")
                S.op("pe", lambda e, pb=pb: e.matmul(psb[PE_][:], gzTa[pb][:], gatew[:], start=True, stop=True),
                     reads=[("gzTa", pb), "gatew0", "gatew1"], writes=[pk(PE_)])
                S.op("act", lambda e: e.activation(et, psb[PE_][:], AF.Exp, scale=-1.0), reads=[pk(PE_)], writes=[g + "et"])
                S.op("act", lambda e: e.activation(lat, et, AF.Ln, bias=1.0), reads=[g + "et"], writes=[g + "lat"])
                S.op("pe", lambda e: e.matmul(psb[PE_][:], strictl, lat, start=True, stop=True), reads=["mats", g + "lat"], writes=[pk(PE_)])
                for h in range(4):
                    S.op("pe", lambda e, h=h: e.matmul(psb[PG][:, 2 * h:2 * h + 2], lat[:, h * 128:(h + 1) * 128], chunksel, start=True, stop=True),
                         reads=["mats", g + "lat"], writes=[pk(PG)])
                S.op("act", lambda e: e.activation(Ebl, psb[PE_][:], AF.Exp), reads=[pk(PE_)], writes=[g + "Ebl"])
                S.op("act", lambda e: e.activation(small[:, 40:48], psb[PG][:, 0:8], AF.Exp), reads=[pk(PG)], writes=["dec"])
                S.op("dve", lambda e: e.tensor_tensor(Dl, psb[PG][:, 0:8], Dl, ALU.add), reads=[pk(PG), g + "Dl"], writes=[g + "Dl"])
                S.op("pool", lambda e, P=P: e.tensor_tensor(kst, P[:, 0:512], Ebl, ALU.mult), reads=[Pk, g + "Ebl"], writes=[g + "kst"])
                S.op("act", lambda e, P=P: e.copy(gvb, P[:, 512:1536]), reads=[Pk], writes=[g + "gvb"])
                for h in range(4):
                    S.op("pe", lambda e, h=h: e.matmul(psb[PD][:, 0:256], kst[0:64, h * 128:(h + 1) * 128], gvb[0:64, h * 256:(h + 1) * 256],
                                                       start=True, stop=True),
                         reads=[g + "kst", g + "gvb"], writes=[pk(PD)])
                    S.op("dve", lambda e, h=h: e.scalar_tensor_tensor(Smid[:, h, :], Sst[:, h, :], small[:, 40 + 2 * h:41 + 2 * h],
                                                                      psb[PD][:, 0:256], ALU.mult, ALU.add),
                         reads=["Sst", "dec", pk(PD)], writes=[("Smid", h)])
                    S.op("pe", lambda e, h=h: e.matmul(psb[PF][:, 0:256], kst[64:128, h * 128:(h + 1) * 128], gvb[64:128, h * 256:(h + 1) * 256],
                                                       start=True, stop=True),
                         reads=[g + "kst", g + "gvb"], writes=[pk(PF)])
                    S.op("dve", lambda e, h=h: e.scalar_tensor_tensor(Sst[:, h, :], Smid[:, h, :], small[:, 41 + 2 * h:42 + 2 * h],
                                                                      psb[PF][:, 0:256], ALU.mult, ALU.add),
                         reads=[("Smid", h), "dec", pk(PF), "Sst"], writes=["Sst"])
            tb = NTB - 1
            S.op("sp", lambda e: e.dma_start(out=Pkv, in_=proj[tb * 128:(tb + 1) * 128, 1024:1536]),
                 reads=[("proj", tb)], writes=[g + "Pkv"], dma=gd + "kv")
            S.op("pool", lambda e: e.tensor_tensor(sq, Pkv[:, 0:256], Pkv[:, 0:256], ALU.mult), reads=[g + "Pkv"], writes=[g + "sq"])
            S.op("dve", lambda e: e.tensor_reduce(small[:, 8:12], sq.rearrange("p (h d) -> p h d", h=4), AX.X, ALU.add),
                 reads=[g + "sq"], writes=["ssq"])
            S.op("act", lambda e: e.activation(small[:, 8:12], small[:, 8:12], AF.Ln, scale=1.0 / 64, bias=EPS), reads=["ssq"], writes=["ssq"])
            S.op("act", lambda e: e.activation(small[:, 8:12], small[:, 8:12], AF.Exp, scale=-0.5), reads=["ssq"], writes=["ssq"])
            S.op("dve", lambda e: e.tensor_tensor(sq.rearrange("p (h d) -> p h d", h=4), Pkv[:, 0:256].rearrange("p (h d) -> p h d", h=4),
                                                  small[:, 8:12].unsqueeze(2).to_broadcast([128, 4, 64]), ALU.mult),
                 reads=[g + "Pkv", "ssq", g + "sq"], writes=[g + "sq"])
            S.op("pool", lambda e: e.tensor_tensor(kn.rearrange("p (h d) -> p h d", h=4), sq.rearrange("p (h d) -> p h d", h=4),
                                                   gqk[:, 16:20, :], ALU.mult), reads=[g + "sq", "gqk_k"], writes=[g + "kn"])
            pgb = psb[PG][:].bitcast(BF16)
            for j in range(4):
                S.op("pe", lambda e, j=j: e.transpose(pgb[0:64, j * 128:(j + 1) * 128], kn[:, j * 64:(j + 1) * 64], ident[:]),
                     reads=[g + "kn", "ident"], writes=[pk(PG)])
            S.op("act", lambda e: e.copy(KTo[0:64, :], pgb[0:64, 0:512]), reads=[pk(PG)], writes=[g + "KTo"])
            S.op("act", lambda e: e.copy(Vo, Pkv[:, 256:512]), reads=[g + "Pkv"], writes=[g + "Vo"])
            S.op("dve", lambda e: e.tensor_tensor(small[:, 12:16], Dl.rearrange("p (h c) -> p h c", c=2)[:, :, 0],
                                                  Dl.rearrange("p (h c) -> p h c", c=2)[:, :, 1], ALU.add),
                 reads=[g + "Dl"], writes=["dl4"])
            S.op("sp", lambda e: e.dma_start(out=pl_S[:, 0:1024], in_=Sst[:].rearrange("p h d -> p (h d)")), reads=["Sst"], dma=gd + "o0")
            S.op("sp", lambda e: e.dma_start(out=pl_S[:, 1024:1028], in_=small[:, 12:16]), reads=["dl4"], dma=gd + "o1")
            S.op("sp", lambda e: e.dma_start(out=pl_KT, in_=KTo[0:64, :]), reads=[g + "KTo"], dma=gd + "o2")
            S.op("sp", lambda e: e.dma_start(out=pl_V, in_=Vo), reads=[g + "Vo"], dma=gd + "o3")
            S.barrier()

        def phase_C(l, x_src, r0):
            ar.reset()
            g = f"C{ar.gen}."
            gd = "C."
            mT = ar.alloc(NTB * 2048, BF16).rearrange("p (b k t) -> p b k t", b=NTB, k=16)
            wt = [ar.alloc(16 * 512, BF16).rearrange("p (k n) -> p k n", k=16) for _ in range(2)]
            xs = [ar.alloc(512) for _ in range(4)]
            for tb in range(NTB):
                S.op("sp", lambda e, tb=tb: e.dma_start(out=mT[:, tb].rearrange("p k t -> p (k t)"), in_=mixT_h[tb]),
                     reads=[("mixT_h", tb)], writes=[(g + "mT", tb)], dma=gd + f"m{tb % 4}")
            cnt = 0
            for nt in range(4):
                w = wt[nt % 2]
                wk = (g + "wt", nt % 2)
                S.op("pool", lambda e, nt=nt, w=w: e.dma_start(
                    out=w, in_=w_out[l, :, nt * 512:(nt + 1) * 512].rearrange("(kc p) n -> p kc n", p=128)),
                    writes=[wk], dma=gd + f"w{nt % 2}")
                for tb in range(NTB):
                    pi = PB + (cnt % 2)
                    x_ = xs[cnt % 4]
                    xk = (g + "xs", cnt % 4)
                    S.op("sp", lambda e, x_=x_, tb=tb, nt=nt: e.dma_start(
                        out=x_, in_=x_src[r0 + tb * 128:r0 + (tb + 1) * 128, nt * 512:(nt + 1) * 512]),
                        writes=[xk], dma=gd + f"xl{cnt % 4}")
                    for kc in range(16):
                        S.op("pe", lambda e, kc=kc, tb=tb, w=w, pi=pi: e.matmul(
                            psb[pi][:], mT[:, tb, kc, :], w[:, kc, :], start=(kc == 0), stop=(kc == 15)),
                            reads=[(g + "mT", tb), wk], writes=[pk(pi)])
                    S.op("dve", lambda e, x_=x_, pi=pi: e.tensor_tensor(x_, psb[pi][:], x_, ALU.add), reads=[pk(pi), xk], writes=[xk])
                    S.op("sp", lambda e, x_=x_, tb=tb, nt=nt: e.dma_start(
                        out=xmid[tb * 128:(tb + 1) * 128, nt * 512:(nt + 1) * 512], in_=x_),
                        reads=[xk], writes=[("xmid", tb)], dma=gd + f"xs{cnt % 4}")
                    cnt += 1
            S.barrier()

        def phase_D(l):
            ar.reset()
            g = f"D{ar.gen}."
            gd = "D."
            xt = [ar.alloc(D) for _ in range(2)]
            hb = [ar.alloc(D, BF16) for _ in range(2)]
            hs = [ar.alloc(2048, BF16) for _ in range(2)]
            for tb in range(NTB):
                i = tb % 2
                hv = hs[i].rearrange("p (k t) -> p k t", k=16)
                norm_block(g + f"{i}", gd + f"{i}", xmid[tb * 128:(tb + 1) * 128, :], g2b, "g2b", xt[i], hb[i],
                           lambda a, b, hv=hv: hv[:, a:b, :], [(g + "hs", i, 0), (g + "hs", i, 1)], 4 * i)
                S.op("sp", lambda e, i=i, tb=tb: e.dma_start(out=h2T_h[tb], in_=hs[i]),
                     reads=[(g + "hs", i, 0), (g + "hs", i, 1), ("xmid", tb)], writes=[("h2T_h", tb)], dma=gd + f"h{i}")
            S.barrier()

        def phase_CD(l, x_src, r0):
            ar.reset()
            g = f"F{ar.gen}."
            gd = "F."
            wo = ar.alloc(4 * 16 * 512, BF16).rearrange("p (n k c) -> p n k c", n=4, k=16)
            mTb = [ar.alloc(2048, BF16).rearrange("p (k t) -> p k t", k=16) for _ in range(2)]
            xr = [ar.alloc(D) for _ in range(2)]
            hb = [ar.alloc(D, BF16) for _ in range(2)]
            hs = [ar.alloc(2048, BF16) for _ in range(2)]
            for nt in range(4):
                S.op("pool", lambda e, nt=nt: e.dma_start(
                    out=wo[:, nt], in_=w_out[l, :, nt * 512:(nt + 1) * 512].rearrange("(kc p) n -> p kc n", p=128)),
                    writes=[(g + "wo", nt)], dma=gd + f"w{nt % 2}")
            cnt = [0]

            def mm_part(tb):
                i = tb % 2
                tag = g + f"{i}"
                S.op("sp", lambda e, i=i, tb=tb: e.dma_start(out=mTb[i].rearrange("p k t -> p (k t)"), in_=mixT_h[tb]),
                     reads=[("mixT_h", tb)], writes=[(g + "mT", i)], dma=gd + f"m{i}")
                S.op("sp", lambda e, i=i, tb=tb: e.dma_start(out=xr[i], in_=x_src[r0 + tb * 128:r0 + (tb + 1) * 128, :]),
                     writes=[tag + "xt"], dma=gd + f"x{i}")
                for nt in range(4):
                    pi = cnt[0] % 6
                    cnt[0] += 1
                    for kc in range(16):
                        S.op("pe", lambda e, kc=kc, nt=nt, i=i, pi=pi: e.matmul(
                            psb[pi][:], mTb[i][:, kc, :], wo[:, nt, kc, :], start=(kc == 0), stop=(kc == 15)),
                            reads=[(g + "mT", i), (g + "wo", nt)], writes=[pk(pi)])
                    S.op("dve", lambda e, nt=nt, i=i, pi=pi: e.tensor_tensor(
                        xr[i][:, nt * 512:(nt + 1) * 512], psb[pi][:], xr[i][:, nt * 512:(nt + 1) * 512], ALU.add),
                        reads=[pk(pi), tag + "xt"], writes=[tag + "xt"])

            def norm_part(tb):
                i = tb % 2
                tag = g + f"{i}"
                S.op("sp", lambda e, i=i, tb=tb: e.dma_start(out=xmid[tb * 128:(tb + 1) * 128, :], in_=xr[i]),
                     reads=[tag + "xt"], writes=[("xmid", tb)], dma=gd + f"s{i}")
                hv = hs[i].rearrange("p (k t) -> p k t", k=16)
                norm_block(tag, gd + f"{i}", None, g2b, "g2b", xr[i], hb[i],
                           lambda a, b, hv=hv: hv[:, a:b, :], [(g + "hs", i, 0), (g + "hs", i, 1)], 4 * i)
                S.op("sp", lambda e, i=i, tb=tb: e.dma_start(out=h2T_h[tb], in_=hs[i]),
                     reads=[(g + "hs", i, 0), (g + "hs", i, 1)], writes=[("h2T_h", tb)], dma=gd + f"h{i}")

            mm_part(0)
            for tb in range(NTB):
                if tb + 1 < NTB:
                    mm_part(tb + 1)
                norm_part(tb)
            S.barrier()

        def phase_E(l, y_dst, r0):
            ar.reset()
            g = f"E{ar.gen}."
            for half in range(2):
                ar.off = 0
                gd = "E."
                HT = GT // 2
                hT = ar.alloc(16 * HT, BF16).rearrange("p (k t) -> p k t", k=16)
                acc = ar.alloc(8 * D).rearrange("p (b d) -> p b d", b=8)
                wu = ar.alloc(16 * 512, BF16).rearrange("p (k f) -> p k f", k=16)
                wd = ar.alloc(4 * D, BF16).rearrange("p (f n) -> p f n", f=4)
                uT = ar.alloc(4 * HT, BF16).rearrange("p (f t) -> p f t", f=4)
                rt = [ar.alloc(512) for _ in range(2)]
                for b in range(8):
                    tb = half * 8 + b
                    S.op("sp", lambda e, b=b, tb=tb: e.dma_start(out=hT[:, :, b * 128:(b + 1) * 128],
                                                                 in_=h2T_h[tb].rearrange("p (k t) -> p k t", k=16)),
                         reads=[("h2T_h", tb)], writes=[(g + "hT", b)], dma=gd + f"h{b % 4}")
                for b in range(8):
                    tb = half * 8 + b
                    S.op("sp", lambda e, b=b, tb=tb: e.dma_start(out=acc[:, b, :], in_=xmid[tb * 128:(tb + 1) * 128, :]),
                         reads=[("xmid", tb)], writes=[(g + "acc", b)], dma=gd + f"a{b % 4}")
                cu = 0
                cd = 0
                for fcg in range(DFF // 512):
                    S.op("pool", lambda e, fcg=fcg: e.dma_start(
                        out=wu, in_=w_up[l, :, fcg * 512:(fcg + 1) * 512].rearrange("(kc p) f -> p kc f", p=128)),
                        writes=[g + "wu"], dma=gd + "wu")
                    S.op("pool", lambda e, fcg=fcg: e.dma_start(
                        out=wd, in_=w_down[l, fcg * 512:(fcg + 1) * 512, :].rearrange("(fb p) n -> p fb n", p=128)),
                        writes=[g + "wd"], dma=gd + "wd")
                    for fb in range(4):
                        for tt in range(HT // 512):
                            pi = PB + (cu % 2)
                            for kc in range(16):
                                S.op("pe", lambda e, kc=kc, fb=fb, tt=tt, pi=pi: e.matmul(
                                    psb[pi][:], wu[:, kc, fb * 128:(fb + 1) * 128], hT[:, kc, tt * 512:(tt + 1) * 512],
                                    start=(kc == 0), stop=(kc == 15)),
                                    reads=[g + "wu"] + [(g + "hT", b) for b in range(tt * 4, tt * 4 + 4)], writes=[pk(pi)])
                            r = rt[cu % 2]
                            rk = (g + "rt", cu % 2)
                            S.op("act", lambda e, r=r, pi=pi: e.activation(r, psb[pi][:], AF.Relu), reads=[pk(pi)], writes=[rk])
                            S.op("pool", lambda e, r=r, fb=fb, tt=tt: e.tensor_tensor(uT[:, fb, tt * 512:(tt + 1) * 512], r, r, ALU.mult),
                                 reads=[rk], writes=[(g + "uT", tt)])
                            cu += 1
                    for b in range(8):
                        for nt in range(4):
                            pi = PD + (cd % 4)
                            for fb in range(4):
                                S.op("pe", lambda e, fb=fb, b=b, nt=nt, pi=pi: e.matmul(
                                    psb[pi][:], uT[:, fb, b * 128:(b + 1) * 128], wd[:, fb, nt * 512:(nt + 1) * 512],
                                    start=(fb == 0), stop=(fb == 3)),
                                    reads=[(g + "uT", b // 4), g + "wd"], writes=[pk(pi)])
                            S.op("dve", lambda e, b=b, nt=nt, pi=pi: e.tensor_tensor(
                                acc[:, b, nt * 512:(nt + 1) * 512], psb[pi][:], acc[:, b, nt * 512:(nt + 1) * 512], ALU.add),
                                reads=[pk(pi), (g + "acc", b)], writes=[(g + "acc", b)])
                            cd += 1
                for b in range(8):
                    tb = half * 8 + b
                    S.op("sp", lambda e, b=b, tb=tb: e.dma_start(out=y_dst[r0 + tb * 128:r0 + (tb + 1) * 128, :], in_=acc[:, b, :]),
                         reads=[(g + "acc", b)], writes=[("ydst", r0, tb)], dma=gd + f"o{b % 4}")
            S.barrier()

        for step, l in (plan or []):
            layer_consts(l)
            src = x1_out if (mid_out and step == "pre") else x_in
            phase_A(l, src, 0)
            if step == "pre":
                phase_Bs(l)
            else:
                phase_B(l, 0, False, init_payload=True)
                phase_CD(l, src, 0)
                phase_E(l, x1_out if mid_out else y_out, 0)
        for l in range(n_layers if plan is None else 0):
            layer_consts(l)
            src = x_in if l == 0 else xa
            dst = y_out if l == n_layers - 1 else xa
            for gi in range(n_groups):
                r0 = gi * GT
                phase_A(l, src, r0)
                phase_B(l, gi * NTB, gi == 0)
                phase_CD(l, src, r0)
                phase_E(l, dst, r0)
        S.emit(nc)
    return nc, S


_CACHE = {}
MODE = "unfused"
PLANS = ([("pre", 0)], [("main", 0), ("pre", 1)], [("main", 1)])


def kernel_unfused(x, **w):
    x = np.asarray(x, dtype=np.float32)
    B = x.shape[0]
    NP = SEQ // GT
    ncores = B * NP
    if "plans" not in _CACHE:
        _CACHE["plans"] = [build_program(GT, DEPTH, True, plan=list(p))[0] for p in PLANS]
    progs = _CACHE["plans"]
    consts = host_consts()
    shared = {k: np.ascontiguousarray(np.asarray(v, dtype=np.float32)) for k, v in w.items()}
    cb_first = consts["c_bias"]
    cb_rest = cb_first.copy()
    cb_rest[:, 2 * 2048:3 * 2048] = cb_first[:, 0:2048]
    cms = []
    for p in range(NP):
        cmk = np.zeros((128, 8), np.float32)
        for j in range(3):
            cmk[:, j] = 1.0 if j < p else 0.0
            cmk[:, 4 + j] = 1.0 - cmk[:, j]
        cms.append(cmk)
    lite_keys = ("norm1_g", "w_in", "q_norm_g", "k_norm_g", "attn_sinks", "gla_gate_w", "gla_gate_b", "gla_norm_g")

    def base(c, lite=False):
        p = c % NP
        d = {k: shared[k] for k in (lite_keys if lite else shared)}
        d.update(c_mats=consts["c_mats"], c_ident=consts["c_ident"], c_bias=cb_first if p == 0 else cb_rest)
        return d

    def halo(res):
        out = []
        zk = np.zeros((64, 512), ml_dtypes.bfloat16)
        zv = np.zeros((128, 256), ml_dtypes.bfloat16)
        for c in range(ncores):
            b, p = divmod(c, NP)
            gs = np.stack([np.asarray(res[b * NP + j]["pl_S"], np.float32) for j in range(NP)], 0)
            out.append(dict(g_S=gs, h_KT=res[c - 1]["pl_KT"] if p > 0 else zk, h_V=res[c - 1]["pl_V"] if p > 0 else zv,
                            cmask=cms[p]))
        return out

    xs = [np.ascontiguousarray(x[c // NP, (c % NP) * GT:(c % NP + 1) * GT]) for c in range(ncores)]
    r1 = run_bass_kernel_spmd(progs[0], [dict(base(c, True), x=xs[c]) for c in range(ncores)], core_ids=list(range(ncores))).results
    h1 = halo(r1)
    r2 = run_bass_kernel_spmd(progs[1], [dict(base(c), x=xs[c], **h1[c]) for c in range(ncores)], core_ids=list(range(ncores))).results
    h2 = halo(r2)
    r3 = run_bass_kernel_spmd(progs[2], [dict(base(c), x=np.asarray(r2[c]["x1"], np.float32), **h2[c]) for c in range(ncores)],
                              core_ids=list(range(ncores))).results
    out = np.empty((B, SEQ, D), np.float32)
    for c in range(ncores):
        out[c // NP, (c % NP) * GT:(c % NP + 1) * GT] = r3[c]["out"]
    return out


def kernel(x, norm1_g, w_in, q_norm_g, k_norm_g, attn_sinks, gla_gate_w, gla_gate_b,
           gla_norm_g, w_out, norm2_g, w_up, w_down):
    if MODE == "unfused":
        return kernel_unfused(x, norm1_g=norm1_g, w_in=w_in, q_norm_g=q_norm_g, k_norm_g=k_norm_g, attn_sinks=attn_sinks,
                              gla_gate_w=gla_gate_w, gla_gate_b=gla_gate_b, gla_norm_g=gla_norm_g, w_out=w_out,
                              norm2_g=norm2_g, w_up=w_up, w_down=w_down)
    x = np.asarray(x, dtype=np.float32)
    B = x.shape[0]
    if "nc" not in _CACHE:
        _CACHE["nc"] = build_program(SEQ, DEPTH)[0]
    nc = _CACHE["nc"]
    consts = host_consts()
    shared = dict(norm1_g=norm1_g, w_in=w_in, q_norm_g=q_norm_g, k_norm_g=k_norm_g, attn_sinks=attn_sinks,
                  gla_gate_w=gla_gate_w, gla_gate_b=gla_gate_b, gla_norm_g=gla_norm_g, w_out=w_out,
                  norm2_g=norm2_g, w_up=w_up, w_down=w_down)
    shared = {k: np.ascontiguousarray(np.asarray(v, dtype=np.float32)) for k, v in shared.items()}
    shared.update(consts)
    in_maps = [dict(shared, x=np.ascontiguousarray(x[b])) for b in range(B)]
    res = run_bass_kernel_spmd(nc, in_maps, core_ids=list(range(B)))
    return np.stack([res.results[b]["out"] for b in range(B)], 0).astype(np.float32)
```
